# Optimizing a Trainium2 kernel written in Bass

```python
import jax, jax.numpy as jnp
from jax import lax
import numpy as np

D_MODEL = 2048
BATCH = 16
SEQ = 256
DEPTH = 4
DEC_BATCH = 4
DEC_SEQ = 4096
PAST_LEN = 512

GRID_W = 64
N_MIXERS = 3
N_POOL = (DEPTH + 2) // 3
N_MLA = (DEPTH + 1) // 3
N_CONV = DEPTH // 3
N_SUB = 3
N_MOD = 3 * N_SUB
FFN_HIDDEN = 5632
POOL_WINDOWS = (2, 4, 8, 16)
POOL_GROUP = D_MODEL // len(POOL_WINDOWS)
MLA_HEADS = 16
QK_NOPE = 128
QK_ROPE = 64
V_HEAD = 128
QK_HEAD = QK_NOPE + QK_ROPE
Q_LORA = 512
KV_LORA = 512
ROPE_BASE = 10000.0
CONV_WIDTH = 31
Q_BLOCK = 128
EPS = 1e-6

kernel_name = 'hybrid_pool_mla_conformer_diffusion_step'


def rms_norm(x, g):
    xf = x.astype(jnp.float32)
    y = xf * lax.rsqrt(jnp.mean(xf * xf, axis=-1, keepdims=True) + EPS)
    return (y * g.astype(jnp.float32)).astype(x.dtype)


def layer_norm(x, g, b):
    xf = x.astype(jnp.float32)
    mu = jnp.mean(xf, axis=-1, keepdims=True)
    var = jnp.mean(jnp.square(xf - mu), axis=-1, keepdims=True)
    y = (xf - mu) * lax.rsqrt(var + EPS) * g.astype(jnp.float32) + b.astype(jnp.float32)
    return y.astype(x.dtype)


def modulate(h, shift, scale):
    return h * (1 + scale[:, None, :]) + shift[:, None, :]


def swiglu(h, w1, w3, w2):
    return (jax.nn.silu(h @ w1) * (h @ w3)) @ w2


def pool_mixer(h, w_pool, scale):
    B, T, D = h.shape
    t = jnp.arange(T)
    cs = jnp.concatenate([jnp.zeros((B, 1, D), jnp.float32),
                          jnp.cumsum(h.astype(jnp.float32), axis=1)], axis=1)
    outs = []
    for g, w in enumerate(POOL_WINDOWS):
        lo = jnp.clip(t - w // 2, 0, T - 1)
        hi = jnp.clip(t + w // 2 - 1, 0, T - 1)
        sl = slice(g * POOL_GROUP, (g + 1) * POOL_GROUP)
        csg = cs[:, :, sl]
        cnt = (hi - lo + 1).astype(jnp.float32)[None, :, None]
        mean = (csg[:, hi + 1] - csg[:, lo]) / cnt
        diff = (mean - h[:, :, sl].astype(jnp.float32)).astype(h.dtype)
        outs.append(diff @ w_pool[g])
    return jnp.concatenate(outs, axis=-1) * scale


def conv_module(h, w1, b1, dw, dw_b, ln_g, ln_b, w2, b2):
    u = h @ w1 + b1
    u = u[..., :D_MODEL] * jax.nn.sigmoid(u[..., D_MODEL:])
    u = lax.conv_general_dilated(u, dw[:, None, :], window_strides=(1,),
                                 padding=[(CONV_WIDTH // 2, CONV_WIDTH // 2)],
                                 dimension_numbers=('NWC', 'WIO', 'NWC'),
                                 feature_group_count=D_MODEL) + dw_b
    u = jax.nn.silu(layer_norm(u, ln_g, ln_b))
    return u @ w2 + b2


def axial_rope_tables(T):
    rows = T // GRID_W
    row = jnp.repeat(jnp.arange(rows), GRID_W).astype(jnp.float32)
    col = jnp.tile(jnp.arange(GRID_W), rows).astype(jnp.float32)
    half = QK_ROPE // 2
    inv_freq = ROPE_BASE ** (-jnp.arange(0, half, 2, dtype=jnp.float32) / half)
    ang = jnp.stack([row[:, None] * inv_freq, col[:, None] * inv_freq], axis=1)
    return jnp.cos(ang), jnp.sin(ang)


def rope_part(t, cos, sin):
    nope, rope = t[..., :QK_NOPE], t[..., QK_NOPE:]
    r = rope.astype(jnp.float32).reshape(rope.shape[:-1] + (2, 2, QK_ROPE // 4))
    r1, r2 = r[..., 0, :], r[..., 1, :]
    c = cos[None, :, None]
    s = sin[None, :, None]
    out = jnp.stack([r1 * c - r2 * s, r2 * c + r1 * s], axis=-2).reshape(rope.shape)
    return jnp.concatenate([nope, out.astype(t.dtype)], axis=-1)


def mla_queries(h, p, j):
    B, T, _ = h.shape
    cq = rms_norm(h @ p['mla_w_dq'][j], p['mla_q_norm'][j])
    q = (cq @ p['mla_w_uq'][j]).reshape(B, T, MLA_HEADS, QK_HEAD)
    return rms_norm(q, p['mla_q_gain'][j])


def mla_compress_kv(h, p, j):
    kv = h @ p['mla_w_dkv'][j]
    return rms_norm(kv[..., :KV_LORA], p['mla_kv_norm'][j]), kv[..., KV_LORA:]


def mla_expand_kv(ckv, kpe, p, j):
    B, T, _ = ckv.shape
    kv = (ckv @ p['mla_w_ukv'][j]).reshape(B, T, MLA_HEADS, QK_NOPE + V_HEAD)
    k_nope, v = kv[..., :QK_NOPE], kv[..., QK_NOPE:]
    k_pe = jnp.broadcast_to(kpe[:, :, None, :], (B, T, MLA_HEADS, QK_ROPE))
    k = rms_norm(jnp.concatenate([k_nope, k_pe], axis=-1), p['mla_k_gain'][j])
    return k, v


def block_attention(q, k, v):
    B, Tq, H, dq = q.shape
    nb = Tq // Q_BLOCK
    qb = jnp.moveaxis(q.reshape(B, nb, Q_BLOCK, H, dq), 1, 0)
    scale = QK_HEAD ** -0.5

    def one(qblk):
        s = jnp.einsum('bqhd,bkhd->bhqk', qblk, k).astype(jnp.float32) * scale
        pr = jax.nn.softmax(s, axis=-1).astype(v.dtype)
        return jnp.einsum('bhqk,bkhd->bqhd', pr, v)

    o = lax.map(one, qb)
    return jnp.moveaxis(o, 0, 1).reshape(B, Tq, H, v.shape[-1])


def mla_context(h, p, j):
    B, T, _ = h.shape
    q = mla_queries(h, p, j)
    ckv, kpe = mla_compress_kv(h, p, j)
    k, v = mla_expand_kv(ckv, kpe, p, j)
    o = block_attention(q, k, v).reshape(B, T, MLA_HEADS * V_HEAD)
    return o @ p['mla_w_o'][j], ckv, kpe


def mla_latent(h, ctx_ckv, ctx_kpe, p, j):
    B, T, _ = h.shape
    cos, sin = axial_rope_tables(T)
    q = rope_part(mla_queries(h, p, j), cos, sin)
    ckv, kpe = mla_compress_kv(h, p, j)
    k_lat, v_lat = mla_expand_kv(ckv, kpe, p, j)
    k_lat = rope_part(k_lat, cos, sin)
    k_ctx, v_ctx = mla_expand_kv(ctx_ckv, ctx_kpe, p, j)
    k = jnp.concatenate([k_ctx, k_lat], axis=1)
    v = jnp.concatenate([v_ctx, v_lat], axis=1)
    o = block_attention(q, k, v).reshape(B, T, MLA_HEADS * V_HEAD)
    return o @ p['mla_w_o'][j]


def trunk(x, cond, ckv_cache, kpe_cache, p):
    is_ctx = ckv_cache is None
    new_ckv, new_kpe = [], []
    for i in range(DEPTH):
        mod = jax.nn.silu(cond) @ p['w_ada'][i] + p['b_ada'][i]
        sh1, sc1, g1, sh2, sc2, g2, sh3, sc3, g3 = jnp.split(mod, N_MOD, axis=-1)
        h = modulate(rms_norm(x, p['norm_g'][i, 0]), sh1, sc1)
        x = x + 0.5 * g1[:, None, :] * swiglu(h, p['ffn_w1'][i, 0], p['ffn_w3'][i, 0], p['ffn_w2'][i, 0])
        h = modulate(rms_norm(x, p['norm_g'][i, 1]), sh2, sc2)
        kind, j = i % N_MIXERS, i // N_MIXERS
        if kind == 0:
            y = pool_mixer(h, p['pool_w'][j], p['pool_scale'][j])
        elif kind == 1:
            if is_ctx:
                y, ckv, kpe = mla_context(h, p, j)
                new_ckv.append(ckv)
                new_kpe.append(kpe)
            else:
                y = mla_latent(h, ckv_cache[:, j], kpe_cache[:, j], p, j)
        else:
            y = conv_module(h, p['conv_w1'][j], p['conv_b1'][j], p['conv_dw'][j], p['conv_dw_b'][j],
                            p['conv_ln_g'][j], p['conv_ln_b'][j], p['conv_w2'][j], p['conv_b2'][j])
        x = x + g2[:, None, :] * y
        h = modulate(rms_norm(x, p['norm_g'][i, 2]), sh3, sc3)
        x = x + 0.5 * g3[:, None, :] * swiglu(h, p['ffn_w1'][i, 1], p['ffn_w3'][i, 1], p['ffn_w2'][i, 1])
    return x, new_ckv, new_kpe


def setup_inputs(seed: int = 0) -> dict:
    key = jax.random.key(seed)
    ks = iter(jax.random.split(key, 40))
    D = D_MODEL

    def nrm(shape, scale):
        return jax.random.normal(next(ks), shape, jnp.float32) * scale

    inp = {}
    inp['x_prompt'] = nrm((BATCH, SEQ, D), 1.0)
    inp['x_sample'] = nrm((DEC_BATCH, DEC_SEQ, D), 1.0)
    inp['cache_ckv'] = nrm((DEC_BATCH, N_MLA, PAST_LEN, KV_LORA), 1.0)
    inp['cache_kpe'] = nrm((DEC_BATCH, N_MLA, PAST_LEN, QK_ROPE), 1.0)
    inp['c'] = nrm((DEC_BATCH, D), 1.0)
    inp['c_ctx'] = nrm((D,), 1.0)
    inp['norm_g'] = 1.0 + nrm((DEPTH, N_SUB, D), 0.05)
    inp['w_ada'] = nrm((DEPTH, D, N_MOD * D), 0.5 * D ** -0.5)
    inp['b_ada'] = nrm((DEPTH, N_MOD * D), 0.01)
    inp['ffn_w1'] = nrm((DEPTH, 2, D, FFN_HIDDEN), D ** -0.5)
    inp['ffn_w3'] = nrm((DEPTH, 2, D, FFN_HIDDEN), D ** -0.5)
    inp['ffn_w2'] = nrm((DEPTH, 2, FFN_HIDDEN, D), FFN_HIDDEN ** -0.5)
    inp['pool_w'] = nrm((N_POOL, len(POOL_WINDOWS), POOL_GROUP, POOL_GROUP), POOL_GROUP ** -0.5)
    inp['pool_scale'] = 1.0 + nrm((N_POOL, D), 0.05)
    inp['mla_w_dq'] = nrm((N_MLA, D, Q_LORA), D ** -0.5)
    inp['mla_q_norm'] = 1.0 + nrm((N_MLA, Q_LORA), 0.05)
    inp['mla_w_uq'] = nrm((N_MLA, Q_LORA, MLA_HEADS * QK_HEAD), Q_LORA ** -0.5)
    inp['mla_w_dkv'] = nrm((N_MLA, D, KV_LORA + QK_ROPE), D ** -0.5)
    inp['mla_kv_norm'] = 1.0 + nrm((N_MLA, KV_LORA), 0.05)
    inp['mla_w_ukv'] = nrm((N_MLA, KV_LORA, MLA_HEADS * (QK_NOPE + V_HEAD)), KV_LORA ** -0.5)
    inp['mla_w_o'] = nrm((N_MLA, MLA_HEADS * V_HEAD, D), (MLA_HEADS * V_HEAD) ** -0.5)
    inp['mla_q_gain'] = 1.0 + nrm((N_MLA, QK_HEAD), 0.05)
    inp['mla_k_gain'] = 1.0 + nrm((N_MLA, QK_HEAD), 0.05)
    inp['conv_w1'] = nrm((N_CONV, D, 2 * D), D ** -0.5)
    inp['conv_b1'] = nrm((N_CONV, 2 * D), 0.01)
    inp['conv_dw'] = nrm((N_CONV, CONV_WIDTH, D), CONV_WIDTH ** -0.5)
    inp['conv_dw_b'] = nrm((N_CONV, D), 0.01)
    inp['conv_ln_g'] = 1.0 + nrm((N_CONV, D), 0.05)
    inp['conv_ln_b'] = nrm((N_CONV, D), 0.01)
    inp['conv_w2'] = nrm((N_CONV, D, D), D ** -0.5)
    inp['conv_b2'] = nrm((N_CONV, D), 0.01)
    return inp


def reference(x_prompt, x_sample, cache_ckv, cache_kpe, c, c_ctx, norm_g, w_ada, b_ada,
              ffn_w1, ffn_w3, ffn_w2, pool_w, pool_scale, mla_w_dq, mla_q_norm, mla_w_uq,
              mla_w_dkv, mla_kv_norm, mla_w_ukv, mla_w_o, mla_q_gain, mla_k_gain,
              conv_w1, conv_b1, conv_dw, conv_dw_b, conv_ln_g, conv_ln_b, conv_w2, conv_b2):
    p = dict(norm_g=norm_g, w_ada=w_ada, b_ada=b_ada, ffn_w1=ffn_w1, ffn_w3=ffn_w3, ffn_w2=ffn_w2,
             pool_w=pool_w, pool_scale=pool_scale, mla_w_dq=mla_w_dq, mla_q_norm=mla_q_norm,
             mla_w_uq=mla_w_uq, mla_w_dkv=mla_w_dkv, mla_kv_norm=mla_kv_norm, mla_w_ukv=mla_w_ukv,
             mla_w_o=mla_w_o, mla_q_gain=mla_q_gain, mla_k_gain=mla_k_gain,
             conv_w1=conv_w1, conv_b1=conv_b1, conv_dw=conv_dw, conv_dw_b=conv_dw_b,
             conv_ln_g=conv_ln_g, conv_ln_b=conv_ln_b, conv_w2=conv_w2, conv_b2=conv_b2)
    y_prompt, ckv_list, kpe_list = trunk(x_prompt, c_ctx[None, :], None, None, p)
    state_ckv = jnp.stack(ckv_list, axis=1)
    state_kpe = jnp.stack(kpe_list, axis=1)
    y_sample, _, _ = trunk(x_sample, c, cache_ckv, cache_kpe, p)
    return (y_prompt, y_sample, state_ckv, state_kpe)
```

```python
import os
import numpy as np
from contextlib import ExitStack
import concourse.bass as bass
import concourse.mybir as mybir
from concourse.bass_utils import run_bass_kernel_spmd

F32 = mybir.dt.float32
BF16 = mybir.dt.bfloat16
ALU = mybir.AluOpType
AF = mybir.ActivationFunctionType

NCORES = 8
D = 2048
DC = 16
FH = 5632
HC = 44
DEPTH = 4
SEQ = 256
DSEQ = 4096
PAST = 512
HALO = 32
NPR = 512
WIN = 2048 + 2 * HALO
T = NPR + WIN
CH = [(0, 512)] + [(512 + 424 * i, 424) for i in range(4)] + [(512 + 1696, 416)]
NCH = len(CH)
EPS = 1e-6
NPASS = 4
HPP = HC // NPASS
PW = T + 64
SEGS = [(0, 256, 16), (256, 256, 288), (512, WIN, 560)]
POOLW = (2, 4, 8, 16)
NK = 512 + PAST + DSEQ
NKT = NK // 128
CONVW = 31
PAIRS = [[0, 1], [2, 3], [4, 5], [6, 7]]
PERM = list(range(16, 32)) + list(range(0, 16)) + list(range(48, 64)) + list(range(32, 48))


class Eng:
    def __init__(self, name, h, sem, inc):
        self.name, self.h, self.sem, self.inc, self.count = name, h, sem, inc, 0


class Sched:
    def __init__(self, nc):
        self.nc = nc
        self.engs = {}
        self.regions = {}
        self.seen = {}
        self._sems = []

    def add_engine(self, name, h, inc=1):
        sem = self.nc.alloc_semaphore(name="s_" + name)
        e = Eng(name, h, sem, inc)
        self.engs[name] = e
        self.seen[name] = {}
        return e

    def _deps(self, reads, writes):
        deps = {}
        for k in reads:
            r = self.regions.get(k)
            if r and r["w"]:
                e, i = r["w"]
                deps[e] = max(deps.get(e, 0), i)
        for k in writes:
            r = self.regions.get(k)
            if r:
                if r["w"]:
                    e, i = r["w"]
                    deps[e] = max(deps.get(e, 0), i)
                for e, i in r["r"].items():
                    deps[e] = max(deps.get(e, 0), i)
        return deps

    def _wait(self, issuer, deps):
        sn = self.seen[issuer.name]
        for ename, idx in deps.items():
            if ename == "pe" and issuer.name == "pe":
                continue
            if sn.get(ename, 0) >= idx:
                continue
            e = self.engs[ename]
            issuer.h.wait_ge(e.sem, idx * e.inc)
            sn[ename] = idx

    def _commit(self, e, reads, writes):
        e.count += 1
        for k in writes:
            self.regions[k] = {"w": (e.name, e.count), "r": {}}
        for k in reads:
            r = self.regions.setdefault(k, {"w": None, "r": {}})
            r["r"][e.name] = e.count

    def op(self, ename, fn, reads=(), writes=()):
        e = self.engs[ename]
        self._wait(e, self._deps(reads, writes))
        ins = fn(e.h)
        ins.then_inc(e.sem, e.inc)
        self._commit(e, reads, writes)

    def dma(self, qname, slot, out, in_, reads=(), writes=(), **kw):
        q = self.engs[qname]
        s = self.engs[slot]
        self._wait(q, self._deps(reads, writes))
        q.h.dma_start(out=out, in_=in_, **kw).then_inc(s.sem, 16)
        self._commit(s, reads, writes)

    def coll(self, fn, reads=(), writes=()):
        q = self.engs["pool"]
        c = self.engs["cc"]
        self._wait(q, self._deps(reads, writes))
        fn(q.h).then_inc(c.sem, 1)
        self._commit(c, reads, writes)

    def barrier(self):
        tgt = {n: e.count for n, e in self.engs.items() if e.count > 0}
        for n in ("pe", "act", "dve", "pool", "sp"):
            self._wait(self.engs[n], dict(tgt))
        self.regions = {}


def _chunks_of(n, size=128):
    return [(s, min(size, n - s)) for s in range(0, n, size)]


def build_program(nlayers=DEPTH, do_mix=True, ncores=NCORES, do_ffn=True):
    PAIRS = [[2 * i, 2 * i + 1] for i in range(ncores // 2)]
    nc = bass.Bass("TRN2", target_bir_lowering=False)

    def din(name, shape, dt=F32):
        return nc.dram_tensor(name, list(shape), dt, kind="ExternalInput").ap()

    def dout(name, shape, dt=F32):
        return nc.dram_tensor(name, list(shape), dt, kind="ExternalOutput").ap()

    xin = din("xin", [T, D])
    cond2 = din("cond2", [2, D])
    ident_d = din("ident", [128, 128])
    vec_d = din("vecs", [NVEC_PAD, 128])
    if do_ffn:
        w_ada = din("w_ada", [DEPTH, D, 9 * D])
        ffn_w1 = din("ffn_w1", [DEPTH, 2, D, FH])
        ffn_w3 = din("ffn_w3", [DEPTH, 2, D, FH])
        ffn_w2 = din("ffn_w2", [DEPTH, 2, FH, D])
    yout = dout("yout", [NPR + 2048, D])
    sck_o = dout("sck", [NPR, 512])
    skp_o = dout("skp", [NPR, 64])
    maskw = din("maskw", [128, WIN])
    invcnt = din("invcnt", [4, 128, PW])
    cosq = din("cosq", [64, T])
    sinq = din("sinq", [64, T])
    cosk = din("cosk", [64, NK])
    sink = din("sink", [64, NK])
    cck = din("cck", [PAST, 512])
    ckp = din("ckp", [PAST, 64])
    gains_d = din("gains", [1, 384])
    pool_w = din("pool_w", [2, 4, 512, 512])
    w_dq = din("mla_w_dq", [D, 512])
    w_uq = din("mla_w_uq", [512, 3072])
    w_uqs = din("mla_w_uq_sw", [512, 1024])
    w_dkv = din("mla_w_dkv", [D, 576])
    w_ukv = din("mla_w_ukv", [512, 4096])
    w_o = din("mla_w_o", [D, D])
    conv_w1 = din("conv_w1", [D, 2 * D])
    conv_w2 = din("conv_w2", [D, D])
    bin_ = nc.dram_tensor("kv_bounce_in", [512, 2048], BF16).ap()
    bout = nc.dram_tensor("kv_bounce_out", [1024, 2048], BF16).ap()
    binp = nc.dram_tensor("kp_bounce_in", [64, 2048], BF16).ap()
    boutp = nc.dram_tensor("kp_bounce_out", [128, 2048], BF16).ap()

    XD = nc.dram_tensor("xd_scratch", [DC, 128, T], F32).ap()

    S = Sched(nc)
    S.add_engine("pe", nc.tensor)
    S.add_engine("act", nc.scalar)
    S.add_engine("dve", nc.vector)
    S.add_engine("pool", nc.gpsimd)
    S.add_engine("sp", nc.sync)
    for nm in ["ld0", "ld1", "ld2", "ld3", "wa0", "wa1", "wb0", "wb1", "wc0", "wc1",
               "xl0", "xl1", "xl2", "xl3", "xs0", "xs1", "xs2", "xs3", "misc", "out0", "out1",
               "tb0", "tb1", "tb2", "tb3", "gth", "ad0", "ad1", "ad2", "ad3"]:
        S.add_engine(nm, None, inc=16)
    S.add_engine("cc", None, inc=1)

    ARENA_F32 = 47 * 1024 + 512
    with ExitStack() as _es:
        arena = _es.enter_context(nc.sbuf_tensor("arena", [128, ARENA_F32], F32))
        ident = _es.enter_context(nc.sbuf_tensor("sb_ident", [128, 128], F32))
        VEC = _es.enter_context(nc.sbuf_tensor("sb_vec", [128, NVEC_PAD], F32))
        MOD = _es.enter_context(nc.sbuf_tensor("sb_mod", [128, DEPTH * 144 * 2], F32))
        GS = _es.enter_context(nc.sbuf_tensor("gs", [128, DEPTH * 3 * DC * 2], F32))
        HG = _es.enter_context(nc.sbuf_tensor("hg", [128, DEPTH * 3 * DC * 2], F32))
        ones_b = _es.enter_context(nc.sbuf_tensor("onesb", [128, 128], BF16))
        scond = _es.enter_context(nc.sbuf_tensor("scond", [128, DC, 2], BF16))
        identb = _es.enter_context(nc.sbuf_tensor("identb", [128, 128], BF16))
        PG = _es.enter_context(nc.sbuf_tensor("pg", [128, 2 * DC * 2], F32))
        G2B2 = _es.enter_context(nc.sbuf_tensor("g2b2", [128, DC * 2], F32))
        GQK = _es.enter_context(nc.sbuf_tensor("gqk", [128, 1], F32))
        NEGC = _es.enter_context(nc.sbuf_tensor("negc", [128, 1], F32))
        ones_ff = _es.enter_context(nc.sbuf_tensor("onesff", [128, 128], F32))
        ones_f = _es.enter_context(nc.sbuf_tensor("onesf", [1, 128], F32))
        GROW = _es.enter_context(nc.sbuf_tensor("grow", [1, 392], F32))
        ps0 = _es.enter_context(nc.psum_tensor("ps0", [128, 512], F32))
        ps1 = _es.enter_context(nc.psum_tensor("ps1", [128, 512], F32))
        ps2 = _es.enter_context(nc.psum_tensor("ps2", [128, 512], F32))
        ps3 = _es.enter_context(nc.psum_tensor("ps3", [128, 512], F32))
        ps4 = _es.enter_context(nc.psum_tensor("ps4", [128, 512], F32))
        ps5 = _es.enter_context(nc.psum_tensor("ps5", [128, 512], F32))
        ps6 = _es.enter_context(nc.psum_tensor("ps6", [128, 512], F32))
        ps7 = _es.enter_context(nc.psum_tensor("ps7", [128, 512], F32))
        PS = [ps0, ps1, ps2, ps3, ps4, ps5, ps6, ps7]

        def carve(off_bytes, shape, dt):
            esz = 2 if dt == BF16 else 4
            n = int(np.prod(shape[1:]))
            nf32 = (n * esz + 3) // 4
            base = arena[:, off_bytes // 4: off_bytes // 4 + nf32]
            v = base.bitcast(dt) if dt != F32 else base
            if len(shape) == 3:
                v = v.rearrange("p (a b) -> p a b", a=shape[1])
            return v, off_bytes + nf32 * 4

        mmT = nc.tensor

        def mm_group(ps_ap, pairs, reads, writes):
            def fn(h):
                ins = None
                n = len(pairs)
                for q, (l, r) in enumerate(pairs):
                    ins = h.matmul(ps_ap, l, r, start=(q == 0), stop=(q == n - 1))
                return ins
            S.op("pe", fn, reads=reads, writes=writes)

        def transpose_group(ps_ap_list, ins_list, reads, writes):
            def fn(h):
                ins = None
                for o, i_ in zip(ps_ap_list, ins_list):
                    ins = h.transpose(o, i_, ident[:i_.shape[0], :i_.shape[0]])
                return ins
            S.op("pe", fn, reads=reads, writes=writes)

        S.dma("sp", "misc", ident[:], ident_d[:, :], writes=["ident"])
        S.op("dve", lambda h: h.memset(ones_b[:], 1.0), writes=["ones"])

        o = 0
        vstage, o = carve(o, [128, 128], F32)
        for rb in range(NVEC_PAD // 128):
            S.dma("sp", "ld0", vstage, vec_d[rb * 128:(rb + 1) * 128, :], writes=["vstage"])
            transpose_group([PS[0][:, :128]], [vstage], reads=["vstage", "ident"], writes=["ps0"])
            S.op("dve", lambda h, rb=rb: h.tensor_copy(VEC[:, rb * 128:(rb + 1) * 128], PS[0][:, :128]),
                 reads=["ps0"], writes=["VEC"])

        cst, o2 = carve(o, [128, D], F32)
        S.dma("sp", "ld1", cst[0:2, :], cond2[:, :], writes=["cst"])
        S.op("act", lambda h: h.activation(cst[0:2, :], cst[0:2, :], AF.Silu), reads=["cst"], writes=["cst"])
        for k in range(DC):
            transpose_group([PS[1][:, 2 * k:2 * k + 2]], [cst[0:2, k * 128:(k + 1) * 128]],
                            reads=["cst", "ident"], writes=["ps1"])
        S.op("dve", lambda h: h.tensor_copy(scond[:].rearrange("p a b -> p (a b)"), PS[1][:, :2 * DC]),
             reads=["ps1"], writes=["scond"])

        MODv = MOD[:].rearrange("p (l c t) -> p l c t", l=DEPTH, c=144)
        GSv = GS[:].rearrange("p (l s c t) -> p l s c t", l=DEPTH, s=3, c=DC)
        HGv = HG[:].rearrange("p (l s c t) -> p l s c t", l=DEPTH, s=3, c=DC)
        PGv = PG[:].rearrange("p (j c t) -> p j c t", j=2, c=DC)
        G2B2v = G2B2[:].rearrange("p (c t) -> p c t", c=DC)
        ADA_OFF = 73728
        adab = [carve(ADA_OFF + q * 16384, [128, DC, 512], BF16)[0] for q in range(2)]
        arow = [carve(ADA_OFF + 32768 + q * 2048, [128, 512], F32)[0] for q in range(2)]

        def derive(li):
            for s_ in range(3):
                g = VEC[:, VOFF["norm_g"] + (li * 3 + s_) * DC: VOFF["norm_g"] + (li * 3 + s_ + 1) * DC]
                for t in range(2):
                    sc = MODv[:, li, (3 * s_ + 1) * DC:(3 * s_ + 2) * DC, t]
                    S.op("dve", lambda h, s_=s_, t=t, sc=sc, g=g: h.scalar_tensor_tensor(
                        GSv[:, li, s_, :, t], sc, 1.0, g, ALU.add, ALU.mult), reads=["MOD", "VEC"], writes=["GS"])
                    gt = MODv[:, li, (3 * s_ + 2) * DC:(3 * s_ + 3) * DC, t]
                    S.op("dve", lambda h, s_=s_, t=t, gt=gt: h.tensor_scalar(
                        HGv[:, li, s_, :, t], gt, 0.5 if s_ != 1 else 1.0, None, ALU.mult),
                        reads=["MOD"], writes=["HG"])
            for t in range(2):
                if li % 3 == 0:
                    j = li // 3
                    ps_ = VEC[:, VOFF["pool_scale"] + j * DC: VOFF["pool_scale"] + (j + 1) * DC]
                    S.op("dve", lambda h, j=j, t=t, ps_=ps_: h.tensor_tensor(
                        PGv[:, j, :, t], HGv[:, li, 1, :, t], ps_, ALU.mult), reads=["HG", "VEC"], writes=["PG"])
                if li == 2:
                    b2_ = VEC[:, VOFF["conv_b2"]: VOFF["conv_b2"] + DC]
                    S.op("dve", lambda h, t=t, b2_=b2_: h.tensor_tensor(
                        G2B2v[:, :, t], HGv[:, 2, 1, :, t], b2_, ALU.mult), reads=["HG", "VEC"], writes=["G2B2"])

        def ada_gen(li):
            wsrc = w_ada[li].rearrange("(k p) n -> p k n", p=128)

            def load(cb):
                S.dma("pool", "ad%d" % (cb % 2), adab[cb % 2], wsrc[:, :, cb * 512:(cb + 1) * 512],
                      writes=["adab%d" % (cb % 2)])
            load(0)
            for cb in range(36):
                if cb + 1 < 36:
                    load(cb + 1)
                q = cb % 2
                mm_group(PS[7][0:2, :512], [(scond[:, k, :], adab[q][:, k, :]) for k in range(DC)],
                         reads=["adab%d" % q, "scond"], writes=["ps7"])
                S.op("act", lambda h, q=q: h.activation(arow[q][0:2, :], PS[7][0:2, :512], AF.Identity),
                     reads=["ps7"], writes=["arow%d" % q])
                transpose_group([PS[6][:, 2 * (cb * 4 + cc):2 * (cb * 4 + cc) + 2] for cc in range(4)],
                                [arow[q][0:2, cc * 128:(cc + 1) * 128] for cc in range(4)],
                                reads=["arow%d" % q, "ident"], writes=["ps6"])
                yield
            bias = VEC[:, VOFF["b_ada"] + li * 144: VOFF["b_ada"] + (li + 1) * 144]
            for t in range(2):
                S.op("dve", lambda h, t=t, bias=bias: h.tensor_tensor(
                    MODv[:, li, :, t], PS[6][:, :288].rearrange("p (c t) -> p c t", t=2)[:, :, t], bias, ALU.add),
                    reads=["ps6", "VEC"], writes=["MOD"])
            derive(li)
            yield

        def bg_step(bg, k):
            if bg is None:
                return
            for _ in range(k):
                try:
                    next(bg)
                except StopIteration:
                    return

        def bg_drain(bg):
            bg_step(bg, 1000)

        S.op("dve", lambda h: h.tensor_copy(identb[:], ident[:]), reads=["ident"], writes=["identb"])
        S.op("dve", lambda h: h.memset(ones_f[:], 1.0), writes=["onesf"])
        S.op("dve", lambda h: h.memset(ones_ff[:], 1.0), writes=["onesff"])
        ACC_OFF = 184064
        ACC, _ = carve(ACC_OFF, [128, T], F32)

        def acc_update(xs_ap, n, cw_, first, sqt_ap, sqt_key):
            c0_ = CH[n][0]
            accv = ACC[:, c0_:c0_ + cw_]
            if first:
                S.op("pool", lambda h: h.tensor_tensor(accv, xs_ap, xs_ap, ALU.mult), writes=[("acc", n)])
            else:
                S.op("pool", lambda h: h.tensor_tensor(sqt_ap[:, :cw_], xs_ap, xs_ap, ALU.mult), writes=[sqt_key])
                S.op("pool", lambda h: h.tensor_tensor(accv, accv, sqt_ap[:, :cw_], ALU.add),
                     reads=[sqt_key], writes=[("acc", n)])
        if not do_ffn:
            S.op("dve", lambda h: h.memset(MOD[:], 0.5), writes=["MOD"])
            for li in range(nlayers):
                derive(li)
        S.op("dve", lambda h: h.tensor_tensor(GQK[:], VEC[:, VOFF["gains"]:VOFF["gains"] + 1],
                                              VEC[:, VOFF["gains"] + 1:VOFF["gains"] + 2], ALU.mult),
             reads=["VEC"], writes=["GQK"])
        S.dma("sp", "misc", GROW[0:1, 0:384], gains_d[:, :], writes=["GROW"])
        S.op("act", lambda h: h.activation(GROW[0:1, 0:384], GROW[0:1, 0:384], AF.Abs),
             reads=["GROW"], writes=["GROW"])
        S.op("dve", lambda h: h.tensor_scalar(GROW[0:1, 0:192], GROW[0:1, 0:192], 1.0, None, ALU.mult, ALU.max,
                                              accum_out=GROW[0:1, 384:385]), reads=["GROW"], writes=["GROW"])
        S.op("dve", lambda h: h.tensor_scalar(GROW[0:1, 192:384], GROW[0:1, 192:384], 1.0, None, ALU.mult, ALU.max,
                                              accum_out=GROW[0:1, 385:386]), reads=["GROW"], writes=["GROW"])
        S.op("dve", lambda h: h.tensor_scalar(GROW[0:1, 386:387], GROW[0:1, 384:385], GROW[0:1, 385:386],
                                              -float(np.sqrt(192.0)), ALU.mult, ALU.mult), reads=["GROW"], writes=["GROW"])
        mm_group(PS[6][:, 0:1], [(ones_f[0:1, :], GROW[0:1, 386:387])], reads=["GROW", "onesf"], writes=["ps6"])
        S.op("dve", lambda h: h.tensor_copy(NEGC[:], PS[6][:, 0:1]), reads=["ps6"], writes=["NEGC"])

        def shift_col(li, s, c, t):
            return MODv[:, li, (3 * s) * DC + c, t:t + 1]

        S.barrier()
        def _all_ada():
            for li_ in range(nlayers):
                yield from ada_gen(li_)
        bg0 = _all_ada() if do_ffn else None
        sqt_i = [carve(ADA_OFF + 32768 + 4096 + q * 2048, [128, 512], F32)[0] for q in range(2)]
        xt_bufs = [carve(s * 4 * D * 4, [128, 4, D], F32)[0] for s in range(2)]
        xs_off = 2 * 4 * D * 4
        xs_bufs = [carve(xs_off + s * 2048, [128, 512], F32)[0] for s in range(4)]
        nxs = 0

        def acc_update_i(xs, n, cw_, first, k):
            c0_ = CH[n][0]
            accv = ACC[:, c0_:c0_ + cw_]
            src = xs_bufs[xs][:, :cw_]
            if first:
                S.op("pool", lambda h: h.tensor_tensor(accv, src, src, ALU.mult), reads=["xs%d" % xs], writes=[("acc", n)])
            else:
                sq_ = sqt_i[k % 2]
                S.op("pool", lambda h: h.tensor_tensor(sq_[:, :cw_], src, src, ALU.mult),
                     reads=["xs%d" % xs], writes=["sqti%d" % (k % 2)])
                S.op("pool", lambda h: h.tensor_tensor(accv, accv, sq_[:, :cw_], ALU.add),
                     reads=["sqti%d" % (k % 2)], writes=[("acc", n)])
        for n, (c0, cw) in enumerate(CH):
            tl = _chunks_of(cw)
            s = n % 2
            for ti, (t0, tw) in enumerate(tl):
                S.dma("sp", "ld%d" % (2 * s + (ti % 2)), xt_bufs[s][:tw, ti, :], xin[c0 + t0:c0 + t0 + tw, :],
                      writes=[("xt", s, ti)])
            for i in range(DC):
                pb = 3 + (i % 2)
                transpose_group([PS[pb][:, t0:t0 + tw] for (t0, tw) in tl],
                                [xt_bufs[s][:tw, ti, i * 128:(i + 1) * 128] for ti, (t0, tw) in enumerate(tl)],
                                reads=[("xt", s, ti) for ti in range(len(tl))] + ["ident"], writes=["ps%d" % pb])
                xs = nxs % 4
                nxs += 1
                S.op("act" if i % 2 else "dve",
                     (lambda h, xs=xs, pb=pb, cw=cw: h.activation(xs_bufs[xs][:, :cw], PS[pb][:, :cw], AF.Identity))
                     if i % 2 else
                     (lambda h, xs=xs, pb=pb, cw=cw: h.tensor_copy(xs_bufs[xs][:, :cw], PS[pb][:, :cw])),
                     reads=["ps%d" % pb], writes=["xs%d" % xs])
                S.dma("sp", "xs%d" % xs, XD[i, :, c0:c0 + cw], xs_bufs[xs][:, :cw],
                      reads=["xs%d" % xs], writes=[("XD", i, n)])
                acc_update_i(xs, n, cw, i == 0, nxs)
                bg_step(bg0, 2)
        bg_drain(bg0)

        def xl_at(o_):
            return [carve(o_ + q * 2048, [128, 512], F32)[0] for q in range(4)]

        def norm_mod(li, s, Hb, xl):
            for n, (c0, cw) in enumerate(CH):
                mm_group(PS[5][:, :cw], [(ones_ff[:, :], ACC[:, c0:c0 + cw])], reads=[("acc", n), "onesff"], writes=["ps5"])
                S.op("act", lambda h, c0=c0, cw=cw: h.activation(ACC[:, c0:c0 + cw], PS[5][:, :cw], AF.Sqrt,
                                                                  bias=eps_t[:, 0:1], scale=1.0 / D),
                     reads=["ps5", "eps"], writes=[("acc", n)])
                S.op("dve", lambda h, c0=c0, cw=cw: h.reciprocal(ACC[:, c0:c0 + cw], ACC[:, c0:c0 + cw]),
                     reads=[("acc", n)], writes=[("acc", n)])
            nl = 0
            for n, (c0, cw) in enumerate(CH):
                t = 0 if n == 0 else 1
                for i in range(DC):
                    q = nl % 4
                    nl += 1
                    S.dma("sp", "xl%d" % q, xl[q][:, :cw], XD[i, :, c0:c0 + cw],
                          reads=[("XD", i, n)], writes=["fx%d" % q])
                    S.op("dve", lambda h, q=q, c0=c0, cw=cw: h.tensor_tensor(
                        xl[q][:, :cw], xl[q][:, :cw], ACC[:, c0:c0 + cw], ALU.mult),
                        reads=["fx%d" % q, ("acc", n)], writes=["fx%d" % q])
                    S.op("act", lambda h, q=q, c0=c0, cw=cw, i=i, t=t: h.activation(
                        Hb[:, i, c0:c0 + cw], xl[q][:, :cw], AF.Identity,
                        bias=shift_col(li, s, i, t), scale=GSv[:, li, s, i, t:t + 1]),
                        reads=["fx%d" % q, "MOD", "GS"], writes=[("H", i, n)])

        def _acc(xs_ap, n, cw_, first, sqt_ap, sqt_key, fxkey):
            c0_ = CH[n][0]
            accv = ACC[:, c0_:c0_ + cw_]
            if first:
                S.op("pool", lambda h: h.tensor_tensor(accv, xs_ap, xs_ap, ALU.mult), reads=[fxkey], writes=[("acc", n)])
            else:
                S.op("pool", lambda h: h.tensor_tensor(sqt_ap[:, :cw_], xs_ap, xs_ap, ALU.mult), reads=[fxkey], writes=[sqt_key])
                S.op("pool", lambda h: h.tensor_tensor(accv, accv, sqt_ap[:, :cw_], ALU.add),
                     reads=[sqt_key], writes=[("acc", n)])

        def ffn(li, f, s, bg=None):
            S.barrier()
            o = 0
            Hb, o = carve(o, [128, DC, T], BF16)
            Ub, o = carve(o, [128, HPP, T], BF16)
            w1b = []
            w3b = []
            w2b = []
            for q in range(2):
                v, o = carve(o, [128, DC, 128], BF16); w1b.append(v)
                v, o = carve(o, [128, DC, 128], BF16); w3b.append(v)
                v, o = carve(o, [128, HPP, 128], BF16); w2b.append(v)
            sab = []
            for q in range(2):
                v, o = carve(o, [128, 512], F32); sab.append(v)
            xsb = []
            for q in range(4):
                v, o = carve(o, [128, 512], F32); xsb.append(v)
            sqt, o = carve_list(o, 2, [128, 512], F32)
            assert o <= ACC_OFF, o
            norm_mod(li, s, Hb, xsb)
            w1s = ffn_w1[li, f].rearrange("(k p) n -> p k n", p=128)
            w3s = ffn_w3[li, f].rearrange("(k p) n -> p k n", p=128)
            w2s = ffn_w2[li, f].rearrange("(j p) n -> p j n", p=128)
            ng = 0
            nx = 0
            nw2 = 0
            for p in range(NPASS):
                for jj in range(HPP):
                    j = p * HPP + jj
                    q = j % 2
                    S.dma("pool", "wa%d" % q, w1b[q], w1s[:, :, j * 128:(j + 1) * 128], writes=["w1b%d" % q])
                    S.dma("pool", "wb%d" % q, w3b[q], w3s[:, :, j * 128:(j + 1) * 128], writes=["w3b%d" % q])
                    for n, (c0, cw) in enumerate(CH):
                        g = ng % 2
                        ng += 1
                        mm_group(PS[g][:, :cw], [(w1b[q][:, k, :], Hb[:, k, c0:c0 + cw]) for k in range(DC)],
                                 reads=["w1b%d" % q] + [("H", k, n) for k in range(DC)], writes=["ps%d" % g])
                        mm_group(PS[2 + g][:, :cw], [(w3b[q][:, k, :], Hb[:, k, c0:c0 + cw]) for k in range(DC)],
                                 reads=["w3b%d" % q] + [("H", k, n) for k in range(DC)], writes=["ps%d" % (2 + g)])
                        S.op("act", lambda h, g=g, cw=cw: h.activation(sab[g][:, :cw], PS[g][:, :cw], AF.Silu),
                             reads=["ps%d" % g], writes=["sa%d" % g])
                        S.op("dve", lambda h, g=g, jj=jj, c0=c0, cw=cw: h.tensor_tensor(
                            Ub[:, jj, c0:c0 + cw], sab[g][:, :cw], PS[2 + g][:, :cw], ALU.mult),
                            reads=["sa%d" % g, "ps%d" % (2 + g)], writes=[("U", jj, n)])
                    bg_step(bg, 4)
                for i in range(DC):
                    q = nw2 % 2
                    nw2 += 1
                    S.dma("pool", "wc%d" % q, w2b[q], w2s[:, p * HPP:(p + 1) * HPP, i * 128:(i + 1) * 128],
                          writes=["w2b%d" % q])
                    for n, (c0, cw) in enumerate(CH):
                        t = 0 if n == 0 else 1
                        g = 4 + (ng % 2)
                        ng += 1
                        xq = nx % 4
                        nx += 1
                        S.dma("sp", "xl%d" % xq, xsb[xq][:, :cw], XD[i, :, c0:c0 + cw],
                              reads=[("XD", i, n)], writes=["fx%d" % xq])
                        mm_group(PS[g][:, :cw], [(w2b[q][:, jj, :], Ub[:, jj, c0:c0 + cw]) for jj in range(HPP)],
                                 reads=["w2b%d" % q] + [("U", jj, n) for jj in range(HPP)], writes=["ps%d" % g])
                        S.op("dve", lambda h, g=g, xq=xq, cw=cw, i=i, t=t: h.scalar_tensor_tensor(
                            xsb[xq][:, :cw], PS[g][:, :cw], HGv[:, li, s, i, t:t + 1], xsb[xq][:, :cw],
                            ALU.mult, ALU.add), reads=["ps%d" % g, "fx%d" % xq, "HG"], writes=["fx%d" % xq])
                        if p == NPASS - 1:
                            _acc(xsb[xq][:, :cw], n, cw, i == 0, sqt[nx % 2], "sqt%d" % (nx % 2), "fx%d" % xq)
                        S.dma("act", "xs%d" % xq, XD[i, :, c0:c0 + cw], xsb[xq][:, :cw],
                              reads=["fx%d" % xq], writes=[("XD", i, n)])
            bg_drain(bg)

        def x_rmw(xsb, nx, i, n, pb, pcols, scal_ap, bias_ap=None, acc=None):
            c0, cw = CH[n]
            xq = nx % 4
            S.dma("sp", "xl%d" % xq, xsb[xq][:, :cw], XD[i, :, c0:c0 + cw],
                  reads=[("XD", i, n)], writes=["fx%d" % xq])
            S.op("dve", lambda h: h.scalar_tensor_tensor(
                xsb[xq][:, :cw], PS[pb][:, pcols:pcols + cw], scal_ap, xsb[xq][:, :cw], ALU.mult, ALU.add),
                reads=["ps%d" % pb, "fx%d" % xq, "HG", "PG"], writes=["fx%d" % xq])
            if bias_ap is not None:
                S.op("dve", lambda h: h.tensor_scalar(xsb[xq][:, :cw], xsb[xq][:, :cw], bias_ap, None, ALU.add),
                     reads=["fx%d" % xq, "G2B2"], writes=["fx%d" % xq])
            if acc is not None:
                _acc(xsb[xq][:, :cw], n, cw, acc[0], acc[1][nx % 2],
                     "sqt%d" % (0 if acc[1][0] is acc[1][1] else nx % 2), "fx%d" % xq)
            S.dma("act", "xs%d" % xq, XD[i, :, c0:c0 + cw], xsb[xq][:, :cw],
                  reads=["fx%d" % xq], writes=[("XD", i, n)])

        def carve_list(o, k, shape, dt):
            out = []
            for _ in range(k):
                v, o = carve(o, shape, dt)
                out.append(v)
            return out, o

        def pool_mixer(li, j):
            S.barrier()
            o = 0
            Hb, o = carve(o, [128, DC, T], BF16)
            norm_mod(li, 1, Hb, xl_at(o))
            S.barrier()
            hp, o = carve(o, [128, PW], F32)
            sA, o = carve(o, [128, PW], F32)
            sB, o = carve(o, [128, PW], F32)
            IC, o = carve(o, [128, PW], F32)
            MK, o = carve(o, [128, WIN], F32)
            DF, o = carve(o, [128, 4, T], BF16)
            wp, o = carve_list(o, 2, [128, 4, 512], BF16)
            xsb, o = carve_list(o, 4, [128, 512], F32)
            sqt, o = carve_list(o, 2, [128, 512], F32)
            assert o <= ACC_OFF, o
            S.dma("sp", "misc", MK, maskw[:, :], writes=["MK"])
            for nm_, b_ in (("hp", hp), ("sA", sA), ("sB", sB)):
                S.op("pool", lambda h, b_=b_: h.memset(b_, 0.0), writes=[nm_])
            nx = 0
            ng = 0
            for g in range(4):
                w = POOLW[g]
                S.dma("sp", "ld0", IC, invcnt[g], writes=["IC"])
                S.dma("pool", "wa%d" % (g % 2), wp[g % 2], pool_w[j, g].rearrange("(k p) n -> p k n", p=128),
                      writes=["wp%d" % (g % 2)])
                for kk in range(4):
                    i = g * 4 + kk
                    S.op("dve", lambda h, i=i: h.tensor_copy(hp[:, 16:272], Hb[:, i, 0:256]),
                         reads=[("H", i, 0)], writes=["hp"])
                    S.op("dve", lambda h, i=i: h.tensor_copy(hp[:, 288:544], Hb[:, i, 256:512]),
                         reads=[("H", i, 0)], writes=["hp"])
                    S.op("dve", lambda h, i=i: h.tensor_tensor(hp[:, 560:560 + WIN], Hb[:, i, 512:T], MK, ALU.mult),
                         reads=[("H", i, n) for n in range(1, NCH)] + ["MK"], writes=["hp"])
                    S.op("dve", lambda h: h.tensor_tensor(sA[:, 1:PW], hp[:, 0:PW - 1], hp[:, 1:PW], ALU.add),
                         reads=["hp"], writes=["sA"])
                    cur, curn, oth, othn = sA, "sA", sB, "sB"
                    for lvl, sh in ((4, 1), (8, 2), (16, 4)):
                        if w >= lvl:
                            S.op("dve", lambda h, cur=cur, oth=oth, sh=sh: h.tensor_tensor(
                                oth[:, sh:PW - sh], cur[:, 0:PW - 2 * sh], cur[:, 2 * sh:PW], ALU.add),
                                reads=[curn], writes=[othn])
                            cur, curn, oth, othn = oth, othn, cur, curn
                    S.op("dve", lambda h, cur=cur, oth=oth: h.tensor_tensor(oth[:, :], cur[:, :], IC[:, :], ALU.mult),
                         reads=[curn, "IC"], writes=[othn])
                    for (sc0, sw, pc0) in SEGS:
                        S.op("dve", lambda h, oth=oth, kk=kk, sc0=sc0, sw=sw, pc0=pc0: h.tensor_tensor(
                            DF[:, kk, sc0:sc0 + sw], oth[:, pc0:pc0 + sw], hp[:, pc0:pc0 + sw], ALU.subtract),
                            reads=[othn, "hp"], writes=[("DF", kk)])
                for oc in range(4):
                    io = g * 4 + oc
                    for n, (c0, cw) in enumerate(CH):
                        t = 0 if n == 0 else 1
                        pb = 4 + (ng % 2)
                        ng += 1
                        mm_group(PS[pb][:, :cw],
                                 [(wp[g % 2][:, kk, oc * 128:(oc + 1) * 128], DF[:, kk, c0:c0 + cw]) for kk in range(4)],
                                 reads=["wp%d" % (g % 2)] + [("DF", kk) for kk in range(4)], writes=["ps%d" % pb])
                        x_rmw(xsb, nx, io, n, pb, 0, PGv[:, j, io, t:t + 1], acc=(io == 0, sqt))
                        nx += 1

        def conv_mixer(li):
            S.barrier()
            A0, B0, C0 = 0, DC * T * 2, DC * T * 2 + DC * PW * 2
            Hb, _ = carve(A0, [128, DC, T], BF16)
            norm_mod(li, 1, Hb, xl_at(DC * T * 2))
            S.barrier()
            Gf, _ = carve(B0, [128, DC * PW], BF16)
            G = Gf.rearrange("p (a b) -> p a b", a=DC)
            o = C0
            w1a, o = carve(o, [128, DC, 128], BF16)
            w1b, o = carve(o, [128, DC, 128], BF16)
            sab, o = carve_list(o, 2, [128, 512], F32)
            MKb, o = carve(o, [128, WIN], BF16)
            assert o <= ARENA_F32 * 4, o
            S.dma("pool", "misc", MKb[:, 0:1056], maskw[:, 0:1056], writes=["MKb"])
            S.dma("pool", "misc", MKb[:, 1056:WIN], maskw[:, 1056:WIN], reads=["MKb"], writes=["MKb"])
            S.op("pool", lambda h: h.memset(Gf, 0.0), writes=["G"])
            w1s = conv_w1.rearrange("(k p) n -> p k n", p=128)
            b1o = VOFF["conv_b1"]
            ng = 0
            for oc in range(DC):
                S.dma("pool", "wa0", w1a, w1s[:, :, oc * 128:(oc + 1) * 128], writes=["cw1a"])
                S.dma("pool", "wb0", w1b, w1s[:, :, (DC + oc) * 128:(DC + oc + 1) * 128], writes=["cw1b"])
                for n, (c0, cw) in enumerate(CH):
                    g = ng % 2
                    ng += 1
                    mm_group(PS[g][:, :cw], [(w1a[:, k, :], Hb[:, k, c0:c0 + cw]) for k in range(DC)],
                             reads=["cw1a"] + [("H", k, n) for k in range(DC)], writes=["ps%d" % g])
                    mm_group(PS[2 + g][:, :cw], [(w1b[:, k, :], Hb[:, k, c0:c0 + cw]) for k in range(DC)],
                             reads=["cw1b"] + [("H", k, n) for k in range(DC)], writes=["ps%d" % (2 + g)])
                    S.op("act", lambda h, g=g, cw=cw, oc=oc: h.activation(
                        sab[g][:, :cw], PS[2 + g][:, :cw], AF.Sigmoid, bias=VEC[:, b1o + DC + oc:b1o + DC + oc + 1]),
                        reads=["ps%d" % (2 + g), "VEC"], writes=["sa%d" % g])
                    if n == 0:
                        for (sc0, sw, pc0) in SEGS[:2]:
                            S.op("dve", lambda h, g=g, oc=oc, sc0=sc0, sw=sw, pc0=pc0: h.scalar_tensor_tensor(
                                G[:, oc, pc0:pc0 + sw], PS[g][:, sc0:sc0 + sw], VEC[:, b1o + oc:b1o + oc + 1],
                                sab[g][:, sc0:sc0 + sw], ALU.add, ALU.mult),
                                reads=["ps%d" % g, "sa%d" % g, "VEC", "G"], writes=[("G", oc)])
                    else:
                        S.op("dve", lambda h, g=g, oc=oc, cw=cw: h.scalar_tensor_tensor(
                            sab[g][:, :cw], PS[g][:, :cw], VEC[:, b1o + oc:b1o + oc + 1],
                            sab[g][:, :cw], ALU.add, ALU.mult),
                            reads=["ps%d" % g, "sa%d" % g, "VEC"], writes=["sa%d" % g])
                        S.op("dve", lambda h, g=g, oc=oc, c0=c0, cw=cw: h.tensor_tensor(
                            G[:, oc, c0 + 48:c0 + 48 + cw], sab[g][:, :cw], MKb[:, c0 - NPR:c0 - NPR + cw], ALU.mult),
                            reads=["sa%d" % g, "MKb", "G"], writes=[("G", oc)])
            S.barrier()
            CB, _ = carve(A0, [128, DC, T], BF16)
            DG, _ = carve_list(C0, 2, [128, CONVW, 128], BF16)
            dwo = VOFF["conv_dw"]
            for c in range(DC):
                q = c % 2
                for k in range(CONVW):
                    S.op("dve", lambda h, q=q, k=k, c=c: h.tensor_scalar(
                        DG[q][:, k, :], identb[:], VEC[:, dwo + k * DC + c:dwo + k * DC + c + 1], None, ALU.mult),
                        reads=["identb", "VEC"], writes=[("DG", q, k)])
                for n, (c0, cw) in enumerate(CH):
                    pb = 4 + (ng % 2)
                    ng += 1
                    segs = SEGS[:2] if n == 0 else [(c0, cw, c0 + 48)]
                    for (sc0, sw, pc0) in segs:
                        mm_group(PS[pb][:, sc0 - c0:sc0 - c0 + sw],
                                 [(DG[q][:, k, :], G[:, c, pc0 + k - 15:pc0 + k - 15 + sw]) for k in range(CONVW)],
                                 reads=[("DG", q, k) for k in range(CONVW)], writes=["ps%d" % pb])
                    S.op("act", lambda h, pb=pb, c=c, c0=c0, cw=cw: h.activation(
                        CB[:, c, c0:c0 + cw], PS[pb][:, :cw], AF.Identity,
                        bias=VEC[:, VOFF["conv_dw_b"] + c:VOFF["conv_dw_b"] + c + 1]),
                        reads=["ps%d" % pb, "VEC"], writes=[("C", c, n)])
            S.barrier()
            o = C0
            MU, o = carve(o, [128, T], F32)
            RS, o = carve(o, [128, T], F32)
            sq1, o = carve(o, [128, 512], BF16)
            tmp, o = carve(o, [128, 512], F32)
            assert o <= ARENA_F32 * 4, o
            for n, (c0, cw) in enumerate(CH):
                mm_group(PS[6][:, :cw], [(ones_b[:], CB[:, c, c0:c0 + cw]) for c in range(DC)],
                         reads=["ones"], writes=["ps6"])
                for c in range(DC):
                    S.op("act", lambda h, c=c, c0=c0, cw=cw: h.activation(sq1[:, :cw], CB[:, c, c0:c0 + cw], AF.Square),
                         writes=["sq1"])
                    def fn(h, c=c, cw=cw):
                        return h.matmul(PS[7][:, :cw], ones_b[:], sq1[:, :cw], start=(c == 0), stop=(c == DC - 1))
                    S.op("pe", fn, reads=["sq1"], writes=["ps7"])
                S.op("act", lambda h, c0=c0, cw=cw: h.activation(MU[:, c0:c0 + cw], PS[6][:, :cw], AF.Identity, scale=1.0 / D),
                     reads=["ps6"], writes=[("MU", n)])
                S.op("dve", lambda h, c0=c0, cw=cw: h.tensor_tensor(tmp[:, :cw], MU[:, c0:c0 + cw], MU[:, c0:c0 + cw], ALU.mult),
                     reads=[("MU", n)], writes=["tmp"])
                S.op("dve", lambda h, cw=cw: h.scalar_tensor_tensor(tmp[:, :cw], PS[7][:, :cw], 1.0 / D, tmp[:, :cw],
                                                                    ALU.mult, ALU.subtract),
                     reads=["ps7", "tmp"], writes=["tmp"])
                S.op("act", lambda h, c0=c0, cw=cw: h.activation(RS[:, c0:c0 + cw], tmp[:, :cw], AF.Sqrt, bias=eps_t[:, 0:1]),
                     reads=["tmp", "eps"], writes=[("RS", n)])
                S.op("dve", lambda h, c0=c0, cw=cw: h.reciprocal(RS[:, c0:c0 + cw], RS[:, c0:c0 + cw]),
                     reads=[("RS", n)], writes=[("RS", n)])
            ZB, _ = carve(B0, [128, DC, T], BF16)
            lg, lb = VOFF["conv_ln_g"], VOFF["conv_ln_b"]
            for n, (c0, cw) in enumerate(CH):
                for c in range(DC):
                    S.op("dve", lambda h, c=c, c0=c0, cw=cw: h.tensor_tensor(
                        tmp[:, :cw], CB[:, c, c0:c0 + cw], MU[:, c0:c0 + cw], ALU.subtract),
                        reads=[("MU", n)], writes=["tmp"])
                    S.op("dve", lambda h, c0=c0, cw=cw: h.tensor_tensor(tmp[:, :cw], tmp[:, :cw], RS[:, c0:c0 + cw], ALU.mult),
                         reads=["tmp", ("RS", n)], writes=["tmp"])
                    S.op("act", lambda h, c=c, c0=c0, cw=cw: h.activation(
                        ZB[:, c, c0:c0 + cw], tmp[:, :cw], AF.Silu, bias=VEC[:, lb + c:lb + c + 1],
                        scale=VEC[:, lg + c:lg + c + 1]), reads=["tmp", "VEC"], writes=[("Z", c, n)])
            S.barrier()
            o = C0
            wz, o = carve_list(o, 1, [128, DC, 128], BF16)
            wz = wz * 2
            xsb, o = carve_list(o, 4, [128, 512], F32)
            assert o <= ACC_OFF, o
            sq1c, _ = carve(B0 + DC * T * 2, [128, 512], F32)
            w2s = conv_w2.rearrange("(k p) n -> p k n", p=128)
            nx = 0
            for oc in range(DC):
                q = oc % 2
                S.dma("pool", "wa0", wz[0], w2s[:, :, oc * 128:(oc + 1) * 128], writes=["wz0"])
                q = 0
                for n, (c0, cw) in enumerate(CH):
                    t = 0 if n == 0 else 1
                    pb = 4 + (ng % 2)
                    ng += 1
                    mm_group(PS[pb][:, :cw], [(wz[q][:, k, :], ZB[:, k, c0:c0 + cw]) for k in range(DC)],
                             reads=["wz%d" % q], writes=["ps%d" % pb])
                    x_rmw(xsb, nx, oc, n, pb, 0, HGv[:, li, 1, oc, t:t + 1], G2B2v[:, oc, t:t + 1],
                          acc=(oc == 0, [sq1c, sq1c]))
                    nx += 1

        def mla_mixer(li):
            _stop = float(os.environ.get('KSTOP', '99'))
            S.barrier()
            Hb, _ = carve(0, [128, DC, T], BF16)
            norm_mod(li, 1, Hb, xl_at(DC * T * 2))
            S.barrier()
            R_CQR, R_KVR, R_KPR = 83968, 125952, 167936
            CQR, _ = carve(R_CQR, [128, 4, T], F32)
            KVR, _ = carve(R_KVR, [128, 4, T], F32)
            KPR, o = carve(R_KPR, [128, T], F32)
            wd, o = carve_list(o, 2, [128, DC, 128], BF16)
            assert o <= ARENA_F32 * 4, o
            ng = 0
            jobs = [(w_dq, oc * 128, 128, CQR[:, oc, :]) for oc in range(4)] + \
                   [(w_dkv, oc * 128, 128, KVR[:, oc, :]) for oc in range(4)] + [(w_dkv, 512, 64, KPR)]
            for ji, (wsrc, col0, m, dst) in enumerate(jobs):
                q = ji % 2
                S.dma("pool", "wa%d" % q, wd[q][:, :, :m], wsrc.rearrange("(k p) n -> p k n", p=128)[:, :, col0:col0 + m],
                      writes=["wd%d" % q])
                for n, (c0, cw) in enumerate(CH):
                    pb = ng % 2
                    ng += 1
                    mm_group(PS[pb][:m, :cw], [(wd[q][:, k, :m], Hb[:, k, c0:c0 + cw]) for k in range(DC)],
                             reads=["wd%d" % q] + [("H", k, n) for k in range(DC)], writes=["ps%d" % pb])
                    S.op("act" if pb else "dve",
                         (lambda h, pb=pb, m=m, dst=dst, c0=c0, cw=cw: h.activation(dst[:m, c0:c0 + cw], PS[pb][:m, :cw], AF.Identity))
                         if pb else
                         (lambda h, pb=pb, m=m, dst=dst, c0=c0, cw=cw: h.tensor_copy(dst[:m, c0:c0 + cw], PS[pb][:m, :cw])),
                         reads=["ps%d" % pb], writes=[("raw", ji, n)])
            if _stop <= 1:
                return
            S.barrier()
            CQ, o = carve(0, [128, 4, T], BF16)
            CKV, o = carve(o, [128, 4, NK], BF16)
            KPF, o = carve(o, [128, NK], BF16)
            KPSW, o = carve(o, [128, NK], BF16)
            assert o <= 83968
            o = R_KPR + T * 4
            rt, o = carve(o, [128, 512], F32)
            sq1, o = carve(o, [128, 512], BF16)
            for (SRC, which) in ((CQR, 0), (KVR, 1)):
                gno = VOFF["q_norm"] if which == 0 else VOFF["kv_norm"]
                for n, (c0, cw) in enumerate(CH):
                    for c in range(4):
                        S.op("act", lambda h, c=c, c0=c0, cw=cw, SRC=SRC: h.activation(sq1[:, :cw], SRC[:, c, c0:c0 + cw], AF.Square),
                             writes=["sq1"])
                        def fn(h, c=c, cw=cw):
                            return h.matmul(PS[5][:, :cw], ones_b[:], sq1[:, :cw], start=(c == 0), stop=(c == 3))
                        S.op("pe", fn, reads=["sq1"], writes=["ps5"])
                    S.op("act", lambda h, cw=cw: h.activation(rt[:, :cw], PS[5][:, :cw], AF.Sqrt, bias=eps_t[:, 0:1], scale=1.0 / 512),
                         reads=["ps5"], writes=["rt"])
                    S.op("dve", lambda h, cw=cw: h.reciprocal(rt[:, :cw], rt[:, :cw]), reads=["rt"], writes=["rt"])
                    for c in range(4):
                        S.op("dve", lambda h, c=c, c0=c0, cw=cw, SRC=SRC: h.tensor_tensor(
                            SRC[:, c, c0:c0 + cw], SRC[:, c, c0:c0 + cw], rt[:, :cw], ALU.mult),
                            reads=["rt"], writes=[("nrm", which, c, n)])
                        dst = CQ if which == 0 else KVR
                        S.op("act", lambda h, c=c, c0=c0, cw=cw, SRC=SRC, dst=dst, gno=gno: h.activation(
                            dst[:, c, c0:c0 + cw], SRC[:, c, c0:c0 + cw], AF.Identity, scale=VEC[:, gno + c:gno + c + 1]),
                            reads=[("nrm", which, c, n)], writes=[("nrm2", which, c, n)])
            if _stop <= 2:
                return
            S.barrier()
            o = R_CQR
            CKS, o = carve(o, [128, 4, 2048], BF16)
            KPS, o = carve(o, [128, 2048], BF16)
            ct, o = carve_list(o, 4, [128, 576], F32)
            so, o = carve_list(o, 2, [128, 512], F32)
            sk, o = carve_list(o, 2, [128, 64], F32)
            assert o <= R_KVR, o
            OWN0 = NPR + HALO
            for c in range(4):
                S.op("dve", lambda h, c=c: h.tensor_copy(CKV[:, c, 0:NPR], KVR[:, c, 0:NPR]), writes=[("CKV", c)])
                S.op("act", lambda h, c=c: h.activation(CKS[:, c, :], KVR[:, c, OWN0:OWN0 + 2048], AF.Identity), writes=[("CKS", c)])
            S.op("dve", lambda h: h.tensor_copy(KPF[0:64, 0:NPR], KPR[0:64, 0:NPR]), writes=["KPF"])
            S.op("act", lambda h: h.activation(KPS[0:64, :], KPR[0:64, OWN0:OWN0 + 2048], AF.Identity), writes=["KPS"])
            S.op("pool", lambda h: h.memset(KPSW[:, :], 0.0), writes=["KPSW"])
            if _stop <= 2.1:
                return
            for c in range(4):
                S.dma("sp", "gth", bin_[c * 128:(c + 1) * 128, :], CKS[:, c, :], reads=[("CKS", c)], writes=["bin"])
            S.dma("sp", "gth", binp[:, :], KPS[0:64, :], reads=["KPS"], writes=["binp"])
            if _stop <= 2.2:
                return
            S.coll(lambda h: h.collective_compute("AllGather", ALU.bypass, replica_groups=PAIRS,
                                                  ins=[bin_[:, :]], outs=[bout[:, :]]),
                   reads=["bin"], writes=["bout"])
            S.coll(lambda h: h.collective_compute("AllGather", ALU.bypass, replica_groups=PAIRS,
                                                  ins=[binp[:, :]], outs=[boutp[:, :]]),
                   reads=["binp"], writes=["boutp"])
            if _stop <= 2.3:
                return
            for r in range(2):
                for c in range(4):
                    S.dma("sp", "tb%d" % c, CKV[:, c, 1024 + r * 2048:1024 + (r + 1) * 2048],
                          bout[r * 512 + c * 128:r * 512 + (c + 1) * 128, :], reads=["bout"], writes=[("CKVg", r, c)])
                S.dma("sp", "ld0", KPF[0:64, 1024 + r * 2048:1024 + (r + 1) * 2048],
                      boutp[r * 64:(r + 1) * 64, :], reads=["boutp", "KPF"], writes=[("KPFg", r)])
                for blk in range(4):
                    src0 = r * 64 + PERM[blk * 16]
                    S.dma("sp", "ld1", KPSW[blk * 16:(blk + 1) * 16, 1024 + r * 2048:1024 + (r + 1) * 2048],
                          boutp[src0:src0 + 16, :], reads=["boutp", "KPSW"], writes=[("KPSWg", r, blk)])
            if _stop <= 3:
                return
            for ti in range(4):
                q = ti % 2
                transpose_group([PS[q][:, c * 128:(c + 1) * 128] for c in range(4)],
                                [KVR[:, c, ti * 128:(ti + 1) * 128] for c in range(4)], reads=["ident"], writes=["ps%d" % q])
                S.op("dve", lambda h, q=q: h.tensor_copy(so[q][:, :], PS[q][:, :]), reads=["ps%d" % q], writes=["so%d" % q])
                S.dma("sp", "out%d" % q, sck_o[ti * 128:(ti + 1) * 128, :], so[q][:, :], reads=["so%d" % q], writes=[("sck", ti)])
                transpose_group([PS[2 + q][:, 0:64]], [KPR[0:64, ti * 128:(ti + 1) * 128]], reads=["ident"], writes=["ps%d" % (2 + q)])
                S.op("act", lambda h, q=q: h.activation(sk[q][:, :], PS[2 + q][:, 0:64], AF.Identity),
                     reads=["ps%d" % (2 + q)], writes=["sk%d" % q])
                S.dma("sp", "xs%d" % q, skp_o[ti * 128:(ti + 1) * 128, :], sk[q][:, :], reads=["sk%d" % q], writes=[("skp", ti)])
            if _stop <= 4:
                return
            for ti in range(4):
                S.dma("sp", "ld2", ct[ti][:, 0:512], cck[ti * 128:(ti + 1) * 128, :], writes=[("ct", ti)])
                S.dma("sp", "ld3", ct[ti][:, 512:576], ckp[ti * 128:(ti + 1) * 128, :], writes=[("ctp", ti)])
            for c in range(4):
                pb = 4 + (c % 2)
                transpose_group([PS[pb][:, ti * 128:(ti + 1) * 128] for ti in range(4)],
                                [ct[ti][:, c * 128:(c + 1) * 128] for ti in range(4)],
                                reads=[("ct", ti) for ti in range(4)] + ["ident"], writes=["ps%d" % pb])
                S.op("dve", lambda h, c=c, pb=pb: h.tensor_copy(CKV[:, c, 512:1024], PS[pb][:, :]),
                     reads=["ps%d" % pb], writes=[("CKVc", c)])
            transpose_group([PS[6][0:64, ti * 128:(ti + 1) * 128] for ti in range(4)],
                            [ct[ti][:, 512:576] for ti in range(4)],
                            reads=[("ctp", ti) for ti in range(4)] + ["ident"], writes=["ps6"])
            S.op("dve", lambda h: h.tensor_copy(KPF[0:64, 512:1024], PS[6][0:64, :]), reads=["ps6", "KPF"], writes=["KPFc"])
            if _stop <= 5:
                return
            S.barrier()
            KPG, o = carve(R_KVR, [128, NK], BF16)
            SQPE, o = carve(o, [128, NK], BF16)
            o = R_CQR
            tk, o = carve_list(o, 4, [128, 512], F32)
            ta, o = carve(o, [128, 512], F32)
            tb, o = carve(o, [128, 512], F32)
            go = VOFF["gains"]
            S.op("pool", lambda h: h.memset(KPG[64:128, :], 0.0), writes=["KPGz"])
            for kc in range(NK // 512):
                ks = slice(kc * 512, (kc + 1) * 512)
                q = kc % 2
                S.dma("sp", "tb%d" % (2 * q), tk[2 * q][0:64, :], cosk[:, ks], writes=[("tk", 2 * q)])
                S.dma("sp", "tb%d" % (2 * q + 1), tk[2 * q + 1][0:64, :], sink[:, ks], writes=[("tk", 2 * q + 1)])
                S.op("dve", lambda h, ks=ks, q=q: h.scalar_tensor_tensor(
                    ta[0:64, :], KPF[0:64, ks], VEC[0:64, go + 4:go + 5], tk[2 * q][0:64, :], ALU.mult, ALU.mult),
                    reads=[("tk", 2 * q)], writes=["ta"])
                S.op("dve", lambda h, ks=ks, q=q: h.scalar_tensor_tensor(
                    tb[0:64, :], KPSW[0:64, ks], VEC[0:64, go + 5:go + 6], tk[2 * q + 1][0:64, :], ALU.mult, ALU.mult),
                    reads=[("tk", 2 * q + 1)], writes=["tb"])
                S.op("dve", lambda h, ks=ks: h.tensor_tensor(KPG[0:64, ks], ta[0:64, :], tb[0:64, :], ALU.add),
                     reads=["ta", "tb"], writes=[("KPG", kc)])
                S.op("act", lambda h, ks=ks: h.activation(SQPE[0:64, ks], KPF[0:64, ks], AF.Square), writes=[("SQPE", kc)])
            if _stop <= 6:
                return
            S.barrier()
            o = 61952
            KN, o = carve(o, [128, NK], BF16)
            Vt, o = carve(o, [128, NKT, 128], BF16)
            assert o <= 83968
            o = R_CQR
            OT, o = carve(o, [128, 4, T], BF16)
            SQN, o = carve(o, [128, NK], BF16)
            QN, o = carve(o, [128, T], BF16)
            QR, o = carve(o, [128, T], BF16)
            assert o <= R_KVR, o
            o = R_KVR + 2 * NK * 2
            PT, o = carve_list(o, 4, [128, 512], BF16)
            wkv, o = carve(o, [128, 4, 256], BF16)
            wq, o = carve(o, [128, 4, 192], BF16)
            wqs, o = carve(o, [128, 4, 64], BF16)
            wo, o = carve_list(o, 2, [128, 4, 128], BF16)
            RQ, o = carve(o, [128, 512], F32)
            ta, o = carve(o, [128, 512], F32)
            tb, o = carve(o, [128, 512], F32)
            tq, o = carve_list(o, 2, [128, 512], F32)
            sqa, o = carve(o, [128, 512], BF16)
            sqb, o = carve(o, [128, 512], BF16)
            RK, o = carve(o, [128, NKT], F32)
            rden, o = carve(o, [128, 512], F32)
            xsb, o = carve_list(o, 4, [128, 512], F32)
            sqt, o = carve_list(o, 2, [128, 512], F32)
            assert o <= ACC_OFF, o
            S.op("pool", lambda h: h.memset(QR[64:128, :], 0.0), writes=["QRz"])
            ukv_s = w_ukv.rearrange("(k p) n -> p k n", p=128)
            uq_s = w_uq.rearrange("(k p) n -> p k n", p=128)
            uqs_s = w_uqs.rearrange("(k p) n -> p k n", p=128)
            att_jobs = [((0, 256), (0, 2)), ((256, 256), (2, 4))] + [(CH[n], (4, NKT)) for n in range(1, NCH)]
            nx = 0
            nwo = 0
            inv192 = 1.0 / 192.0
            for hd in range(16):
                hh = hd % 4
                S.dma("pool", "wa0", wkv, ukv_s[:, :, hd * 256:(hd + 1) * 256], writes=["wkv"])
                S.dma("pool", "wb0", wq, uq_s[:, :, hd * 192:(hd + 1) * 192], writes=["wq"])
                S.dma("pool", "wc0", wqs, uqs_s[:, :, hd * 64:(hd + 1) * 64], writes=["wqs"])
                for kc in range(NK // 512):
                    ks = slice(kc * 512, (kc + 1) * 512)
                    mm_group(PS[4][:, :], [(wkv[:, k, 0:128], CKV[:, k, ks]) for k in range(4)], reads=["wkv"], writes=["ps4"])
                    S.op("act", lambda h, ks=ks: h.activation(KN[:, ks], PS[4][:, :], AF.Identity), reads=["ps4"], writes=[("KN", kc)])
                    S.op("act", lambda h, ks=ks: h.activation(SQN[:, ks], PS[4][:, :], AF.Square), reads=["ps4"], writes=[("SQN", kc)])
                for kg in range(NKT // 4):
                    for t4 in range(4):
                        kt = kg * 4 + t4
                        mm_group(PS[5][:, t4 * 128:(t4 + 1) * 128],
                                 [(CKV[:, k, kt * 128:(kt + 1) * 128], wkv[:, k, 128:256]) for k in range(4)],
                                 reads=["wkv"], writes=["ps5"])
                    S.op("dve", lambda h, kg=kg: h.tensor_copy(
                        Vt[:, kg * 4:(kg + 1) * 4, :].rearrange("p a b -> p (a b)"), PS[5][:, :]),
                        reads=["ps5"], writes=[("V", kg)])
                for kt in range(NKT):
                    kc = kt // 4
                    mm_group(PS[7][:, kt:kt + 1],
                             [(SQN[:, kt * 128:(kt + 1) * 128], ones_b[:, 0:1]),
                              (SQPE[0:64, kt * 128:(kt + 1) * 128], ones_b[0:64, 0:1])],
                             reads=[("SQN", kc)], writes=["ps7"])
                S.op("act", lambda h: h.activation(RK[:, :], PS[7][:, 0:NKT], AF.Sqrt, bias=eps_t[:, 0:1], scale=inv192),
                     reads=["ps7"], writes=["RK"])
                S.op("dve", lambda h: h.reciprocal(RK[:, :], RK[:, :]), reads=["RK"], writes=["RK"])
                S.op("dve", lambda h: h.tensor_scalar(RK[:, :], RK[:, :], float(192.0 ** -0.5), None, ALU.mult),
                     reads=["RK"], writes=["RK"])
                if _stop <= 7:
                    return
                for n, (c0, cw) in enumerate(CH):
                    S.dma("sp", "tb0", tq[0][0:64, :cw], cosq[:, c0:c0 + cw], writes=["tq0"])
                    S.dma("sp", "tb1", tq[1][0:64, :cw], sinq[:, c0:c0 + cw], writes=["tq1"])
                    mm_group(PS[4][:, :cw], [(wq[:, k, 0:128], CQ[:, k, c0:c0 + cw]) for k in range(4)], reads=["wq"], writes=["ps4"])
                    mm_group(PS[5][0:64, :cw], [(wq[:, k, 128:192], CQ[:, k, c0:c0 + cw]) for k in range(4)], reads=["wq"], writes=["ps5"])
                    mm_group(PS[6][0:64, :cw], [(wqs[:, k, :], CQ[:, k, c0:c0 + cw]) for k in range(4)], reads=["wqs"], writes=["ps6"])
                    S.op("act", lambda h, cw=cw: h.activation(sqa[:, :cw], PS[4][:, :cw], AF.Square), reads=["ps4"], writes=["sqa"])
                    S.op("act", lambda h, cw=cw: h.activation(sqb[0:64, :cw], PS[5][0:64, :cw], AF.Square), reads=["ps5"], writes=["sqb"])
                    mm_group(PS[7][:, :cw], [(ones_b[:, :], sqa[:, :cw]), (ones_b[0:64, :], sqb[0:64, :cw])],
                             reads=["sqa", "sqb"], writes=["ps7"])
                    S.op("act", lambda h, cw=cw: h.activation(RQ[:, :cw], PS[7][:, :cw], AF.Sqrt, bias=eps_t[:, 0:1], scale=inv192),
                         reads=["ps7"], writes=["RQ"])
                    S.op("dve", lambda h, cw=cw: h.reciprocal(RQ[:, :cw], RQ[:, :cw]), reads=["RQ"], writes=["RQ"])
                    S.op("dve", lambda h, c0=c0, cw=cw: h.scalar_tensor_tensor(
                        QN[:, c0:c0 + cw], PS[4][:, :cw], GQK[:, 0:1], RQ[:, :cw], ALU.mult, ALU.mult),
                        reads=["ps4", "RQ"], writes=[("QN", n)])
                    S.op("dve", lambda h, cw=cw: h.scalar_tensor_tensor(
                        ta[0:64, :cw], PS[5][0:64, :cw], VEC[0:64, go + 2:go + 3], RQ[0:64, :cw], ALU.mult, ALU.mult),
                        reads=["ps5", "RQ"], writes=["ta"])
                    S.op("dve", lambda h, cw=cw: h.scalar_tensor_tensor(
                        tb[0:64, :cw], PS[6][0:64, :cw], VEC[0:64, go + 3:go + 4], RQ[0:64, :cw], ALU.mult, ALU.mult),
                        reads=["ps6", "RQ"], writes=["tb"])
                    S.op("dve", lambda h, cw=cw: h.tensor_tensor(ta[0:64, :cw], ta[0:64, :cw], tq[0][0:64, :cw], ALU.mult),
                         reads=["ta", "tq0"], writes=["ta"])
                    S.op("dve", lambda h, cw=cw: h.tensor_tensor(tb[0:64, :cw], tb[0:64, :cw], tq[1][0:64, :cw], ALU.mult),
                         reads=["tb", "tq1"], writes=["tb"])
                    S.op("dve", lambda h, c0=c0, cw=cw: h.tensor_tensor(QR[0:64, c0:c0 + cw], ta[0:64, :cw], tb[0:64, :cw], ALU.add),
                         reads=["ta", "tb"], writes=[("QR", n)])
                if _stop <= 8:
                    return
                for (q0, qw), (k0, k1) in att_jobs:
                    qn_ = [n for n, (a, w_) in enumerate(CH) if a < q0 + qw and q0 < a + w_]
                    qreads = [("QN", n) for n in qn_] + [("QR", n) for n in qn_]

                    SB = (0, 1, 4, 5)
                    LA = 3

                    def score(kt, slot):
                        mm_group(PS[SB[slot]][:, :qw],
                                 [(KN[:, kt * 128:(kt + 1) * 128], QN[:, q0:q0 + qw]),
                                  (KPG[:, kt * 128:(kt + 1) * 128], QR[:, q0:q0 + qw])],
                                 reads=[("KN", kt // 4), "QRz"] + qreads, writes=["ps%d" % SB[slot]])
                    for a_ in range(min(LA, k1 - k0)):
                        score(k0 + a_, a_ % 4)
                    for kt in range(k0, k1):
                        sl = (kt - k0) % 4
                        if kt + LA < k1:
                            score(kt + LA, (kt - k0 + LA) % 4)
                        S.op("act", lambda h, kt=kt, sl=sl, qw=qw: h.activation(
                            PT[sl][:, :qw], PS[SB[sl]][:, :qw], AF.Exp, bias=NEGC[:, 0:1], scale=RK[:, kt:kt + 1]),
                            reads=["ps%d" % SB[sl], "RK"], writes=["pt%d" % sl])
                        def fpv(h, kt=kt, sl=sl, qw=qw):
                            return h.matmul(PS[2][:, :qw], Vt[:, kt, :], PT[sl][:, :qw], start=(kt == k0), stop=(kt == k1 - 1))
                        S.op("pe", fpv, reads=["pt%d" % sl, ("V", kt // 4)], writes=["ps2"])
                        def fdn(h, kt=kt, sl=sl, qw=qw):
                            return h.matmul(PS[3][:, :qw], ones_b[:, :], PT[sl][:, :qw], start=(kt == k0), stop=(kt == k1 - 1))
                        S.op("pe", fdn, reads=["pt%d" % sl], writes=["ps3"])
                    S.op("dve", lambda h, qw=qw: h.reciprocal(rden[:, :qw], PS[3][:, :qw]), reads=["ps3"], writes=["rden"])
                    S.op("dve", lambda h, q0=q0, qw=qw, hh=hh: h.tensor_tensor(
                        OT[:, hh, q0:q0 + qw], PS[2][:, :qw], rden[:, :qw], ALU.mult),
                        reads=["ps2", "rden"], writes=[("OT", hh, q0)])
                if _stop <= 9:
                    return
                if hh == 3:
                    g4 = hd // 4
                    wos = w_o[g4 * 512:(g4 + 1) * 512, :].rearrange("(k p) n -> p k n", p=128)
                    for oc in range(DC):
                        q = nwo % 2
                        nwo += 1
                        S.dma("pool", "wc1" if q else "wb1", wo[q], wos[:, :, oc * 128:(oc + 1) * 128], writes=["wo%d" % q])
                        for n, (c0, cw) in enumerate(CH):
                            t = 0 if n == 0 else 1
                            oreads = [("OT", h4, a) for h4 in range(4) for (a, w_) in [j_[0] for j_ in att_jobs]]
                            mm_group(PS[6][:, :cw], [(wo[q][:, h4, :], OT[:, h4, c0:c0 + cw]) for h4 in range(4)],
                                     reads=["wo%d" % q] + oreads, writes=["ps6"])
                            x_rmw(xsb, nx, oc, n, 6, 0, HGv[:, li, 1, oc, t:t + 1],
                                  acc=((oc == 0, sqt) if g4 == 3 else None))
                            nx += 1

        eps_t = _es.enter_context(nc.sbuf_tensor("sb_eps", [128, 1], F32))
        if True:
            S.op("dve", lambda h: h.memset(eps_t[:], EPS), writes=["eps"])
            for li in range(nlayers):
                if do_ffn:
                    ffn(li, 0, 0)
                if do_mix:
                    if li % 3 == 0:
                        pool_mixer(li, li // 3)
                    elif li % 3 == 1:
                        mla_mixer(li)
                    else:
                        conv_mixer(li)
                if do_ffn:
                    ffn(li, 1, 2)

            S.barrier()
            o = 0
            xf = []
            for q in range(2):
                v, o = carve(o, [128, DC, 128], F32); xf.append(v)
            yo = []
            for q in range(2):
                v, o = carve(o, [128, D], F32); yo.append(v)
            tiles = [(c, 128, c) for c in range(0, NPR, 128)] + \
                    [(NPR + HALO + c, 128, NPR + c) for c in range(0, 2048, 128)]
            XDt = XD.rearrange("c p t -> p c t")
            for ti, (c0, tw, r0) in enumerate(tiles):
                q = ti % 2
                n_of = [n for n, (a, w) in enumerate(CH) if a < c0 + tw and c0 < a + w]
                S.dma("sp", "ld%d" % q, xf[q], XDt[:, :, c0:c0 + tw],
                      reads=[("XD", i, n) for i in range(DC) for n in n_of], writes=["xf%d" % q])
                for i4 in range(4):
                    pb = (ti * 4 + i4) % 4
                    transpose_group([PS[pb][:, k * 128:(k + 1) * 128] for k in range(4)],
                                    [xf[q][:, i4 * 4 + k, :] for k in range(4)],
                                    reads=["xf%d" % q, "ident"], writes=["ps%d" % pb])
                    S.op("act" if i4 % 2 else "dve",
                         (lambda h, q=q, pb=pb, i4=i4: h.activation(yo[q][:, i4 * 512:(i4 + 1) * 512], PS[pb][:, :], AF.Identity))
                         if i4 % 2 else
                         (lambda h, q=q, pb=pb, i4=i4: h.tensor_copy(yo[q][:, i4 * 512:(i4 + 1) * 512], PS[pb][:, :])),
                         reads=["ps%d" % pb], writes=[("yo", q, i4)])
                S.dma("sp", "out%d" % q, yout[r0:r0 + tw, :], yo[q][:, :],
                      reads=[("yo", q, i4) for i4 in range(4)], writes=[("yout", ti)])
            S.barrier()
    build_program.last_counts = {n: e.count * e.inc for n, e in S.engs.items()}
    return nc


VOFF = {}
_o = 0
for _nm, _rows in [("b_ada", DEPTH * 144), ("norm_g", DEPTH * 3 * DC), ("pool_scale", 2 * DC), ("conv_b1", 2 * DC),
                   ("conv_dw", CONVW * DC), ("conv_dw_b", DC), ("conv_ln_g", DC), ("conv_ln_b", DC), ("conv_b2", DC),
                   ("q_norm", 4), ("kv_norm", 4), ("gains", 6)]:
    VOFF[_nm] = _o
    _o += _rows
NVEC = _o
NVEC_PAD = ((NVEC + 127) // 128) * 128


def _vec_table(inp):
    f = lambda k: np.asarray(inp[k], np.float32).reshape(-1, 128)
    qg = np.asarray(inp["mla_q_gain"], np.float32).reshape(192)
    kg = np.asarray(inp["mla_k_gain"], np.float32).reshape(192)
    perm = np.asarray(PERM)

    def row(v):
        r = np.zeros((1, 128), np.float32)
        r[0, :v.shape[0]] = v
        return r
    gains = [row(qg[:128]), row(kg[:128]), row(qg[128:]), row(qg[128:][perm]), row(kg[128:]), row(kg[128:][perm])]
    rows = [f("b_ada"), f("norm_g"), f("pool_scale"), f("conv_b1"), f("conv_dw"), f("conv_dw_b"), f("conv_ln_g"),
            f("conv_ln_b"), f("conv_b2"), f("mla_q_norm"), f("mla_kv_norm")] + gains
    tab = np.concatenate(rows, axis=0)
    assert tab.shape[0] == NVEC, (tab.shape, NVEC)
    out = np.zeros((NVEC_PAD, 128), np.float32)
    out[:NVEC] = tab
    return out


def _rope_tables(pos):
    pos = np.asarray(pos, np.int64)
    inv_freq = (np.float32(10000.0) ** (-np.arange(0, 32, 2, dtype=np.float32) / np.float32(32))).astype(np.float32)
    row = (pos // 64).astype(np.float32)
    col = (pos % 64).astype(np.float32)
    cos = np.zeros((64, pos.shape[0]), np.float32)
    sin = np.zeros((64, pos.shape[0]), np.float32)
    for d in range(64):
        a, b, fi = d // 32, (d % 32) // 16, d % 16
        ang = ((row if a == 0 else col) * inv_freq[fi]).astype(np.float32)
        cos[d] = np.cos(ang)
        sin[d] = (-np.sin(ang)) if b == 0 else np.sin(ang)
    return cos, sin


def _invcnt(tseq, pos, valid):
    out = np.zeros((4, pos.shape[0]), np.float32)
    for g, w in enumerate(POOLW):
        lo = np.clip(pos - w // 2, 0, tseq - 1)
        hi = np.clip(pos + w // 2 - 1, 0, tseq - 1)
        out[g] = np.where(valid, 1.0 / (hi - lo + 1).astype(np.float32), 0.0)
    return out


def make_in_maps(inp, ncores=NCORES, do_ffn=True):
    xp = np.asarray(inp["x_prompt"], np.float32)
    xs = np.asarray(inp["x_sample"], np.float32)
    c = np.asarray(inp["c"], np.float32)
    cctx = np.asarray(inp["c_ctx"], np.float32)
    w_uq = np.ascontiguousarray(inp["mla_w_uq"][0], dtype=np.float32)
    sw_cols = np.concatenate([h * 192 + 128 + np.asarray(PERM) for h in range(16)])
    cosk = np.ones((64, NK), np.float32)
    sink = np.zeros((64, NK), np.float32)
    ck, sk = _rope_tables(np.arange(DSEQ))
    cosk[:, 1024:] = ck
    sink[:, 1024:] = sk
    f32 = lambda a: np.ascontiguousarray(a, dtype=np.float32)
    shared = {
        "ident": np.eye(128, dtype=np.float32),
        "vecs": _vec_table(inp),
        "w_ada": f32(inp["w_ada"]) if do_ffn else None, "ffn_w1": f32(inp["ffn_w1"]) if do_ffn else None,
        "ffn_w3": f32(inp["ffn_w3"]) if do_ffn else None, "ffn_w2": f32(inp["ffn_w2"]) if do_ffn else None,
        "pool_w": f32(inp["pool_w"]),
        "mla_w_dq": f32(inp["mla_w_dq"][0]), "mla_w_uq": w_uq, "mla_w_uq_sw": f32(w_uq[:, sw_cols]),
        "mla_w_dkv": f32(inp["mla_w_dkv"][0]), "mla_w_ukv": f32(inp["mla_w_ukv"][0]), "mla_w_o": f32(inp["mla_w_o"][0]),
        "conv_w1": f32(inp["conv_w1"][0]), "conv_w2": f32(inp["conv_w2"][0]),
        "cosk": cosk, "sink": sink,
        "gains": np.concatenate([np.asarray(inp["mla_q_gain"], np.float32).reshape(1, 192),
                                 np.asarray(inp["mla_k_gain"], np.float32).reshape(1, 192)], axis=1),
    }
    if not do_ffn:
        for k_ in ("w_ada", "ffn_w1", "ffn_w3", "ffn_w2"):
            shared.pop(k_)
    ic_prompt = _invcnt(SEQ, np.arange(SEQ), np.ones(SEQ, bool))
    maps = []
    for r in range(ncores):
        b, half = r // 2, r % 2
        xin = np.zeros((T, D), np.float32)
        xin[0:256] = xp[2 * r]
        xin[256:512] = xp[2 * r + 1]
        start = half * 2048 - HALO
        pos = start + np.arange(WIN)
        valid = (pos >= 0) & (pos < DSEQ)
        xin[NPR:][valid] = xs[b, pos[valid]]
        m = dict(shared)
        m["xin"] = xin
        m["cond2"] = np.stack([cctx, c[b]]).astype(np.float32)
        m["maskw"] = np.ascontiguousarray(np.broadcast_to(valid.astype(np.float32)[None, :], (128, WIN)))
        ic = np.zeros((4, PW), np.float32)
        ic[:, 16:272] = ic_prompt
        ic[:, 288:544] = ic_prompt
        ic[:, 560:560 + WIN] = _invcnt(DSEQ, np.clip(pos, 0, DSEQ - 1), valid)
        m["invcnt"] = np.ascontiguousarray(np.broadcast_to(ic[:, None, :], (4, 128, PW)))
        cq = np.ones((64, T), np.float32)
        sq = np.zeros((64, T), np.float32)
        cw_, sw_ = _rope_tables(np.clip(pos, 0, DSEQ - 1))
        cq[:, NPR:] = cw_
        sq[:, NPR:] = sw_
        m["cosq"] = cq
        m["sinq"] = sq
        m["cck"] = f32(inp["cache_ckv"][b, 0])
        m["ckp"] = f32(inp["cache_kpe"][b, 0])
        maps.append(m)
    return maps


_NC_CACHE = {}


def run(inp, nlayers=DEPTH, do_mix=True, ncores=NCORES, do_ffn=True):
    key = (nlayers, do_mix, ncores, do_ffn)
    if key not in _NC_CACHE:
        _NC_CACHE[key] = build_program(nlayers, do_mix, ncores, do_ffn)
    nc = _NC_CACHE[key]
    res = run_bass_kernel_spmd(nc, make_in_maps(inp, ncores, do_ffn), core_ids=list(range(ncores)))
    return res.results


def kernel(**inputs):
    results = run(inputs)
    y_prompt = np.zeros((16, SEQ, D), np.float32)
    y_sample = np.zeros((4, DSEQ, D), np.float32)
    state_ckv = np.zeros((16, 1, SEQ, 512), np.float32)
    state_kpe = np.zeros((16, 1, SEQ, 64), np.float32)
    for r in range(NCORES):
        y = np.asarray(results[r]["yout"])
        y_prompt[2 * r] = y[0:256]
        y_prompt[2 * r + 1] = y[256:512]
        b, half = r // 2, r % 2
        y_sample[b, half * 2048:(half + 1) * 2048] = y[512:]
        a = np.asarray(results[r]["sck"])
        k = np.asarray(results[r]["skp"])
        state_ckv[2 * r, 0] = a[0:256]
        state_ckv[2 * r + 1, 0] = a[256:512]
        state_kpe[2 * r, 0] = k[0:256]
        state_kpe[2 * r + 1, 0] = k[256:512]
    return (y_prompt, y_sample, state_ckv, state_kpe)
```

```python
import os
import numpy as np
from contextlib import ExitStack
import concourse.bass as bass
import concourse.mybir as mybir
from concourse.bass_utils import run_bass_kernel_spmd

F32 = mybir.dt.float32
BF16 = mybir.dt.bfloat16
ALU = mybir.AluOpType
AF = mybir.ActivationFunctionType

NCORES = 8
D = 2048
DC = 16
FH = 5632
HC = 44
DEPTH = 4
SEQ = 256
DSEQ = 4096
PAST = 512
HALO = 32
NPR = 512
WIN = 2048 + 2 * HALO
T = NPR + WIN
CH = [(0, 512)] + [(512 + 424 * i, 424) for i in range(4)] + [(512 + 1696, 416)]
NCH = len(CH)
EPS = 1e-6
NPASS = 4
HPP = HC // NPASS
PW = T + 64
SEGS = [(0, 256, 16), (256, 256, 288), (512, WIN, 560)]
POOLW = (2, 4, 8, 16)
NK = 512 + PAST + DSEQ
NKT = NK // 128
CONVW = 31
PAIRS = [[0, 1], [2, 3], [4, 5], [6, 7]]
PERM = list(range(16, 32)) + list(range(0, 16)) + list(range(48, 64)) + list(range(32, 48))


class Eng:
    def __init__(self, name, h, sem, inc):
        self.name, self.h, self.sem, self.inc, self.count = name, h, sem, inc, 0


class Sched:
    def __init__(self, nc):
        self.nc = nc
        self.engs = {}
        self.regions = {}
        self.seen = {}
        self._sems = []

    def add_engine(self, name, h, inc=1):
        sem = self.nc.alloc_semaphore(name="s_" + name)
        e = Eng(name, h, sem, inc)
        self.engs[name] = e
        self.seen[name] = {}
        return e

    def _deps(self, reads, writes):
        deps = {}
        for k in reads:
            r = self.regions.get(k)
            if r and r["w"]:
                e, i = r["w"]
                deps[e] = max(deps.get(e, 0), i)
        for k in writes:
            r = self.regions.get(k)
            if r:
                if r["w"]:
                    e, i = r["w"]
                    deps[e] = max(deps.get(e, 0), i)
                for e, i in r["r"].items():
                    deps[e] = max(deps.get(e, 0), i)
        return deps

    def _wait(self, issuer, deps):
        sn = self.seen[issuer.name]
        for ename, idx in deps.items():
            if ename == "pe" and issuer.name == "pe":
                continue
            if sn.get(ename, 0) >= idx:
                continue
            e = self.engs[ename]
            issuer.h.wait_ge(e.sem, idx * e.inc)
            sn[ename] = idx

    def _commit(self, e, reads, writes):
        e.count += 1
        for k in writes:
            self.regions[k] = {"w": (e.name, e.count), "r": {}}
        for k in reads:
            r = self.regions.setdefault(k, {"w": None, "r": {}})
            r["r"][e.name] = e.count

    def op(self, ename, fn, reads=(), writes=()):
        e = self.engs[ename]
        self._wait(e, self._deps(reads, writes))
        ins = fn(e.h)
        ins.then_inc(e.sem, e.inc)
        self._commit(e, reads, writes)

    def dma(self, qname, slot, out, in_, reads=(), writes=(), **kw):
        q = self.engs[qname]
        s = self.engs[slot]
        self._wait(q, self._deps(reads, writes))
        q.h.dma_start(out=out, in_=in_, **kw).then_inc(s.sem, 16)
        self._commit(s, reads, writes)

    def coll(self, fn, reads=(), writes=()):
        q = self.engs["pool"]
        c = self.engs["cc"]
        self._wait(q, self._deps(reads, writes))
        fn(q.h).then_inc(c.sem, 1)
        self._commit(c, reads, writes)

    def barrier(self):
        tgt = {n: e.count for n, e in self.engs.items() if e.count > 0}
        for n in ("pe", "act", "dve", "pool", "sp"):
            self._wait(self.engs[n], dict(tgt))
        self.regions = {}


def _chunks_of(n, size=128):
    return [(s, min(size, n - s)) for s in range(0, n, size)]


def build_program(nlayers=DEPTH, do_mix=True, ncores=NCORES, do_ffn=True):
    PAIRS = [[2 * i, 2 * i + 1] for i in range(ncores // 2)]
    nc = bass.Bass("TRN2", target_bir_lowering=False)

    def din(name, shape, dt=F32):
        return nc.dram_tensor(name, list(shape), dt, kind="ExternalInput").ap()

    def dout(name, shape, dt=F32):
        return nc.dram_tensor(name, list(shape), dt, kind="ExternalOutput").ap()

    xin = din("xin", [T, D])
    cond2 = din("cond2", [2, D])
    ident_d = din("ident", [128, 128])
    vec_d = din("vecs", [NVEC_PAD, 128])
    if do_ffn:
        w_ada = din("w_ada", [DEPTH, D, 9 * D])
        ffn_w1 = din("ffn_w1", [DEPTH, 2, D, FH])
        ffn_w3 = din("ffn_w3", [DEPTH, 2, D, FH])
        ffn_w2 = din("ffn_w2", [DEPTH, 2, FH, D])
    yout = dout("yout", [NPR + 2048, D])
    sck_o = dout("sck", [NPR, 512])
    skp_o = dout("skp", [NPR, 64])
    maskw = din("maskw", [128, WIN])
    invcnt = din("invcnt", [4, 128, PW])
    cosq = din("cosq", [64, T])
    sinq = din("sinq", [64, T])
    cosk = din("cosk", [64, NK])
    sink = din("sink", [64, NK])
    cck = din("cck", [PAST, 512])
    ckp = din("ckp", [PAST, 64])
    gains_d = din("gains", [1, 384])
    pool_w = din("pool_w", [2, 4, 512, 512])
    w_dq = din("mla_w_dq", [D, 512])
    w_uq = din("mla_w_uq", [512, 3072])
    w_uqs = din("mla_w_uq_sw", [512, 1024])
    w_dkv = din("mla_w_dkv", [D, 576])
    w_ukv = din("mla_w_ukv", [512, 4096])
    w_o = din("mla_w_o", [D, D])
    conv_w1 = din("conv_w1", [D, 2 * D])
    conv_w2 = din("conv_w2", [D, D])
    bin_ = nc.dram_tensor("kv_bounce_in", [512, 2048], BF16).ap()
    bout = nc.dram_tensor("kv_bounce_out", [1024, 2048], BF16).ap()
    binp = nc.dram_tensor("kp_bounce_in", [64, 2048], BF16).ap()
    boutp = nc.dram_tensor("kp_bounce_out", [128, 2048], BF16).ap()

    XD = nc.dram_tensor("xd_scratch", [DC, 128, T], F32).ap()

    S = Sched(nc)
    S.add_engine("pe", nc.tensor)
    S.add_engine("act", nc.scalar)
    S.add_engine("dve", nc.vector)
    S.add_engine("pool", nc.gpsimd)
    S.add_engine("sp", nc.sync)
    for nm in ["ld0", "ld1", "ld2", "ld3", "wa0", "wa1", "wb0", "wb1", "wc0", "wc1",
               "xl0", "xl1", "xl2", "xl3", "xs0", "xs1", "xs2", "xs3", "misc", "out0", "out1",
               "tb0", "tb1", "tb2", "tb3", "gth", "ad0", "ad1", "ad2", "ad3"]:
        S.add_engine(nm, None, inc=16)
    S.add_engine("cc", None, inc=1)

    ARENA_F32 = 47 * 1024 + 512
    with ExitStack() as _es:
        arena = _es.enter_context(nc.sbuf_tensor("arena", [128, ARENA_F32], F32))
        ident = _es.enter_context(nc.sbuf_tensor("sb_ident", [128, 128], F32))
        VEC = _es.enter_context(nc.sbuf_tensor("sb_vec", [128, NVEC_PAD], F32))
        MOD = _es.enter_context(nc.sbuf_tensor("sb_mod", [128, DEPTH * 144 * 2], F32))
        GS = _es.enter_context(nc.sbuf_tensor("gs", [128, DEPTH * 3 * DC * 2], F32))
        HG = _es.enter_context(nc.sbuf_tensor("hg", [128, DEPTH * 3 * DC * 2], F32))
        ones_b = _es.enter_context(nc.sbuf_tensor("onesb", [128, 128], BF16))
        scond = _es.enter_context(nc.sbuf_tensor("scond", [128, DC, 2], BF16))
        identb = _es.enter_context(nc.sbuf_tensor("identb", [128, 128], BF16))
        PG = _es.enter_context(nc.sbuf_tensor("pg", [128, 2 * DC * 2], F32))
        G2B2 = _es.enter_context(nc.sbuf_tensor("g2b2", [128, DC * 2], F32))
        GQK = _es.enter_context(nc.sbuf_tensor("gqk", [128, 1], F32))
        NEGC = _es.enter_context(nc.sbuf_tensor("negc", [128, 1], F32))
        ones_ff = _es.enter_context(nc.sbuf_tensor("onesff", [128, 128], F32))
        ones_f = _es.enter_context(nc.sbuf_tensor("onesf", [1, 128], F32))
        GROW = _es.enter_context(nc.sbuf_tensor("grow", [1, 392], F32))
        ps0 = _es.enter_context(nc.psum_tensor("ps0", [128, 512], F32))
        ps1 = _es.enter_context(nc.psum_tensor("ps1", [128, 512], F32))
        ps2 = _es.enter_context(nc.psum_tensor("ps2", [128, 512], F32))
        ps3 = _es.enter_context(nc.psum_tensor("ps3", [128, 512], F32))
        ps4 = _es.enter_context(nc.psum_tensor("ps4", [128, 512], F32))
        ps5 = _es.enter_context(nc.psum_tensor("ps5", [128, 512], F32))
        ps6 = _es.enter_context(nc.psum_tensor("ps6", [128, 512], F32))
        ps7 = _es.enter_context(nc.psum_tensor("ps7", [128, 512], F32))
        PS = [ps0, ps1, ps2, ps3, ps4, ps5, ps6, ps7]

        def carve(off_bytes, shape, dt):
            esz = 2 if dt == BF16 else 4
            n = int(np.prod(shape[1:]))
            nf32 = (n * esz + 3) // 4
            base = arena[:, off_bytes // 4: off_bytes // 4 + nf32]
            v = base.bitcast(dt) if dt != F32 else base
            if len(shape) == 3:
                v = v.rearrange("p (a b) -> p a b", a=shape[1])
            return v, off_bytes + nf32 * 4

        mmT = nc.tensor

        def mm_group(ps_ap, pairs, reads, writes):
            def fn(h):
                ins = None
                n = len(pairs)
                for q, (l, r) in enumerate(pairs):
                    ins = h.matmul(ps_ap, l, r, start=(q == 0), stop=(q == n - 1))
                return ins
            S.op("pe", fn, reads=reads, writes=writes)

        def transpose_group(ps_ap_list, ins_list, reads, writes):
            def fn(h):
                ins = None
                for o, i_ in zip(ps_ap_list, ins_list):
                    ins = h.transpose(o, i_, ident[:i_.shape[0], :i_.shape[0]])
                return ins
            S.op("pe", fn, reads=reads, writes=writes)

        S.dma("sp", "misc", ident[:], ident_d[:, :], writes=["ident"])
        S.op("dve", lambda h: h.memset(ones_b[:], 1.0), writes=["ones"])

        o = 0
        vstage, o = carve(o, [128, 128], F32)
        for rb in range(NVEC_PAD // 128):
            S.dma("sp", "ld0", vstage, vec_d[rb * 128:(rb + 1) * 128, :], writes=["vstage"])
            transpose_group([PS[0][:, :128]], [vstage], reads=["vstage", "ident"], writes=["ps0"])
            S.op("dve", lambda h, rb=rb: h.tensor_copy(VEC[:, rb * 128:(rb + 1) * 128], PS[0][:, :128]),
                 reads=["ps0"], writes=["VEC"])

        cst, o2 = carve(o, [128, D], F32)
        S.dma("sp", "ld1", cst[0:2, :], cond2[:, :], writes=["cst"])
        S.op("act", lambda h: h.activation(cst[0:2, :], cst[0:2, :], AF.Silu), reads=["cst"], writes=["cst"])
        for k in range(DC):
            transpose_group([PS[1][:, 2 * k:2 * k + 2]], [cst[0:2, k * 128:(k + 1) * 128]],
                            reads=["cst", "ident"], writes=["ps1"])
        S.op("dve", lambda h: h.tensor_copy(scond[:].rearrange("p a b -> p (a b)"), PS[1][:, :2 * DC]),
             reads=["ps1"], writes=["scond"])

        MODv = MOD[:].rearrange("p (l c t) -> p l c t", l=DEPTH, c=144)
        GSv = GS[:].rearrange("p (l s c t) -> p l s c t", l=DEPTH, s=3, c=DC)
        HGv = HG[:].rearrange("p (l s c t) -> p l s c t", l=DEPTH, s=3, c=DC)
        PGv = PG[:].rearrange("p (j c t) -> p j c t", j=2, c=DC)
        G2B2v = G2B2[:].rearrange("p (c t) -> p c t", c=DC)
        ADA_OFF = 73728
        adab = [carve(ADA_OFF + q * 16384, [128, DC, 512], BF16)[0] for q in range(2)]
        arow = [carve(ADA_OFF + 32768 + q * 2048, [128, 512], F32)[0] for q in range(2)]

        def derive(li):
            for s_ in range(3):
                g = VEC[:, VOFF["norm_g"] + (li * 3 + s_) * DC: VOFF["norm_g"] + (li * 3 + s_ + 1) * DC]
                for t in range(2):
                    sc = MODv[:, li, (3 * s_ + 1) * DC:(3 * s_ + 2) * DC, t]
                    S.op("dve", lambda h, s_=s_, t=t, sc=sc, g=g: h.scalar_tensor_tensor(
                        GSv[:, li, s_, :, t], sc, 1.0, g, ALU.add, ALU.mult), reads=["MOD", "VEC"], writes=["GS"])
                    gt = MODv[:, li, (3 * s_ + 2) * DC:(3 * s_ + 3) * DC, t]
                    S.op("dve", lambda h, s_=s_, t=t, gt=gt: h.tensor_scalar(
                        HGv[:, li, s_, :, t], gt, 0.5 if s_ != 1 else 1.0, None, ALU.mult),
                        reads=["MOD"], writes=["HG"])
            for t in range(2):
                if li % 3 == 0:
                    j = li // 3
                    ps_ = VEC[:, VOFF["pool_scale"] + j * DC: VOFF["pool_scale"] + (j + 1) * DC]
                    S.op("dve", lambda h, j=j, t=t, ps_=ps_: h.tensor_tensor(
                        PGv[:, j, :, t], HGv[:, li, 1, :, t], ps_, ALU.mult), reads=["HG", "VEC"], writes=["PG"])
                if li == 2:
                    b2_ = VEC[:, VOFF["conv_b2"]: VOFF["conv_b2"] + DC]
                    S.op("dve", lambda h, t=t, b2_=b2_: h.tensor_tensor(
                        G2B2v[:, :, t], HGv[:, 2, 1, :, t], b2_, ALU.mult), reads=["HG", "VEC"], writes=["G2B2"])

        def ada_gen(li):
            wsrc = w_ada[li].rearrange("(k p) n -> p k n", p=128)

            def load(cb):
                S.dma("pool", "ad%d" % (cb % 2), adab[cb % 2], wsrc[:, :, cb * 512:(cb + 1) * 512],
                      writes=["adab%d" % (cb % 2)])
            load(0)
            for cb in range(36):
                if cb + 1 < 36:
                    load(cb + 1)
                q = cb % 2
                mm_group(PS[7][0:2, :512], [(scond[:, k, :], adab[q][:, k, :]) for k in range(DC)],
                         reads=["adab%d" % q, "scond"], writes=["ps7"])
                S.op("act", lambda h, q=q: h.activation(arow[q][0:2, :], PS[7][0:2, :512], AF.Identity),
                     reads=["ps7"], writes=["arow%d" % q])
                transpose_group([PS[6][:, 2 * (cb * 4 + cc):2 * (cb * 4 + cc) + 2] for cc in range(4)],
                                [arow[q][0:2, cc * 128:(cc + 1) * 128] for cc in range(4)],
                                reads=["arow%d" % q, "ident"], writes=["ps6"])
                yield
            bias = VEC[:, VOFF["b_ada"] + li * 144: VOFF["b_ada"] + (li + 1) * 144]
            for t in range(2):
                S.op("dve", lambda h, t=t, bias=bias: h.tensor_tensor(
                    MODv[:, li, :, t], PS[6][:, :288].rearrange("p (c t) -> p c t", t=2)[:, :, t], bias, ALU.add),
                    reads=["ps6", "VEC"], writes=["MOD"])
            derive(li)
            yield

        def bg_step(bg, k):
            if bg is None:
                return
            for _ in range(k):
                try:
                    next(bg)
                except StopIteration:
                    return

        def bg_drain(bg):
            bg_step(bg, 1000)

        S.op("dve", lambda h: h.tensor_copy(identb[:], ident[:]), reads=["ident"], writes=["identb"])
        S.op("dve", lambda h: h.memset(ones_f[:], 1.0), writes=["onesf"])
        S.op("dve", lambda h: h.memset(ones_ff[:], 1.0), writes=["onesff"])
        ACC_OFF = 184064
        ACC, _ = carve(ACC_OFF, [128, T], F32)

        def acc_update(xs_ap, n, cw_, first, sqt_ap, sqt_key):
            c0_ = CH[n][0]
            accv = ACC[:, c0_:c0_ + cw_]
            if first:
                S.op("pool", lambda h: h.tensor_tensor(accv, xs_ap, xs_ap, ALU.mult), writes=[("acc", n)])
            else:
                S.op("pool", lambda h: h.tensor_tensor(sqt_ap[:, :cw_], xs_ap, xs_ap, ALU.mult), writes=[sqt_key])
                S.op("pool", lambda h: h.tensor_tensor(accv, accv, sqt_ap[:, :cw_], ALU.add),
                     reads=[sqt_key], writes=[("acc", n)])
        if not do_ffn:
            S.op("dve", lambda h: h.memset(MOD[:], 0.5), writes=["MOD"])
            for li in range(nlayers):
                derive(li)
        S.op("dve", lambda h: h.tensor_tensor(GQK[:], VEC[:, VOFF["gains"]:VOFF["gains"] + 1],
                                              VEC[:, VOFF["gains"] + 1:VOFF["gains"] + 2], ALU.mult),
             reads=["VEC"], writes=["GQK"])
        S.dma("sp", "misc", GROW[0:1, 0:384], gains_d[:, :], writes=["GROW"])
        S.op("act", lambda h: h.activation(GROW[0:1, 0:384], GROW[0:1, 0:384], AF.Abs),
             reads=["GROW"], writes=["GROW"])
        S.op("dve", lambda h: h.tensor_scalar(GROW[0:1, 0:192], GROW[0:1, 0:192], 1.0, None, ALU.mult, ALU.max,
                                              accum_out=GROW[0:1, 384:385]), reads=["GROW"], writes=["GROW"])
        S.op("dve", lambda h: h.tensor_scalar(GROW[0:1, 192:384], GROW[0:1, 192:384], 1.0, None, ALU.mult, ALU.max,
                                              accum_out=GROW[0:1, 385:386]), reads=["GROW"], writes=["GROW"])
        S.op("dve", lambda h: h.tensor_scalar(GROW[0:1, 386:387], GROW[0:1, 384:385], GROW[0:1, 385:386],
                                              -float(np.sqrt(192.0)), ALU.mult, ALU.mult), reads=["GROW"], writes=["GROW"])
        mm_group(PS[6][:, 0:1], [(ones_f[0:1, :], GROW[0:1, 386:387])], reads=["GROW", "onesf"], writes=["ps6"])
        S.op("dve", lambda h: h.tensor_copy(NEGC[:], PS[6][:, 0:1]), reads=["ps6"], writes=["NEGC"])

        def shift_col(li, s, c, t):
            return MODv[:, li, (3 * s) * DC + c, t:t + 1]

        S.barrier()
        def _all_ada():
            for li_ in range(nlayers):
                yield from ada_gen(li_)
        bg0 = _all_ada() if do_ffn else None
        sqt_i = [carve(ADA_OFF + 32768 + 4096 + q * 2048, [128, 512], F32)[0] for q in range(2)]
        xt_bufs = [carve(s * 4 * D * 4, [128, 4, D], F32)[0] for s in range(2)]
        xs_off = 2 * 4 * D * 4
        xs_bufs = [carve(xs_off + s * 2048, [128, 512], F32)[0] for s in range(4)]
        nxs = 0

        def acc_update_i(xs, n, cw_, first, k):
            c0_ = CH[n][0]
            accv = ACC[:, c0_:c0_ + cw_]
            src = xs_bufs[xs][:, :cw_]
            if first:
                S.op("pool", lambda h: h.tensor_tensor(accv, src, src, ALU.mult), reads=["xs%d" % xs], writes=[("acc", n)])
            else:
                sq_ = sqt_i[k % 2]
                S.op("pool", lambda h: h.tensor_tensor(sq_[:, :cw_], src, src, ALU.mult),
                     reads=["xs%d" % xs], writes=["sqti%d" % (k % 2)])
                S.op("pool", lambda h: h.tensor_tensor(accv, accv, sq_[:, :cw_], ALU.add),
                     reads=["sqti%d" % (k % 2)], writes=[("acc", n)])
        for n, (c0, cw) in enumerate(CH):
            tl = _chunks_of(cw)
            s = n % 2
            for ti, (t0, tw) in enumerate(tl):
                S.dma("sp", "ld%d" % (2 * s + (ti % 2)), xt_bufs[s][:tw, ti, :], xin[c0 + t0:c0 + t0 + tw, :],
                      writes=[("xt", s, ti)])
            for i in range(DC):
                pb = 3 + (i % 2)
                transpose_group([PS[pb][:, t0:t0 + tw] for (t0, tw) in tl],
                                [xt_bufs[s][:tw, ti, i * 128:(i + 1) * 128] for ti, (t0, tw) in enumerate(tl)],
                                reads=[("xt", s, ti) for ti in range(len(tl))] + ["ident"], writes=["ps%d" % pb])
                xs = nxs % 4
                nxs += 1
                S.op("act" if i % 2 else "dve",
                     (lambda h, xs=xs, pb=pb, cw=cw: h.activation(xs_bufs[xs][:, :cw], PS[pb][:, :cw], AF.Identity))
                     if i % 2 else
                     (lambda h, xs=xs, pb=pb, cw=cw: h.tensor_copy(xs_bufs[xs][:, :cw], PS[pb][:, :cw])),
                     reads=["ps%d" % pb], writes=["xs%d" % xs])
                S.dma("sp", "xs%d" % xs, XD[i, :, c0:c0 + cw], xs_bufs[xs][:, :cw],
                      reads=["xs%d" % xs], writes=[("XD", i, n)])
                acc_update_i(xs, n, cw, i == 0, nxs)
                bg_step(bg0, 2)
        bg_drain(bg0)

        def xl_at(o_):
            return [carve(o_ + q * 2048, [128, 512], F32)[0] for q in range(4)]

        def norm_mod(li, s, Hb, xl):
            for n, (c0, cw) in enumerate(CH):
                mm_group(PS[5][:, :cw], [(ones_ff[:, :], ACC[:, c0:c0 + cw])], reads=[("acc", n), "onesff"], writes=["ps5"])
                S.op("act", lambda h, c0=c0, cw=cw: h.activation(ACC[:, c0:c0 + cw], PS[5][:, :cw], AF.Sqrt,
                                                                  bias=eps_t[:, 0:1], scale=1.0 / D),
                     reads=["ps5", "eps"], writes=[("acc", n)])
                S.op("dve", lambda h, c0=c0, cw=cw: h.reciprocal(ACC[:, c0:c0 + cw], ACC[:, c0:c0 + cw]),
                     reads=[("acc", n)], writes=[("acc", n)])
            nl = 0
            for n, (c0, cw) in enumerate(CH):
                t = 0 if n == 0 else 1
                for i in range(DC):
                    q = nl % 4
                    nl += 1
                    S.dma("sp", "xl%d" % q, xl[q][:, :cw], XD[i, :, c0:c0 + cw],
                          reads=[("XD", i, n)], writes=["fx%d" % q])
                    S.op("dve", lambda h, q=q, c0=c0, cw=cw: h.tensor_tensor(
                        xl[q][:, :cw], xl[q][:, :cw], ACC[:, c0:c0 + cw], ALU.mult),
                        reads=["fx%d" % q, ("acc", n)], writes=["fx%d" % q])
                    S.op("act", lambda h, q=q, c0=c0, cw=cw, i=i, t=t: h.activation(
                        Hb[:, i, c0:c0 + cw], xl[q][:, :cw], AF.Identity,
                        bias=shift_col(li, s, i, t), scale=GSv[:, li, s, i, t:t + 1]),
                        reads=["fx%d" % q, "MOD", "GS"], writes=[("H", i, n)])

        def _acc(xs_ap, n, cw_, first, sqt_ap, sqt_key, fxkey):
            c0_ = CH[n][0]
            accv = ACC[:, c0_:c0_ + cw_]
            if first:
                S.op("pool", lambda h: h.tensor_tensor(accv, xs_ap, xs_ap, ALU.mult), reads=[fxkey], writes=[("acc", n)])
            else:
                S.op("pool", lambda h: h.tensor_tensor(sqt_ap[:, :cw_], xs_ap, xs_ap, ALU.mult), reads=[fxkey], writes=[sqt_key])
                S.op("pool", lambda h: h.tensor_tensor(accv, accv, sqt_ap[:, :cw_], ALU.add),
                     reads=[sqt_key], writes=[("acc", n)])

        def ffn(li, f, s, bg=None):
            S.barrier()
            o = 0
            Hb, o = carve(o, [128, DC, T], BF16)
            Ub, o = carve(o, [128, HPP, T], BF16)
            w1b = []
            w3b = []
            w2b = []
            for q in range(2):
                v, o = carve(o, [128, DC, 128], BF16); w1b.append(v)
                v, o = carve(o, [128, DC, 128], BF16); w3b.append(v)
                v, o = carve(o, [128, HPP, 128], BF16); w2b.append(v)
            sab = []
            for q in range(2):
                v, o = carve(o, [128, 512], F32); sab.append(v)
            xsb = []
            for q in range(4):
                v, o = carve(o, [128, 512], F32); xsb.append(v)
            sqt, o = carve_list(o, 2, [128, 512], F32)
            assert o <= ACC_OFF, o
            norm_mod(li, s, Hb, xsb)
            w1s = ffn_w1[li, f].rearrange("(k p) n -> p k n", p=128)
            w3s = ffn_w3[li, f].rearrange("(k p) n -> p k n", p=128)
            w2s = ffn_w2[li, f].rearrange("(j p) n -> p j n", p=128)
            ng = 0
            nx = 0
            nw2 = 0
            for p in range(NPASS):
                for jj in range(HPP):
                    j = p * HPP + jj
                    q = j % 2
                    S.dma("pool", "wa%d" % q, w1b[q], w1s[:, :, j * 128:(j + 1) * 128], writes=["w1b%d" % q])
                    S.dma("pool", "wb%d" % q, w3b[q], w3s[:, :, j * 128:(j + 1) * 128], writes=["w3b%d" % q])
                    for n, (c0, cw) in enumerate(CH):
                        g = ng % 2
                        ng += 1
                        mm_group(PS[g][:, :cw], [(w1b[q][:, k, :], Hb[:, k, c0:c0 + cw]) for k in range(DC)],
                                 reads=["w1b%d" % q] + [("H", k, n) for k in range(DC)], writes=["ps%d" % g])
                        mm_group(PS[2 + g][:, :cw], [(w3b[q][:, k, :], Hb[:, k, c0:c0 + cw]) for k in range(DC)],
                                 reads=["w3b%d" % q] + [("H", k, n) for k in range(DC)], writes=["ps%d" % (2 + g)])
                        S.op("act", lambda h, g=g, cw=cw: h.activation(sab[g][:, :cw], PS[g][:, :cw], AF.Silu),
                             reads=["ps%d" % g], writes=["sa%d" % g])
                        S.op("dve", lambda h, g=g, jj=jj, c0=c0, cw=cw: h.tensor_tensor(
                            Ub[:, jj, c0:c0 + cw], sab[g][:, :cw], PS[2 + g][:, :cw], ALU.mult),
                            reads=["sa%d" % g, "ps%d" % (2 + g)], writes=[("U", jj, n)])
                    bg_step(bg, 4)
                for i in range(DC):
                    q = nw2 % 2
                    nw2 += 1
                    S.dma("pool", "wc%d" % q, w2b[q], w2s[:, p * HPP:(p + 1) * HPP, i * 128:(i + 1) * 128],
                          writes=["w2b%d" % q])
                    for n, (c0, cw) in enumerate(CH):
                        t = 0 if n == 0 else 1
                        g = 4 + (ng % 2)
                        ng += 1
                        xq = nx % 4
                        nx += 1
                        S.dma("sp", "xl%d" % xq, xsb[xq][:, :cw], XD[i, :, c0:c0 + cw],
                              reads=[("XD", i, n)], writes=["fx%d" % xq])
                        mm_group(PS[g][:, :cw], [(w2b[q][:, jj, :], Ub[:, jj, c0:c0 + cw]) for jj in range(HPP)],
                                 reads=["w2b%d" % q] + [("U", jj, n) for jj in range(HPP)], writes=["ps%d" % g])
                        S.op("dve", lambda h, g=g, xq=xq, cw=cw, i=i, t=t: h.scalar_tensor_tensor(
                            xsb[xq][:, :cw], PS[g][:, :cw], HGv[:, li, s, i, t:t + 1], xsb[xq][:, :cw],
                            ALU.mult, ALU.add), reads=["ps%d" % g, "fx%d" % xq, "HG"], writes=["fx%d" % xq])
                        if p == NPASS - 1:
                            _acc(xsb[xq][:, :cw], n, cw, i == 0, sqt[nx % 2], "sqt%d" % (nx % 2), "fx%d" % xq)
                        S.dma("act", "xs%d" % xq, XD[i, :, c0:c0 + cw], xsb[xq][:, :cw],
                              reads=["fx%d" % xq], writes=[("XD", i, n)])
            bg_drain(bg)

        def x_rmw(xsb, nx, i, n, pb, pcols, scal_ap, bias_ap=None, acc=None):
            c0, cw = CH[n]
            xq = nx % 4
            S.dma("sp", "xl%d" % xq, xsb[xq][:, :cw], XD[i, :, c0:c0 + cw],
                  reads=[("XD", i, n)], writes=["fx%d" % xq])
            S.op("dve", lambda h: h.scalar_tensor_tensor(
                xsb[xq][:, :cw], PS[pb][:, pcols:pcols + cw], scal_ap, xsb[xq][:, :cw], ALU.mult, ALU.add),
                reads=["ps%d" % pb, "fx%d" % xq, "HG", "PG"], writes=["fx%d" % xq])
            if bias_ap is not None:
                S.op("dve", lambda h: h.tensor_scalar(xsb[xq][:, :cw], xsb[xq][:, :cw], bias_ap, None, ALU.add),
                     reads=["fx%d" % xq, "G2B2"], writes=["fx%d" % xq])
            if acc is not None:
                _acc(xsb[xq][:, :cw], n, cw, acc[0], acc[1][nx % 2],
                     "sqt%d" % (0 if acc[1][0] is acc[1][1] else nx % 2), "fx%d" % xq)
            S.dma("act", "xs%d" % xq, XD[i, :, c0:c0 + cw], xsb[xq][:, :cw],
                  reads=["fx%d" % xq], writes=[("XD", i, n)])

        def carve_list(o, k, shape, dt):
            out = []
            for _ in range(k):
                v, o = carve(o, shape, dt)
                out.append(v)
            return out, o

        def pool_mixer(li, j):
            S.barrier()
            o = 0
            Hb, o = carve(o, [128, DC, T], BF16)
            norm_mod(li, 1, Hb, xl_at(o))
            S.barrier()
            hp, o = carve(o, [128, PW], F32)
            sA, o = carve(o, [128, PW], F32)
            sB, o = carve(o, [128, PW], F32)
            IC, o = carve(o, [128, PW], F32)
            MK, o = carve(o, [128, WIN], F32)
            DF, o = carve(o, [128, 4, T], BF16)
            wp, o = carve_list(o, 2, [128, 4, 512], BF16)
            xsb, o = carve_list(o, 4, [128, 512], F32)
            sqt, o = carve_list(o, 2, [128, 512], F32)
            assert o <= ACC_OFF, o
            S.dma("sp", "misc", MK, maskw[:, :], writes=["MK"])
            for nm_, b_ in (("hp", hp), ("sA", sA), ("sB", sB)):
                S.op("pool", lambda h, b_=b_: h.memset(b_, 0.0), writes=[nm_])
            nx = 0
            ng = 0
            for g in range(4):
                w = POOLW[g]
                S.dma("sp", "ld0", IC, invcnt[g], writes=["IC"])
                S.dma("pool", "wa%d" % (g % 2), wp[g % 2], pool_w[j, g].rearrange("(k p) n -> p k n", p=128),
                      writes=["wp%d" % (g % 2)])
                for kk in range(4):
                    i = g * 4 + kk
                    S.op("dve", lambda h, i=i: h.tensor_copy(hp[:, 16:272], Hb[:, i, 0:256]),
                         reads=[("H", i, 0)], writes=["hp"])
                    S.op("dve", lambda h, i=i: h.tensor_copy(hp[:, 288:544], Hb[:, i, 256:512]),
                         reads=[("H", i, 0)], writes=["hp"])
                    S.op("dve", lambda h, i=i: h.tensor_tensor(hp[:, 560:560 + WIN], Hb[:, i, 512:T], MK, ALU.mult),
                         reads=[("H", i, n) for n in range(1, NCH)] + ["MK"], writes=["hp"])
                    S.op("dve", lambda h: h.tensor_tensor(sA[:, 1:PW], hp[:, 0:PW - 1], hp[:, 1:PW], ALU.add),
                         reads=["hp"], writes=["sA"])
                    cur, curn, oth, othn = sA, "sA", sB, "sB"
                    for lvl, sh in ((4, 1), (8, 2), (16, 4)):
                        if w >= lvl:
                            S.op("dve", lambda h, cur=cur, oth=oth, sh=sh: h.tensor_tensor(
                                oth[:, sh:PW - sh], cur[:, 0:PW - 2 * sh], cur[:, 2 * sh:PW], ALU.add),
                                reads=[curn], writes=[othn])
                            cur, curn, oth, othn = oth, othn, cur, curn
                    S.op("dve", lambda h, cur=cur, oth=oth: h.tensor_tensor(oth[:, :], cur[:, :], IC[:, :], ALU.mult),
                         reads=[curn, "IC"], writes=[othn])
                    for (sc0, sw, pc0) in SEGS:
                        S.op("dve", lambda h, oth=oth, kk=kk, sc0=sc0, sw=sw, pc0=pc0: h.tensor_tensor(
                            DF[:, kk, sc0:sc0 + sw], oth[:, pc0:pc0 + sw], hp[:, pc0:pc0 + sw], ALU.subtract),
                            reads=[othn, "hp"], writes=[("DF", kk)])
                for oc in range(4):
                    io = g * 4 + oc
                    for n, (c0, cw) in enumerate(CH):
                        t = 0 if n == 0 else 1
                        pb = 4 + (ng % 2)
                        ng += 1
                        mm_group(PS[pb][:, :cw],
                                 [(wp[g % 2][:, kk, oc * 128:(oc + 1) * 128], DF[:, kk, c0:c0 + cw]) for kk in range(4)],
                                 reads=["wp%d" % (g % 2)] + [("DF", kk) for kk in range(4)], writes=["ps%d" % pb])
                        x_rmw(xsb, nx, io, n, pb, 0, PGv[:, j, io, t:t + 1], acc=(io == 0, sqt))
                        nx += 1

        def conv_mixer(li):
            S.barrier()
            A0, B0, C0 = 0, DC * T * 2, DC * T * 2 + DC * PW * 2
            Hb, _ = carve(A0, [128, DC, T], BF16)
            norm_mod(li, 1, Hb, xl_at(DC * T * 2))
            S.barrier()
            Gf, _ = carve(B0, [128, DC * PW], BF16)
            G = Gf.rearrange("p (a b) -> p a b", a=DC)
            o = C0
            w1a, o = carve(o, [128, DC, 128], BF16)
            w1b, o = carve(o, [128, DC, 128], BF16)
            sab, o = carve_list(o, 2, [128, 512], F32)
            MKb, o = carve(o, [128, WIN], BF16)
            assert o <= ARENA_F32 * 4, o
            S.dma("pool", "misc", MKb[:, 0:1056], maskw[:, 0:1056], writes=["MKb"])
            S.dma("pool", "misc", MKb[:, 1056:WIN], maskw[:, 1056:WIN], reads=["MKb"], writes=["MKb"])
            S.op("pool", lambda h: h.memset(Gf, 0.0), writes=["G"])
            w1s = conv_w1.rearrange("(k p) n -> p k n", p=128)
            b1o = VOFF["conv_b1"]
            ng = 0
            for oc in range(DC):
                S.dma("pool", "wa0", w1a, w1s[:, :, oc * 128:(oc + 1) * 128], writes=["cw1a"])
                S.dma("pool", "wb0", w1b, w1s[:, :, (DC + oc) * 128:(DC + oc + 1) * 128], writes=["cw1b"])
                for n, (c0, cw) in enumerate(CH):
                    g = ng % 2
                    ng += 1
                    mm_group(PS[g][:, :cw], [(w1a[:, k, :], Hb[:, k, c0:c0 + cw]) for k in range(DC)],
                             reads=["cw1a"] + [("H", k, n) for k in range(DC)], writes=["ps%d" % g])
                    mm_group(PS[2 + g][:, :cw], [(w1b[:, k, :], Hb[:, k, c0:c0 + cw]) for k in range(DC)],
                             reads=["cw1b"] + [("H", k, n) for k in range(DC)], writes=["ps%d" % (2 + g)])
                    S.op("act", lambda h, g=g, cw=cw, oc=oc: h.activation(
                        sab[g][:, :cw], PS[2 + g][:, :cw], AF.Sigmoid, bias=VEC[:, b1o + DC + oc:b1o + DC + oc + 1]),
                        reads=["ps%d" % (2 + g), "VEC"], writes=["sa%d" % g])
                    if n == 0:
                        for (sc0, sw, pc0) in SEGS[:2]:
                            S.op("dve", lambda h, g=g, oc=oc, sc0=sc0, sw=sw, pc0=pc0: h.scalar_tensor_tensor(
                                G[:, oc, pc0:pc0 + sw], PS[g][:, sc0:sc0 + sw], VEC[:, b1o + oc:b1o + oc + 1],
                                sab[g][:, sc0:sc0 + sw], ALU.add, ALU.mult),
                                reads=["ps%d" % g, "sa%d" % g, "VEC", "G"], writes=[("G", oc)])
                    else:
                        S.op("dve", lambda h, g=g, oc=oc, cw=cw: h.scalar_tensor_tensor(
                            sab[g][:, :cw], PS[g][:, :cw], VEC[:, b1o + oc:b1o + oc + 1],
                            sab[g][:, :cw], ALU.add, ALU.mult),
                            reads=["ps%d" % g, "sa%d" % g, "VEC"], writes=["sa%d" % g])
                        S.op("dve", lambda h, g=g, oc=oc, c0=c0, cw=cw: h.tensor_tensor(
                            G[:, oc, c0 + 48:c0 + 48 + cw], sab[g][:, :cw], MKb[:, c0 - NPR:c0 - NPR + cw], ALU.mult),
                            reads=["sa%d" % g, "MKb", "G"], writes=[("G", oc)])
            S.barrier()
            CB, _ = carve(A0, [128, DC, T], BF16)
            DG, _ = carve_list(C0, 2, [128, CONVW, 128], BF16)
            dwo = VOFF["conv_dw"]
            for c in range(DC):
                q = c % 2
                for k in range(CONVW):
                    S.op("dve", lambda h, q=q, k=k, c=c: h.tensor_scalar(
                        DG[q][:, k, :], identb[:], VEC[:, dwo + k * DC + c:dwo + k * DC + c + 1], None, ALU.mult),
                        reads=["identb", "VEC"], writes=[("DG", q, k)])
                for n, (c0, cw) in enumerate(CH):
                    pb = 4 + (ng % 2)
                    ng += 1
                    segs = SEGS[:2] if n == 0 else [(c0, cw, c0 + 48)]
                    for (sc0, sw, pc0) in segs:
                        mm_group(PS[pb][:, sc0 - c0:sc0 - c0 + sw],
                                 [(DG[q][:, k, :], G[:, c, pc0 + k - 15:pc0 + k - 15 + sw]) for k in range(CONVW)],
                                 reads=[("DG", q, k) for k in range(CONVW)], writes=["ps%d" % pb])
                    S.op("act", lambda h, pb=pb, c=c, c0=c0, cw=cw: h.activation(
                        CB[:, c, c0:c0 + cw], PS[pb][:, :cw], AF.Identity,
                        bias=VEC[:, VOFF["conv_dw_b"] + c:VOFF["conv_dw_b"] + c + 1]),
                        reads=["ps%d" % pb, "VEC"], writes=[("C", c, n)])
            S.barrier()
            o = C0
            MU, o = carve(o, [128, T], F32)
            RS, o = carve(o, [128, T], F32)
            sq1, o = carve(o, [128, 512], BF16)
            tmp, o = carve(o, [128, 512], F32)
            assert o <= ARENA_F32 * 4, o
            for n, (c0, cw) in enumerate(CH):
                mm_group(PS[6][:, :cw], [(ones_b[:], CB[:, c, c0:c0 + cw]) for c in range(DC)],
                         reads=["ones"], writes=["ps6"])
                for c in range(DC):
                    S.op("act", lambda h, c=c, c0=c0, cw=cw: h.activation(sq1[:, :cw], CB[:, c, c0:c0 + cw], AF.Square),
                         writes=["sq1"])
                    def fn(h, c=c, cw=cw):
                        return h.matmul(PS[7][:, :cw], ones_b[:], sq1[:, :cw], start=(c == 0), stop=(c == DC - 1))
                    S.op("pe", fn, reads=["sq1"], writes=["ps7"])
                S.op("act", lambda h, c0=c0, cw=cw: h.activation(MU[:, c0:c0 + cw], PS[6][:, :cw], AF.Identity, scale=1.0 / D),
                     reads=["ps6"], writes=[("MU", n)])
                S.op("dve", lambda h, c0=c0, cw=cw: h.tensor_tensor(tmp[:, :cw], MU[:, c0:c0 + cw], MU[:, c0:c0 + cw], ALU.mult),
                     reads=[("MU", n)], writes=["tmp"])
                S.op("dve", lambda h, cw=cw: h.scalar_tensor_tensor(tmp[:, :cw], PS[7][:, :cw], 1.0 / D, tmp[:, :cw],
                                                                    ALU.mult, ALU.subtract),
                     reads=["ps7", "tmp"], writes=["tmp"])
                S.op("act", lambda h, c0=c0, cw=cw: h.activation(RS[:, c0:c0 + cw], tmp[:, :cw], AF.Sqrt, bias=eps_t[:, 0:1]),
                     reads=["tmp", "eps"], writes=[("RS", n)])
                S.op("dve", lambda h, c0=c0, cw=cw: h.reciprocal(RS[:, c0:c0 + cw], RS[:, c0:c0 + cw]),
                     reads=[("RS", n)], writes=[("RS", n)])
            ZB, _ = carve(B0, [128, DC, T], BF16)
            lg, lb = VOFF["conv_ln_g"], VOFF["conv_ln_b"]
            for n, (c0, cw) in enumerate(CH):
                for c in range(DC):
                    S.op("dve", lambda h, c=c, c0=c0, cw=cw: h.tensor_tensor(
                        tmp[:, :cw], CB[:, c, c0:c0 + cw], MU[:, c0:c0 + cw], ALU.subtract),
                        reads=[("MU", n)], writes=["tmp"])
                    S.op("dve", lambda h, c0=c0, cw=cw: h.tensor_tensor(tmp[:, :cw], tmp[:, :cw], RS[:, c0:c0 + cw], ALU.mult),
                         reads=["tmp", ("RS", n)], writes=["tmp"])
                    S.op("act", lambda h, c=c, c0=c0, cw=cw: h.activation(
                        ZB[:, c, c0:c0 + cw], tmp[:, :cw], AF.Silu, bias=VEC[:, lb + c:lb + c + 1],
                        scale=VEC[:, lg + c:lg + c + 1]), reads=["tmp", "VEC"], writes=[("Z", c, n)])
            S.barrier()
            o = C0
            wz, o = carve_list(o, 1, [128, DC, 128], BF16)
            wz = wz * 2
            xsb, o = carve_list(o, 4, [128, 512], F32)
            assert o <= ACC_OFF, o
            sq1c, _ = carve(B0 + DC * T * 2, [128, 512], F32)
            w2s = conv_w2.rearrange("(k p) n -> p k n", p=128)
            nx = 0
            for oc in range(DC):
                q = oc % 2
                S.dma("pool", "wa0", wz[0], w2s[:, :, oc * 128:(oc + 1) * 128], writes=["wz0"])
                q = 0
                for n, (c0, cw) in enumerate(CH):
                    t = 0 if n == 0 else 1
                    pb = 4 + (ng % 2)
                    ng += 1
                    mm_group(PS[pb][:, :cw], [(wz[q][:, k, :], ZB[:, k, c0:c0 + cw]) for k in range(DC)],
                             reads=["wz%d" % q], writes=["ps%d" % pb])
                    x_rmw(xsb, nx, oc, n, pb, 0, HGv[:, li, 1, oc, t:t + 1], G2B2v[:, oc, t:t + 1],
                          acc=(oc == 0, [sq1c, sq1c]))
                    nx += 1

        def mla_mixer(li):
            _stop = float(os.environ.get('KSTOP', '99'))
            S.barrier()
            Hb, _ = carve(0, [128, DC, T], BF16)
            norm_mod(li, 1, Hb, xl_at(DC * T * 2))
            S.barrier()
            R_CQR, R_KVR, R_KPR = 83968, 125952, 167936
            CQR, _ = carve(R_CQR, [128, 4, T], F32)
            KVR, _ = carve(R_KVR, [128, 4, T], F32)
            KPR, o = carve(R_KPR, [128, T], F32)
            wd, o = carve_list(o, 2, [128, DC, 128], BF16)
            assert o <= ARENA_F32 * 4, o
            ng = 0
            jobs = [(w_dq, oc * 128, 128, CQR[:, oc, :]) for oc in range(4)] + \
                   [(w_dkv, oc * 128, 128, KVR[:, oc, :]) for oc in range(4)] + [(w_dkv, 512, 64, KPR)]
            for ji, (wsrc, col0, m, dst) in enumerate(jobs):
                q = ji % 2
                S.dma("pool", "wa%d" % q, wd[q][:, :, :m], wsrc.rearrange("(k p) n -> p k n", p=128)[:, :, col0:col0 + m],
                      writes=["wd%d" % q])
                for n, (c0, cw) in enumerate(CH):
                    pb = ng % 2
                    ng += 1
                    mm_group(PS[pb][:m, :cw], [(wd[q][:, k, :m], Hb[:, k, c0:c0 + cw]) for k in range(DC)],
                             reads=["wd%d" % q] + [("H", k, n) for k in range(DC)], writes=["ps%d" % pb])
                    S.op("act" if pb else "dve",
                         (lambda h, pb=pb, m=m, dst=dst, c0=c0, cw=cw: h.activation(dst[:m, c0:c0 + cw], PS[pb][:m, :cw], AF.Identity))
                         if pb else
                         (lambda h, pb=pb, m=m, dst=dst, c0=c0, cw=cw: h.tensor_copy(dst[:m, c0:c0 + cw], PS[pb][:m, :cw])),
                         reads=["ps%d" % pb], writes=[("raw", ji, n)])
            if _stop <= 1:
                return
            S.barrier()
            CQ, o = carve(0, [128, 4, T], BF16)
            CKV, o = carve(o, [128, 4, NK], BF16)
            KPF, o = carve(o, [128, NK], BF16)
            KPSW, o = carve(o, [128, NK], BF16)
            assert o <= 83968
            o = R_KPR + T * 4
            rt, o = carve(o, [128, 512], F32)
            sq1, o = carve(o, [128, 512], BF16)
            for (SRC, which) in ((CQR, 0), (KVR, 1)):
                gno = VOFF["q_norm"] if which == 0 else VOFF["kv_norm"]
                for n, (c0, cw) in enumerate(CH):
                    for c in range(4):
                        S.op("act", lambda h, c=c, c0=c0, cw=cw, SRC=SRC: h.activation(sq1[:, :cw], SRC[:, c, c0:c0 + cw], AF.Square),
                             writes=["sq1"])
                        def fn(h, c=c, cw=cw):
                            return h.matmul(PS[5][:, :cw], ones_b[:], sq1[:, :cw], start=(c == 0), stop=(c == 3))
                        S.op("pe", fn, reads=["sq1"], writes=["ps5"])
                    S.op("act", lambda h, cw=cw: h.activation(rt[:, :cw], PS[5][:, :cw], AF.Sqrt, bias=eps_t[:, 0:1], scale=1.0 / 512),
                         reads=["ps5"], writes=["rt"])
                    S.op("dve", lambda h, cw=cw: h.reciprocal(rt[:, :cw], rt[:, :cw]), reads=["rt"], writes=["rt"])
                    for c in range(4):
                        S.op("dve", lambda h, c=c, c0=c0, cw=cw, SRC=SRC: h.tensor_tensor(
                            SRC[:, c, c0:c0 + cw], SRC[:, c, c0:c0 + cw], rt[:, :cw], ALU.mult),
                            reads=["rt"], writes=[("nrm", which, c, n)])
                        dst = CQ if which == 0 else KVR
                        S.op("act", lambda h, c=c, c0=c0, cw=cw, SRC=SRC, dst=dst, gno=gno: h.activation(
                            dst[:, c, c0:c0 + cw], SRC[:, c, c0:c0 + cw], AF.Identity, scale=VEC[:, gno + c:gno + c + 1]),
                            reads=[("nrm", which, c, n)], writes=[("nrm2", which, c, n)])
            if _stop <= 2:
                return
            S.barrier()
            o = R_CQR
            CKS, o = carve(o, [128, 4, 2048], BF16)
            KPS, o = carve(o, [128, 2048], BF16)
            ct, o = carve_list(o, 4, [128, 576], F32)
            so, o = carve_list(o, 2, [128, 512], F32)
            sk, o = carve_list(o, 2, [128, 64], F32)
            assert o <= R_KVR, o
            OWN0 = NPR + HALO
            for c in range(4):
                S.op("dve", lambda h, c=c: h.tensor_copy(CKV[:, c, 0:NPR], KVR[:, c, 0:NPR]), writes=[("CKV", c)])
                S.op("act", lambda h, c=c: h.activation(CKS[:, c, :], KVR[:, c, OWN0:OWN0 + 2048], AF.Identity), writes=[("CKS", c)])
            S.op("dve", lambda h: h.tensor_copy(KPF[0:64, 0:NPR], KPR[0:64, 0:NPR]), writes=["KPF"])
            S.op("act", lambda h: h.activation(KPS[0:64, :], KPR[0:64, OWN0:OWN0 + 2048], AF.Identity), writes=["KPS"])
            S.op("pool", lambda h: h.memset(KPSW[:, :], 0.0), writes=["KPSW"])
            if _stop <= 2.1:
                return
            for c in range(4):
                S.dma("sp", "gth", bin_[c * 128:(c + 1) * 128, :], CKS[:, c, :], reads=[("CKS", c)], writes=["bin"])
            S.dma("sp", "gth", binp[:, :], KPS[0:64, :], reads=["KPS"], writes=["binp"])
            if _stop <= 2.2:
                return
            S.coll(lambda h: h.collective_compute("AllGather", ALU.bypass, replica_groups=PAIRS,
                                                  ins=[bin_[:, :]], outs=[bout[:, :]]),
                   reads=["bin"], writes=["bout"])
            S.coll(lambda h: h.collective_compute("AllGather", ALU.bypass, replica_groups=PAIRS,
                                                  ins=[binp[:, :]], outs=[boutp[:, :]]),
                   reads=["binp"], writes=["boutp"])
            if _stop <= 2.3:
                return
            for r in range(2):
                for c in range(4):
                    S.dma("sp", "tb%d" % c, CKV[:, c, 1024 + r * 2048:1024 + (r + 1) * 2048],
                          bout[r * 512 + c * 128:r * 512 + (c + 1) * 128, :], reads=["bout"], writes=[("CKVg", r, c)])
                S.dma("sp", "ld0", KPF[0:64, 1024 + r * 2048:1024 + (r + 1) * 2048],
                      boutp[r * 64:(r + 1) * 64, :], reads=["boutp", "KPF"], writes=[("KPFg", r)])
                for blk in range(4):
                    src0 = r * 64 + PERM[blk * 16]
                    S.dma("sp", "ld1", KPSW[blk * 16:(blk + 1) * 16, 1024 + r * 2048:1024 + (r + 1) * 2048],
                          boutp[src0:src0 + 16, :], reads=["boutp", "KPSW"], writes=[("KPSWg", r, blk)])
            if _stop <= 3:
                return
            for ti in range(4):
                q = ti % 2
                transpose_group([PS[q][:, c * 128:(c + 1) * 128] for c in range(4)],
                                [KVR[:, c, ti * 128:(ti + 1) * 128] for c in range(4)], reads=["ident"], writes=["ps%d" % q])
                S.op("dve", lambda h, q=q: h.tensor_copy(so[q][:, :], PS[q][:, :]), reads=["ps%d" % q], writes=["so%d" % q])
                S.dma("sp", "out%d" % q, sck_o[ti * 128:(ti + 1) * 128, :], so[q][:, :], reads=["so%d" % q], writes=[("sck", ti)])
                transpose_group([PS[2 + q][:, 0:64]], [KPR[0:64, ti * 128:(ti + 1) * 128]], reads=["ident"], writes=["ps%d" % (2 + q)])
                S.op("act", lambda h, q=q: h.activation(sk[q][:, :], PS[2 + q][:, 0:64], AF.Identity),
                     reads=["ps%d" % (2 + q)], writes=["sk%d" % q])
                S.dma("sp", "xs%d" % q, skp_o[ti * 128:(ti + 1) * 128, :], sk[q][:, :], reads=["sk%d" % q], writes=[("skp", ti)])
            if _stop <= 4:
                return
            for ti in range(4):
                S.dma("sp", "ld2", ct[ti][:, 0:512], cck[ti * 128:(ti + 1) * 128, :], writes=[("ct", ti)])
                S.dma("sp", "ld3", ct[ti][:, 512:576], ckp[ti * 128:(ti + 1) * 128, :], writes=[("ctp", ti)])
            for c in range(4):
                pb = 4 + (c % 2)
                transpose_group([PS[pb][:, ti * 128:(ti + 1) * 128] for ti in range(4)],
                                [ct[ti][:, c * 128:(c + 1) * 128] for ti in range(4)],
                                reads=[("ct", ti) for ti in range(4)] + ["ident"], writes=["ps%d" % pb])
                S.op("dve", lambda h, c=c, pb=pb: h.tensor_copy(CKV[:, c, 512:1024], PS[pb][:, :]),
                     reads=["ps%d" % pb], writes=[("CKVc", c)])
            transpose_group([PS[6][0:64, ti * 128:(ti + 1) * 128] for ti in range(4)],
                            [ct[ti][:, 512:576] for ti in range(4)],
                            reads=[("ctp", ti) for ti in range(4)] + ["ident"], writes=["ps6"])
            S.op("dve", lambda h: h.tensor_copy(KPF[0:64, 512:1024], PS[6][0:64, :]), reads=["ps6", "KPF"], writes=["KPFc"])
            if _stop <= 5:
                return
            S.barrier()
            KPG, o = carve(R_KVR, [128, NK], BF16)
            SQPE, o = carve(o, [128, NK], BF16)
            o = R_CQR
            tk, o = carve_list(o, 4, [128, 512], F32)
            ta, o = carve(o, [128, 512], F32)
            tb, o = carve(o, [128, 512], F32)
            go = VOFF["gains"]
            S.op("pool", lambda h: h.memset(KPG[64:128, :], 0.0), writes=["KPGz"])
            S.op("pool", lambda h: h.memset(SQPE[64:128, :], 0.0), writes=["SQPEz"])
            for kc in range(NK // 512):
                ks = slice(kc * 512, (kc + 1) * 512)
                q = kc % 2
                S.dma("sp", "tb%d" % (2 * q), tk[2 * q][0:64, :], cosk[:, ks], writes=[("tk", 2 * q)])
                S.dma("sp", "tb%d" % (2 * q + 1), tk[2 * q + 1][0:64, :], sink[:, ks], writes=[("tk", 2 * q + 1)])
                S.op("dve", lambda h, ks=ks, q=q: h.scalar_tensor_tensor(
                    ta[0:64, :], KPF[0:64, ks], VEC[0:64, go + 4:go + 5], tk[2 * q][0:64, :], ALU.mult, ALU.mult),
                    reads=[("tk", 2 * q)], writes=["ta"])
                S.op("dve", lambda h, ks=ks, q=q: h.scalar_tensor_tensor(
                    tb[0:64, :], KPSW[0:64, ks], VEC[0:64, go + 5:go + 6], tk[2 * q + 1][0:64, :], ALU.mult, ALU.mult),
                    reads=[("tk", 2 * q + 1)], writes=["tb"])
                S.op("dve", lambda h, ks=ks: h.tensor_tensor(KPG[0:64, ks], ta[0:64, :], tb[0:64, :], ALU.add),
                     reads=["ta", "tb"], writes=[("KPG", kc)])
                S.op("act", lambda h, ks=ks: h.activation(SQPE[0:64, ks], KPF[0:64, ks], AF.Square), writes=[("SQPE", kc)])
            if _stop <= 6:
                return
            S.barrier()
            o = 61952
            KN, o = carve(o, [128, NK], BF16)
            Vt, o = carve(o, [128, NKT, 128], BF16)
            assert o <= 83968
            o = R_CQR
            OT, o = carve(o, [128, 4, T], BF16)
            SQN, o = carve(o, [128, NK], BF16)
            QN, o = carve(o, [128, T], BF16)
            QR, o = carve(o, [128, T], BF16)
            assert o <= R_KVR, o
            o = R_KVR + 2 * NK * 2
            PT, o = carve_list(o, 4, [128, 512], BF16)
            wkv, o = carve(o, [128, 4, 256], BF16)
            wq, o = carve(o, [128, 4, 192], BF16)
            wqs, o = carve(o, [128, 4, 64], BF16)
            wo, o = carve_list(o, 2, [128, 4, 128], BF16)
            RQ, o = carve(o, [128, 512], F32)
            ta, o = carve(o, [128, 512], F32)
            tb, o = carve(o, [128, 512], F32)
            tq, o = carve_list(o, 2, [128, 512], F32)
            sqa, o = carve(o, [128, 512], BF16)
            sqb, o = carve(o, [128, 512], BF16)
            RK, o = carve(o, [128, NKT], F32)
            rden, o = carve(o, [128, 512], F32)
            xsb, o = carve_list(o, 4, [128, 512], F32)
            sqt, o = carve_list(o, 2, [128, 512], F32)
            assert o <= ACC_OFF, o
            S.op("pool", lambda h: h.memset(QR[64:128, :], 0.0), writes=["QRz"])
            S.op("pool", lambda h: h.memset(sqb[64:128, :], 0.0), writes=["sqbz"])
            ukv_s = w_ukv.rearrange("(k p) n -> p k n", p=128)
            uq_s = w_uq.rearrange("(k p) n -> p k n", p=128)
            uqs_s = w_uqs.rearrange("(k p) n -> p k n", p=128)
            att_jobs = [((0, 256), (0, 2)), ((256, 256), (2, 4))] + [(CH[n], (4, NKT)) for n in range(1, NCH)]
            nx = 0
            nwo = 0
            inv192 = 1.0 / 192.0
            for hd in range(16):
                hh = hd % 4
                S.dma("pool", "wa0", wkv, ukv_s[:, :, hd * 256:(hd + 1) * 256], writes=["wkv"])
                S.dma("pool", "wb0", wq, uq_s[:, :, hd * 192:(hd + 1) * 192], writes=["wq"])
                S.dma("pool", "wc0", wqs, uqs_s[:, :, hd * 64:(hd + 1) * 64], writes=["wqs"])
                for kc in range(NK // 512):
                    ks = slice(kc * 512, (kc + 1) * 512)
                    mm_group(PS[4][:, :], [(wkv[:, k, 0:128], CKV[:, k, ks]) for k in range(4)], reads=["wkv"], writes=["ps4"])
                    S.op("act", lambda h, ks=ks: h.activation(KN[:, ks], PS[4][:, :], AF.Identity), reads=["ps4"], writes=[("KN", kc)])
                    S.op("act", lambda h, ks=ks: h.activation(SQN[:, ks], PS[4][:, :], AF.Square), reads=["ps4"], writes=[("SQN", kc)])
                for kg in range(NKT // 4):
                    for t4 in range(4):
                        kt = kg * 4 + t4
                        mm_group(PS[5][:, t4 * 128:(t4 + 1) * 128],
                                 [(CKV[:, k, kt * 128:(kt + 1) * 128], wkv[:, k, 128:256]) for k in range(4)],
                                 reads=["wkv"], writes=["ps5"])
                    S.op("dve", lambda h, kg=kg: h.tensor_copy(
                        Vt[:, kg * 4:(kg + 1) * 4, :].rearrange("p a b -> p (a b)"), PS[5][:, :]),
                        reads=["ps5"], writes=[("V", kg)])
                for kt in range(NKT):
                    kc = kt // 4
                    mm_group(PS[7][:, kt:kt + 1],
                             [(SQN[:, kt * 128:(kt + 1) * 128], ones_b[:, 0:1]),
                              (SQPE[:, kt * 128:(kt + 1) * 128], ones_b[:, 0:1])],
                             reads=[("SQN", kc)], writes=["ps7"])
                S.op("act", lambda h: h.activation(RK[:, :], PS[7][:, 0:NKT], AF.Sqrt, bias=eps_t[:, 0:1], scale=inv192),
                     reads=["ps7"], writes=["RK"])
                S.op("dve", lambda h: h.reciprocal(RK[:, :], RK[:, :]), reads=["RK"], writes=["RK"])
                S.op("dve", lambda h: h.tensor_scalar(RK[:, :], RK[:, :], float(192.0 ** -0.5), None, ALU.mult),
                     reads=["RK"], writes=["RK"])
                if _stop <= 7:
                    return
                for n, (c0, cw) in enumerate(CH):
                    S.dma("sp", "tb0", tq[0][0:64, :cw], cosq[:, c0:c0 + cw], writes=["tq0"])
                    S.dma("sp", "tb1", tq[1][0:64, :cw], sinq[:, c0:c0 + cw], writes=["tq1"])
                    mm_group(PS[4][:, :cw], [(wq[:, k, 0:128], CQ[:, k, c0:c0 + cw]) for k in range(4)], reads=["wq"], writes=["ps4"])
                    mm_group(PS[5][0:64, :cw], [(wq[:, k, 128:192], CQ[:, k, c0:c0 + cw]) for k in range(4)], reads=["wq"], writes=["ps5"])
                    mm_group(PS[6][0:64, :cw], [(wqs[:, k, :], CQ[:, k, c0:c0 + cw]) for k in range(4)], reads=["wqs"], writes=["ps6"])
                    S.op("act", lambda h, cw=cw: h.activation(sqa[:, :cw], PS[4][:, :cw], AF.Square), reads=["ps4"], writes=["sqa"])
                    S.op("act", lambda h, cw=cw: h.activation(sqb[0:64, :cw], PS[5][0:64, :cw], AF.Square), reads=["ps5"], writes=["sqb"])
                    mm_group(PS[7][:, :cw], [(ones_b[:, :], sqa[:, :cw]), (ones_b[:, :], sqb[:, :cw])],
                             reads=["sqa", "sqb", "sqbz"], writes=["ps7"])
                    S.op("act", lambda h, cw=cw: h.activation(RQ[:, :cw], PS[7][:, :cw], AF.Sqrt, bias=eps_t[:, 0:1], scale=inv192),
                         reads=["ps7"], writes=["RQ"])
                    S.op("dve", lambda h, cw=cw: h.reciprocal(RQ[:, :cw], RQ[:, :cw]), reads=["RQ"], writes=["RQ"])
                    S.op("dve", lambda h, c0=c0, cw=cw: h.scalar_tensor_tensor(
                        QN[:, c0:c0 + cw], PS[4][:, :cw], GQK[:, 0:1], RQ[:, :cw], ALU.mult, ALU.mult),
                        reads=["ps4", "RQ"], writes=[("QN", n)])
                    S.op("dve", lambda h, cw=cw: h.scalar_tensor_tensor(
                        ta[0:64, :cw], PS[5][0:64, :cw], VEC[0:64, go + 2:go + 3], RQ[0:64, :cw], ALU.mult, ALU.mult),
                        reads=["ps5", "RQ"], writes=["ta"])
                    S.op("dve", lambda h, cw=cw: h.scalar_tensor_tensor(
                        tb[0:64, :cw], PS[6][0:64, :cw], VEC[0:64, go + 3:go + 4], RQ[0:64, :cw], ALU.mult, ALU.mult),
                        reads=["ps6", "RQ"], writes=["tb"])
                    S.op("dve", lambda h, cw=cw: h.tensor_tensor(ta[0:64, :cw], ta[0:64, :cw], tq[0][0:64, :cw], ALU.mult),
                         reads=["ta", "tq0"], writes=["ta"])
                    S.op("dve", lambda h, cw=cw: h.tensor_tensor(tb[0:64, :cw], tb[0:64, :cw], tq[1][0:64, :cw], ALU.mult),
                         reads=["tb", "tq1"], writes=["tb"])
                    S.op("dve", lambda h, c0=c0, cw=cw: h.tensor_tensor(QR[0:64, c0:c0 + cw], ta[0:64, :cw], tb[0:64, :cw], ALU.add),
                         reads=["ta", "tb"], writes=[("QR", n)])
                if _stop <= 8:
                    return
                for (q0, qw), (k0, k1) in att_jobs:
                    qn_ = [n for n, (a, w_) in enumerate(CH) if a < q0 + qw and q0 < a + w_]
                    qreads = [("QN", n) for n in qn_] + [("QR", n) for n in qn_]

                    SB = (0, 1, 4, 5)
                    LA = 3

                    def score(kt, slot):
                        mm_group(PS[SB[slot]][:, :qw],
                                 [(KN[:, kt * 128:(kt + 1) * 128], QN[:, q0:q0 + qw]),
                                  (KPG[:, kt * 128:(kt + 1) * 128], QR[:, q0:q0 + qw])],
                                 reads=[("KN", kt // 4), "QRz"] + qreads, writes=["ps%d" % SB[slot]])
                    for a_ in range(min(LA, k1 - k0)):
                        score(k0 + a_, a_ % 4)
                    for kt in range(k0, k1):
                        sl = (kt - k0) % 4
                        if kt + LA < k1:
                            score(kt + LA, (kt - k0 + LA) % 4)
                        S.op("act", lambda h, kt=kt, sl=sl, qw=qw: h.activation(
                            PT[sl][:, :qw], PS[SB[sl]][:, :qw], AF.Exp, bias=NEGC[:, 0:1], scale=RK[:, kt:kt + 1]),
                            reads=["ps%d" % SB[sl], "RK"], writes=["pt%d" % sl])
                        def fpv(h, kt=kt, sl=sl, qw=qw):
                            return h.matmul(PS[2][:, :qw], Vt[:, kt, :], PT[sl][:, :qw], start=(kt == k0), stop=(kt == k1 - 1))
                        S.op("pe", fpv, reads=["pt%d" % sl, ("V", kt // 4)], writes=["ps2"])
                        def fdn(h, kt=kt, sl=sl, qw=qw):
                            return h.matmul(PS[3][:, :qw], ones_b[:, :], PT[sl][:, :qw], start=(kt == k0), stop=(kt == k1 - 1))
                        S.op("pe", fdn, reads=["pt%d" % sl], writes=["ps3"])
                    S.op("dve", lambda h, qw=qw: h.reciprocal(rden[:, :qw], PS[3][:, :qw]), reads=["ps3"], writes=["rden"])
                    S.op("dve", lambda h, q0=q0, qw=qw, hh=hh: h.tensor_tensor(
                        OT[:, hh, q0:q0 + qw], PS[2][:, :qw], rden[:, :qw], ALU.mult),
                        reads=["ps2", "rden"], writes=[("OT", hh, q0)])
                if _stop <= 9:
                    return
                if hh == 3:
                    g4 = hd // 4
                    wos = w_o[g4 * 512:(g4 + 1) * 512, :].rearrange("(k p) n -> p k n", p=128)
                    for oc in range(DC):
                        q = nwo % 2
                        nwo += 1
                        S.dma("pool", "wc1" if q else "wb1", wo[q], wos[:, :, oc * 128:(oc + 1) * 128], writes=["wo%d" % q])
                        for n, (c0, cw) in enumerate(CH):
                            t = 0 if n == 0 else 1
                            oreads = [("OT", h4, a) for h4 in range(4) for (a, w_) in [j_[0] for j_ in att_jobs]]
                            mm_group(PS[6][:, :cw], [(wo[q][:, h4, :], OT[:, h4, c0:c0 + cw]) for h4 in range(4)],
                                     reads=["wo%d" % q] + oreads, writes=["ps6"])
                            x_rmw(xsb, nx, oc, n, 6, 0, HGv[:, li, 1, oc, t:t + 1],
                                  acc=((oc == 0, sqt) if g4 == 3 else None))
                            nx += 1

        eps_t = _es.enter_context(nc.sbuf_tensor("sb_eps", [128, 1], F32))
        if True:
            S.op("dve", lambda h: h.memset(eps_t[:], EPS), writes=["eps"])
            for li in range(nlayers):
                if do_ffn:
                    ffn(li, 0, 0)
                if do_mix:
                    if li % 3 == 0:
                        pool_mixer(li, li // 3)
                    elif li % 3 == 1:
                        mla_mixer(li)
                    else:
                        conv_mixer(li)
                if do_ffn:
                    ffn(li, 1, 2)

            S.barrier()
            o = 0
            xf = []
            for q in range(2):
                v, o = carve(o, [128, DC, 128], F32); xf.append(v)
            yo = []
            for q in range(2):
                v, o = carve(o, [128, D], F32); yo.append(v)
            tiles = [(c, 128, c) for c in range(0, NPR, 128)] + \
                    [(NPR + HALO + c, 128, NPR + c) for c in range(0, 2048, 128)]
            XDt = XD.rearrange("c p t -> p c t")
            for ti, (c0, tw, r0) in enumerate(tiles):
                q = ti % 2
                n_of = [n for n, (a, w) in enumerate(CH) if a < c0 + tw and c0 < a + w]
                S.dma("sp", "ld%d" % q, xf[q], XDt[:, :, c0:c0 + tw],
                      reads=[("XD", i, n) for i in range(DC) for n in n_of], writes=["xf%d" % q])
                for i4 in range(4):
                    pb = (ti * 4 + i4) % 4
                    transpose_group([PS[pb][:, k * 128:(k + 1) * 128] for k in range(4)],
                                    [xf[q][:, i4 * 4 + k, :] for k in range(4)],
                                    reads=["xf%d" % q, "ident"], writes=["ps%d" % pb])
                    S.op("act" if i4 % 2 else "dve",
                         (lambda h, q=q, pb=pb, i4=i4: h.activation(yo[q][:, i4 * 512:(i4 + 1) * 512], PS[pb][:, :], AF.Identity))
                         if i4 % 2 else
                         (lambda h, q=q, pb=pb, i4=i4: h.tensor_copy(yo[q][:, i4 * 512:(i4 + 1) * 512], PS[pb][:, :])),
                         reads=["ps%d" % pb], writes=[("yo", q, i4)])
                S.dma("sp", "out%d" % q, yout[r0:r0 + tw, :], yo[q][:, :],
                      reads=[("yo", q, i4) for i4 in range(4)], writes=[("yout", ti)])
            S.barrier()
    build_program.last_counts = {n: e.count * e.inc for n, e in S.engs.items()}
    return nc


VOFF = {}
_o = 0
for _nm, _rows in [("b_ada", DEPTH * 144), ("norm_g", DEPTH * 3 * DC), ("pool_scale", 2 * DC), ("conv_b1", 2 * DC),
                   ("conv_dw", CONVW * DC), ("conv_dw_b", DC), ("conv_ln_g", DC), ("conv_ln_b", DC), ("conv_b2", DC),
                   ("q_norm", 4), ("kv_norm", 4), ("gains", 6)]:
    VOFF[_nm] = _o
    _o += _rows
NVEC = _o
NVEC_PAD = ((NVEC + 127) // 128) * 128


def _vec_table(inp):
    f = lambda k: np.asarray(inp[k], np.float32).reshape(-1, 128)
    qg = np.asarray(inp["mla_q_gain"], np.float32).reshape(192)
    kg = np.asarray(inp["mla_k_gain"], np.float32).reshape(192)
    perm = np.asarray(PERM)

    def row(v):
        r = np.zeros((1, 128), np.float32)
        r[0, :v.shape[0]] = v
        return r
    gains = [row(qg[:128]), row(kg[:128]), row(qg[128:]), row(qg[128:][perm]), row(kg[128:]), row(kg[128:][perm])]
    rows = [f("b_ada"), f("norm_g"), f("pool_scale"), f("conv_b1"), f("conv_dw"), f("conv_dw_b"), f("conv_ln_g"),
            f("conv_ln_b"), f("conv_b2"), f("mla_q_norm"), f("mla_kv_norm")] + gains
    tab = np.concatenate(rows, axis=0)
    assert tab.shape[0] == NVEC, (tab.shape, NVEC)
    out = np.zeros((NVEC_PAD, 128), np.float32)
    out[:NVEC] = tab
    return out


def _rope_tables(pos):
    pos = np.asarray(pos, np.int64)
    inv_freq = (np.float32(10000.0) ** (-np.arange(0, 32, 2, dtype=np.float32) / np.float32(32))).astype(np.float32)
    row = (pos // 64).astype(np.float32)
    col = (pos % 64).astype(np.float32)
    cos = np.zeros((64, pos.shape[0]), np.float32)
    sin = np.zeros((64, pos.shape[0]), np.float32)
    for d in range(64):
        a, b, fi = d // 32, (d % 32) // 16, d % 16
        ang = ((row if a == 0 else col) * inv_freq[fi]).astype(np.float32)
        cos[d] = np.cos(ang)
        sin[d] = (-np.sin(ang)) if b == 0 else np.sin(ang)
    return cos, sin


def _invcnt(tseq, pos, valid):
    out = np.zeros((4, pos.shape[0]), np.float32)
    for g, w in enumerate(POOLW):
        lo = np.clip(pos - w // 2, 0, tseq - 1)
        hi = np.clip(pos + w // 2 - 1, 0, tseq - 1)
        out[g] = np.where(valid, 1.0 / (hi - lo + 1).astype(np.float32), 0.0)
    return out


def make_in_maps(inp, ncores=NCORES, do_ffn=True):
    xp = np.asarray(inp["x_prompt"], np.float32)
    xs = np.asarray(inp["x_sample"], np.float32)
    c = np.asarray(inp["c"], np.float32)
    cctx = np.asarray(inp["c_ctx"], np.float32)
    w_uq = np.ascontiguousarray(inp["mla_w_uq"][0], dtype=np.float32)
    sw_cols = np.concatenate([h * 192 + 128 + np.asarray(PERM) for h in range(16)])
    cosk = np.ones((64, NK), np.float32)
    sink = np.zeros((64, NK), np.float32)
    ck, sk = _rope_tables(np.arange(DSEQ))
    cosk[:, 1024:] = ck
    sink[:, 1024:] = sk
    f32 = lambda a: np.ascontiguousarray(a, dtype=np.float32)
    shared = {
        "ident": np.eye(128, dtype=np.float32),
        "vecs": _vec_table(inp),
        "w_ada": f32(inp["w_ada"]) if do_ffn else None, "ffn_w1": f32(inp["ffn_w1"]) if do_ffn else None,
        "ffn_w3": f32(inp["ffn_w3"]) if do_ffn else None, "ffn_w2": f32(inp["ffn_w2"]) if do_ffn else None,
        "pool_w": f32(inp["pool_w"]),
        "mla_w_dq": f32(inp["mla_w_dq"][0]), "mla_w_uq": w_uq, "mla_w_uq_sw": f32(w_uq[:, sw_cols]),
        "mla_w_dkv": f32(inp["mla_w_dkv"][0]), "mla_w_ukv": f32(inp["mla_w_ukv"][0]), "mla_w_o": f32(inp["mla_w_o"][0]),
        "conv_w1": f32(inp["conv_w1"][0]), "conv_w2": f32(inp["conv_w2"][0]),
        "cosk": cosk, "sink": sink,
        "gains": np.concatenate([np.asarray(inp["mla_q_gain"], np.float32).reshape(1, 192),
                                 np.asarray(inp["mla_k_gain"], np.float32).reshape(1, 192)], axis=1),
    }
    if not do_ffn:
        for k_ in ("w_ada", "ffn_w1", "ffn_w3", "ffn_w2"):
            shared.pop(k_)
    ic_prompt = _invcnt(SEQ, np.arange(SEQ), np.ones(SEQ, bool))
    maps = []
    for r in range(ncores):
        b, half = r // 2, r % 2
        xin = np.zeros((T, D), np.float32)
        xin[0:256] = xp[2 * r]
        xin[256:512] = xp[2 * r + 1]
        start = half * 2048 - HALO
        pos = start + np.arange(WIN)
        valid = (pos >= 0) & (pos < DSEQ)
        xin[NPR:][valid] = xs[b, pos[valid]]
        m = dict(shared)
        m["xin"] = xin
        m["cond2"] = np.stack([cctx, c[b]]).astype(np.float32)
        m["maskw"] = np.ascontiguousarray(np.broadcast_to(valid.astype(np.float32)[None, :], (128, WIN)))
        ic = np.zeros((4, PW), np.float32)
        ic[:, 16:272] = ic_prompt
        ic[:, 288:544] = ic_prompt
        ic[:, 560:560 + WIN] = _invcnt(DSEQ, np.clip(pos, 0, DSEQ - 1), valid)
        m["invcnt"] = np.ascontiguousarray(np.broadcast_to(ic[:, None, :], (4, 128, PW)))
        cq = np.ones((64, T), np.float32)
        sq = np.zeros((64, T), np.float32)
        cw_, sw_ = _rope_tables(np.clip(pos, 0, DSEQ - 1))
        cq[:, NPR:] = cw_
        sq[:, NPR:] = sw_
        m["cosq"] = cq
        m["sinq"] = sq
        m["cck"] = f32(inp["cache_ckv"][b, 0])
        m["ckp"] = f32(inp["cache_kpe"][b, 0])
        maps.append(m)
    return maps


_NC_CACHE = {}


def run(inp, nlayers=DEPTH, do_mix=True, ncores=NCORES, do_ffn=True):
    key = (nlayers, do_mix, ncores, do_ffn)
    if key not in _NC_CACHE:
        _NC_CACHE[key] = build_program(nlayers, do_mix, ncores, do_ffn)
    nc = _NC_CACHE[key]
    res = run_bass_kernel_spmd(nc, make_in_maps(inp, ncores, do_ffn), core_ids=list(range(ncores)))
    return res.results


def kernel(**inputs):
    results = run(inputs)
    y_prompt = np.zeros((16, SEQ, D), np.float32)
    y_sample = np.zeros((4, DSEQ, D), np.float32)
    state_ckv = np.zeros((16, 1, SEQ, 512), np.float32)
    state_kpe = np.zeros((16, 1, SEQ, 64), np.float32)
    for r in range(NCORES):
        y = np.asarray(results[r]["yout"])
        y_prompt[2 * r] = y[0:256]
        y_prompt[2 * r + 1] = y[256:512]
        b, half = r // 2, r % 2
        y_sample[b, half * 2048:(half + 1) * 2048] = y[512:]
        a = np.asarray(results[r]["sck"])
        k = np.asarray(results[r]["skp"])
        state_ckv[2 * r, 0] = a[0:256]
        state_ckv[2 * r + 1, 0] = a[256:512]
        state_kpe[2 * r, 0] = k[0:256]
        state_kpe[2 * r + 1, 0] = k[256:512]
    return (y_prompt, y_sample, state_ckv, state_kpe)
```

```python
import os
import numpy as np
from contextlib import ExitStack
import concourse.bass as bass
import concourse.mybir as mybir
from concourse.bass_utils import run_bass_kernel_spmd

F32 = mybir.dt.float32
BF16 = mybir.dt.bfloat16
ALU = mybir.AluOpType
AF = mybir.ActivationFunctionType

NCORES = 8
D = 2048
DC = 16
FH = 5632
HC = 44
DEPTH = 4
SEQ = 256
DSEQ = 4096
PAST = 512
HALO = 32
NPR = 512
WIN = 2048 + 2 * HALO
T = NPR + WIN
CH = [(0, 512)] + [(512 + 424 * i, 424) for i in range(4)] + [(512 + 1696, 416)]
NCH = len(CH)
EPS = 1e-6
NPASS = 4
HPP = HC // NPASS
PW = T + 64
SEGS = [(0, 256, 16), (256, 256, 288), (512, WIN, 560)]
POOLW = (2, 4, 8, 16)
NK = 512 + PAST + DSEQ
NKT = NK // 128
CONVW = 31
PAIRS = [[0, 1], [2, 3], [4, 5], [6, 7]]
PERM = list(range(16, 32)) + list(range(0, 16)) + list(range(48, 64)) + list(range(32, 48))


class Eng:
    def __init__(self, name, h, sem, inc):
        self.name, self.h, self.sem, self.inc, self.count = name, h, sem, inc, 0


class Sched:
    def __init__(self, nc):
        self.nc = nc
        self.engs = {}
        self.regions = {}
        self.seen = {}
        self._sems = []

    def add_engine(self, name, h, inc=1):
        sem = self.nc.alloc_semaphore(name="s_" + name)
        e = Eng(name, h, sem, inc)
        self.engs[name] = e
        self.seen[name] = {}
        return e

    def _deps(self, reads, writes):
        deps = {}
        for k in reads:
            r = self.regions.get(k)
            if r and r["w"]:
                e, i = r["w"]
                deps[e] = max(deps.get(e, 0), i)
        for k in writes:
            r = self.regions.get(k)
            if r:
                if r["w"]:
                    e, i = r["w"]
                    deps[e] = max(deps.get(e, 0), i)
                for e, i in r["r"].items():
                    deps[e] = max(deps.get(e, 0), i)
        return deps

    def _wait(self, issuer, deps):
        sn = self.seen[issuer.name]
        for ename, idx in deps.items():
            if ename == "pe" and issuer.name == "pe":
                continue
            if sn.get(ename, 0) >= idx:
                continue
            e = self.engs[ename]
            issuer.h.wait_ge(e.sem, idx * e.inc)
            sn[ename] = idx

    def _commit(self, e, reads, writes):
        e.count += 1
        for k in writes:
            self.regions[k] = {"w": (e.name, e.count), "r": {}}
        for k in reads:
            r = self.regions.setdefault(k, {"w": None, "r": {}})
            r["r"][e.name] = e.count

    def op(self, ename, fn, reads=(), writes=()):
        e = self.engs[ename]
        self._wait(e, self._deps(reads, writes))
        ins = fn(e.h)
        ins.then_inc(e.sem, e.inc)
        self._commit(e, reads, writes)

    def dma(self, qname, slot, out, in_, reads=(), writes=(), **kw):
        q = self.engs[qname]
        s = self.engs[slot]
        self._wait(q, self._deps(reads, writes))
        q.h.dma_start(out=out, in_=in_, **kw).then_inc(s.sem, 16)
        self._commit(s, reads, writes)

    def coll(self, fn, reads=(), writes=()):
        q = self.engs["pool"]
        c = self.engs["cc"]
        self._wait(q, self._deps(reads, writes))
        fn(q.h).then_inc(c.sem, 1)
        self._commit(c, reads, writes)

    def barrier(self):
        tgt = {n: e.count for n, e in self.engs.items() if e.count > 0}
        for n in ("pe", "act", "dve", "pool", "sp"):
            self._wait(self.engs[n], dict(tgt))
        self.regions = {}


def _chunks_of(n, size=128):
    return [(s, min(size, n - s)) for s in range(0, n, size)]


def build_program(nlayers=DEPTH, do_mix=True, ncores=NCORES, do_ffn=True):
    PAIRS = [[2 * i, 2 * i + 1] for i in range(ncores // 2)]
    nc = bass.Bass("TRN2", target_bir_lowering=False)

    def din(name, shape, dt=F32):
        return nc.dram_tensor(name, list(shape), dt, kind="ExternalInput").ap()

    def dout(name, shape, dt=F32):
        return nc.dram_tensor(name, list(shape), dt, kind="ExternalOutput").ap()

    xin = din("xin", [T, D])
    cond2 = din("cond2", [2, D])
    ident_d = din("ident", [128, 128])
    vec_d = din("vecs", [NVEC_PAD, 128])
    if do_ffn:
        w_ada = din("w_ada", [DEPTH, D, 9 * D])
        ffn_w1 = din("ffn_w1", [DEPTH, 2, D, FH])
        ffn_w3 = din("ffn_w3", [DEPTH, 2, D, FH])
        ffn_w2 = din("ffn_w2", [DEPTH, 2, FH, D])
    yout = dout("yout", [NPR + 2048, D])
    sck_o = dout("sck", [NPR, 512])
    skp_o = dout("skp", [NPR, 64])
    maskw = din("maskw", [128, WIN])
    invcnt = din("invcnt", [4, 128, PW])
    cosq = din("cosq", [64, T])
    sinq = din("sinq", [64, T])
    cosk = din("cosk", [64, NK])
    sink = din("sink", [64, NK])
    cck = din("cck", [PAST, 512])
    ckp = din("ckp", [PAST, 64])
    gains_d = din("gains", [1, 384])
    pool_w = din("pool_w", [2, 4, 512, 512])
    w_dq = din("mla_w_dq", [D, 512])
    w_uq = din("mla_w_uq", [512, 3072])
    w_uqs = din("mla_w_uq_sw", [512, 1024])
    w_dkv = din("mla_w_dkv", [D, 576])
    w_ukv = din("mla_w_ukv", [512, 4096])
    w_o = din("mla_w_o", [D, D])
    conv_w1 = din("conv_w1", [D, 2 * D])
    conv_w2 = din("conv_w2", [D, D])
    bin_ = nc.dram_tensor("kv_bounce_in", [512, 2048], BF16).ap()
    bout = nc.dram_tensor("kv_bounce_out", [1024, 2048], BF16).ap()
    binp = nc.dram_tensor("kp_bounce_in", [64, 2048], BF16).ap()
    boutp = nc.dram_tensor("kp_bounce_out", [128, 2048], BF16).ap()

    XD = nc.dram_tensor("xd_scratch", [DC, 128, T], F32).ap()

    S = Sched(nc)
    S.add_engine("pe", nc.tensor)
    S.add_engine("act", nc.scalar)
    S.add_engine("dve", nc.vector)
    S.add_engine("pool", nc.gpsimd)
    S.add_engine("sp", nc.sync)
    for nm in ["ld0", "ld1", "ld2", "ld3", "wa0", "wa1", "wb0", "wb1", "wc0", "wc1",
               "xl0", "xl1", "xl2", "xl3", "xs0", "xs1", "xs2", "xs3", "misc", "out0", "out1",
               "tb0", "tb1", "tb2", "tb3", "gth", "ad0", "ad1", "ad2", "ad3"]:
        S.add_engine(nm, None, inc=16)
    S.add_engine("cc", None, inc=1)

    ARENA_F32 = 47 * 1024 + 512
    with ExitStack() as _es:
        arena = _es.enter_context(nc.sbuf_tensor("arena", [128, ARENA_F32], F32))
        ident = _es.enter_context(nc.sbuf_tensor("sb_ident", [128, 128], F32))
        VEC = _es.enter_context(nc.sbuf_tensor("sb_vec", [128, NVEC_PAD], F32))
        MOD = _es.enter_context(nc.sbuf_tensor("sb_mod", [128, DEPTH * 144 * 2], F32))
        GS = _es.enter_context(nc.sbuf_tensor("gs", [128, DEPTH * 3 * DC * 2], F32))
        HG = _es.enter_context(nc.sbuf_tensor("hg", [128, DEPTH * 3 * DC * 2], F32))
        ones_b = _es.enter_context(nc.sbuf_tensor("onesb", [128, 128], BF16))
        scond = _es.enter_context(nc.sbuf_tensor("scond", [128, DC, 2], BF16))
        identb = _es.enter_context(nc.sbuf_tensor("identb", [128, 128], BF16))
        PG = _es.enter_context(nc.sbuf_tensor("pg", [128, 2 * DC * 2], F32))
        G2B2 = _es.enter_context(nc.sbuf_tensor("g2b2", [128, DC * 2], F32))
        GQK = _es.enter_context(nc.sbuf_tensor("gqk", [128, 1], F32))
        NEGC = _es.enter_context(nc.sbuf_tensor("negc", [128, 1], F32))
        ones_ff = _es.enter_context(nc.sbuf_tensor("onesff", [128, 128], F32))
        ones_f = _es.enter_context(nc.sbuf_tensor("onesf", [1, 128], F32))
        GROW = _es.enter_context(nc.sbuf_tensor("grow", [1, 392], F32))
        ps0 = _es.enter_context(nc.psum_tensor("ps0", [128, 512], F32))
        ps1 = _es.enter_context(nc.psum_tensor("ps1", [128, 512], F32))
        ps2 = _es.enter_context(nc.psum_tensor("ps2", [128, 512], F32))
        ps3 = _es.enter_context(nc.psum_tensor("ps3", [128, 512], F32))
        ps4 = _es.enter_context(nc.psum_tensor("ps4", [128, 512], F32))
        ps5 = _es.enter_context(nc.psum_tensor("ps5", [128, 512], F32))
        ps6 = _es.enter_context(nc.psum_tensor("ps6", [128, 512], F32))
        ps7 = _es.enter_context(nc.psum_tensor("ps7", [128, 512], F32))
        PS = [ps0, ps1, ps2, ps3, ps4, ps5, ps6, ps7]

        def carve(off_bytes, shape, dt):
            esz = 2 if dt == BF16 else 4
            n = int(np.prod(shape[1:]))
            nf32 = (n * esz + 3) // 4
            base = arena[:, off_bytes // 4: off_bytes // 4 + nf32]
            v = base.bitcast(dt) if dt != F32 else base
            if len(shape) == 3:
                v = v.rearrange("p (a b) -> p a b", a=shape[1])
            return v, off_bytes + nf32 * 4

        mmT = nc.tensor

        def mm_group(ps_ap, pairs, reads, writes):
            def fn(h):
                ins = None
                n = len(pairs)
                for q, (l, r) in enumerate(pairs):
                    ins = h.matmul(ps_ap, l, r, start=(q == 0), stop=(q == n - 1))
                return ins
            S.op("pe", fn, reads=reads, writes=writes)

        def transpose_group(ps_ap_list, ins_list, reads, writes):
            def fn(h):
                ins = None
                for o, i_ in zip(ps_ap_list, ins_list):
                    ins = h.transpose(o, i_, ident[:i_.shape[0], :i_.shape[0]])
                return ins
            S.op("pe", fn, reads=reads, writes=writes)

        S.dma("sp", "misc", ident[:], ident_d[:, :], writes=["ident"])
        S.op("dve", lambda h: h.memset(ones_b[:], 1.0), writes=["ones"])

        o = 0
        vstage, o = carve(o, [128, 128], F32)
        for rb in range(NVEC_PAD // 128):
            S.dma("sp", "ld0", vstage, vec_d[rb * 128:(rb + 1) * 128, :], writes=["vstage"])
            transpose_group([PS[0][:, :128]], [vstage], reads=["vstage", "ident"], writes=["ps0"])
            S.op("dve", lambda h, rb=rb: h.tensor_copy(VEC[:, rb * 128:(rb + 1) * 128], PS[0][:, :128]),
                 reads=["ps0"], writes=["VEC"])

        cst, o2 = carve(o, [128, D], F32)
        S.dma("sp", "ld1", cst[0:2, :], cond2[:, :], writes=["cst"])
        S.op("act", lambda h: h.activation(cst[0:2, :], cst[0:2, :], AF.Silu), reads=["cst"], writes=["cst"])
        for k in range(DC):
            transpose_group([PS[1][:, 2 * k:2 * k + 2]], [cst[0:2, k * 128:(k + 1) * 128]],
                            reads=["cst", "ident"], writes=["ps1"])
        S.op("dve", lambda h: h.tensor_copy(scond[:].rearrange("p a b -> p (a b)"), PS[1][:, :2 * DC]),
             reads=["ps1"], writes=["scond"])

        MODv = MOD[:].rearrange("p (l c t) -> p l c t", l=DEPTH, c=144)
        GSv = GS[:].rearrange("p (l s c t) -> p l s c t", l=DEPTH, s=3, c=DC)
        HGv = HG[:].rearrange("p (l s c t) -> p l s c t", l=DEPTH, s=3, c=DC)
        PGv = PG[:].rearrange("p (j c t) -> p j c t", j=2, c=DC)
        G2B2v = G2B2[:].rearrange("p (c t) -> p c t", c=DC)
        ADA_OFF = 73728
        adab = [carve(ADA_OFF + q * 16384, [128, DC, 512], BF16)[0] for q in range(2)]
        arow = [carve(ADA_OFF + 32768 + q * 2048, [128, 512], F32)[0] for q in range(2)]

        def derive(li):
            for s_ in range(3):
                g = VEC[:, VOFF["norm_g"] + (li * 3 + s_) * DC: VOFF["norm_g"] + (li * 3 + s_ + 1) * DC]
                for t in range(2):
                    sc = MODv[:, li, (3 * s_ + 1) * DC:(3 * s_ + 2) * DC, t]
                    S.op("dve", lambda h, s_=s_, t=t, sc=sc, g=g: h.scalar_tensor_tensor(
                        GSv[:, li, s_, :, t], sc, 1.0, g, ALU.add, ALU.mult), reads=["MOD", "VEC"], writes=["GS"])
                    gt = MODv[:, li, (3 * s_ + 2) * DC:(3 * s_ + 3) * DC, t]
                    S.op("dve", lambda h, s_=s_, t=t, gt=gt: h.tensor_scalar(
                        HGv[:, li, s_, :, t], gt, 0.5 if s_ != 1 else 1.0, None, ALU.mult),
                        reads=["MOD"], writes=["HG"])
            for t in range(2):
                if li % 3 == 0:
                    j = li // 3
                    ps_ = VEC[:, VOFF["pool_scale"] + j * DC: VOFF["pool_scale"] + (j + 1) * DC]
                    S.op("dve", lambda h, j=j, t=t, ps_=ps_: h.tensor_tensor(
                        PGv[:, j, :, t], HGv[:, li, 1, :, t], ps_, ALU.mult), reads=["HG", "VEC"], writes=["PG"])
                if li == 2:
                    b2_ = VEC[:, VOFF["conv_b2"]: VOFF["conv_b2"] + DC]
                    S.op("dve", lambda h, t=t, b2_=b2_: h.tensor_tensor(
                        G2B2v[:, :, t], HGv[:, 2, 1, :, t], b2_, ALU.mult), reads=["HG", "VEC"], writes=["G2B2"])

        def ada_gen(li):
            wsrc = w_ada[li].rearrange("(k p) n -> p k n", p=128)

            def load(cb):
                S.dma("pool", "ad%d" % (cb % 2), adab[cb % 2], wsrc[:, :, cb * 512:(cb + 1) * 512],
                      writes=["adab%d" % (cb % 2)])
            load(0)
            for cb in range(36):
                if cb + 1 < 36:
                    load(cb + 1)
                q = cb % 2
                mm_group(PS[7][0:2, :512], [(scond[:, k, :], adab[q][:, k, :]) for k in range(DC)],
                         reads=["adab%d" % q, "scond"], writes=["ps7"])
                S.op("act", lambda h, q=q: h.activation(arow[q][0:2, :], PS[7][0:2, :512], AF.Identity),
                     reads=["ps7"], writes=["arow%d" % q])
                transpose_group([PS[6][:, 2 * (cb * 4 + cc):2 * (cb * 4 + cc) + 2] for cc in range(4)],
                                [arow[q][0:2, cc * 128:(cc + 1) * 128] for cc in range(4)],
                                reads=["arow%d" % q, "ident"], writes=["ps6"])
                yield
            bias = VEC[:, VOFF["b_ada"] + li * 144: VOFF["b_ada"] + (li + 1) * 144]
            for t in range(2):
                S.op("dve", lambda h, t=t, bias=bias: h.tensor_tensor(
                    MODv[:, li, :, t], PS[6][:, :288].rearrange("p (c t) -> p c t", t=2)[:, :, t], bias, ALU.add),
                    reads=["ps6", "VEC"], writes=["MOD"])
            derive(li)
            yield

        def bg_step(bg, k):
            if bg is None:
                return
            for _ in range(k):
                try:
                    next(bg)
                except StopIteration:
                    return

        def bg_drain(bg):
            bg_step(bg, 1000)

        S.op("dve", lambda h: h.tensor_copy(identb[:], ident[:]), reads=["ident"], writes=["identb"])
        S.op("dve", lambda h: h.memset(ones_f[:], 1.0), writes=["onesf"])
        S.op("dve", lambda h: h.memset(ones_ff[:], 1.0), writes=["onesff"])
        ACC_OFF = 184064
        ACC, _ = carve(ACC_OFF, [128, T], F32)

        def acc_update(xs_ap, n, cw_, first, sqt_ap, sqt_key):
            c0_ = CH[n][0]
            accv = ACC[:, c0_:c0_ + cw_]
            if first:
                S.op("pool", lambda h: h.tensor_tensor(accv, xs_ap, xs_ap, ALU.mult), writes=[("acc", n)])
            else:
                S.op("pool", lambda h: h.tensor_tensor(sqt_ap[:, :cw_], xs_ap, xs_ap, ALU.mult), writes=[sqt_key])
                S.op("pool", lambda h: h.tensor_tensor(accv, accv, sqt_ap[:, :cw_], ALU.add),
                     reads=[sqt_key], writes=[("acc", n)])
        if not do_ffn:
            S.op("dve", lambda h: h.memset(MOD[:], 0.5), writes=["MOD"])
            for li in range(nlayers):
                derive(li)
        S.op("dve", lambda h: h.tensor_tensor(GQK[:], VEC[:, VOFF["gains"]:VOFF["gains"] + 1],
                                              VEC[:, VOFF["gains"] + 1:VOFF["gains"] + 2], ALU.mult),
             reads=["VEC"], writes=["GQK"])
        S.dma("sp", "misc", GROW[0:1, 0:384], gains_d[:, :], writes=["GROW"])
        S.op("act", lambda h: h.activation(GROW[0:1, 0:384], GROW[0:1, 0:384], AF.Abs),
             reads=["GROW"], writes=["GROW"])
        S.op("dve", lambda h: h.tensor_scalar(GROW[0:1, 0:192], GROW[0:1, 0:192], 1.0, None, ALU.mult, ALU.max,
                                              accum_out=GROW[0:1, 384:385]), reads=["GROW"], writes=["GROW"])
        S.op("dve", lambda h: h.tensor_scalar(GROW[0:1, 192:384], GROW[0:1, 192:384], 1.0, None, ALU.mult, ALU.max,
                                              accum_out=GROW[0:1, 385:386]), reads=["GROW"], writes=["GROW"])
        S.op("dve", lambda h: h.tensor_scalar(GROW[0:1, 386:387], GROW[0:1, 384:385], GROW[0:1, 385:386],
                                              -float(np.sqrt(192.0)), ALU.mult, ALU.mult), reads=["GROW"], writes=["GROW"])
        mm_group(PS[6][:, 0:1], [(ones_f[0:1, :], GROW[0:1, 386:387])], reads=["GROW", "onesf"], writes=["ps6"])
        S.op("dve", lambda h: h.tensor_copy(NEGC[:], PS[6][:, 0:1]), reads=["ps6"], writes=["NEGC"])

        def shift_col(li, s, c, t):
            return MODv[:, li, (3 * s) * DC + c, t:t + 1]

        S.barrier()
        def _all_ada():
            for li_ in range(nlayers):
                yield from ada_gen(li_)
        bg0 = _all_ada() if do_ffn else None
        sqt_i = [carve(ADA_OFF + 32768 + 4096 + q * 2048, [128, 512], F32)[0] for q in range(2)]
        xt_bufs = [carve(s * 4 * D * 4, [128, 4, D], F32)[0] for s in range(2)]
        xs_off = 2 * 4 * D * 4
        xs_bufs = [carve(xs_off + s * 2048, [128, 512], F32)[0] for s in range(4)]
        nxs = 0

        def acc_update_i(xs, n, cw_, first, k):
            c0_ = CH[n][0]
            accv = ACC[:, c0_:c0_ + cw_]
            src = xs_bufs[xs][:, :cw_]
            if first:
                S.op("pool", lambda h: h.tensor_tensor(accv, src, src, ALU.mult), reads=["xs%d" % xs], writes=[("acc", n)])
            else:
                sq_ = sqt_i[k % 2]
                S.op("pool", lambda h: h.tensor_tensor(sq_[:, :cw_], src, src, ALU.mult),
                     reads=["xs%d" % xs], writes=["sqti%d" % (k % 2)])
                S.op("pool", lambda h: h.tensor_tensor(accv, accv, sq_[:, :cw_], ALU.add),
                     reads=["sqti%d" % (k % 2)], writes=[("acc", n)])
        for n, (c0, cw) in enumerate(CH):
            tl = _chunks_of(cw)
            s = n % 2
            for ti, (t0, tw) in enumerate(tl):
                S.dma("sp", "ld%d" % (2 * s + (ti % 2)), xt_bufs[s][:tw, ti, :], xin[c0 + t0:c0 + t0 + tw, :],
                      writes=[("xt", s, ti)])
            for i in range(DC):
                pb = 3 + (i % 2)
                transpose_group([PS[pb][:, t0:t0 + tw] for (t0, tw) in tl],
                                [xt_bufs[s][:tw, ti, i * 128:(i + 1) * 128] for ti, (t0, tw) in enumerate(tl)],
                                reads=[("xt", s, ti) for ti in range(len(tl))] + ["ident"], writes=["ps%d" % pb])
                xs = nxs % 4
                nxs += 1
                S.op("act" if i % 2 else "dve",
                     (lambda h, xs=xs, pb=pb, cw=cw: h.activation(xs_bufs[xs][:, :cw], PS[pb][:, :cw], AF.Identity))
                     if i % 2 else
                     (lambda h, xs=xs, pb=pb, cw=cw: h.tensor_copy(xs_bufs[xs][:, :cw], PS[pb][:, :cw])),
                     reads=["ps%d" % pb], writes=["xs%d" % xs])
                S.dma("sp", "xs%d" % xs, XD[i, :, c0:c0 + cw], xs_bufs[xs][:, :cw],
                      reads=["xs%d" % xs], writes=[("XD", i, n)])
                acc_update_i(xs, n, cw, i == 0, nxs)
                bg_step(bg0, 2)
        bg_drain(bg0)

        def xl_at(o_):
            return [carve(o_ + q * 2048, [128, 512], F32)[0] for q in range(4)]

        def norm_mod(li, s, Hb, xl):
            for n, (c0, cw) in enumerate(CH):
                mm_group(PS[5][:, :cw], [(ones_ff[:, :], ACC[:, c0:c0 + cw])], reads=[("acc", n), "onesff"], writes=["ps5"])
                S.op("act", lambda h, c0=c0, cw=cw: h.activation(ACC[:, c0:c0 + cw], PS[5][:, :cw], AF.Sqrt,
                                                                  bias=eps_t[:, 0:1], scale=1.0 / D),
                     reads=["ps5", "eps"], writes=[("acc", n)])
                S.op("dve", lambda h, c0=c0, cw=cw: h.reciprocal(ACC[:, c0:c0 + cw], ACC[:, c0:c0 + cw]),
                     reads=[("acc", n)], writes=[("acc", n)])
            nl = 0
            for n, (c0, cw) in enumerate(CH):
                t = 0 if n == 0 else 1
                for i in range(DC):
                    q = nl % 4
                    nl += 1
                    S.dma("sp", "xl%d" % q, xl[q][:, :cw], XD[i, :, c0:c0 + cw],
                          reads=[("XD", i, n)], writes=["fx%d" % q])
                    S.op("dve", lambda h, q=q, c0=c0, cw=cw: h.tensor_tensor(
                        xl[q][:, :cw], xl[q][:, :cw], ACC[:, c0:c0 + cw], ALU.mult),
                        reads=["fx%d" % q, ("acc", n)], writes=["fx%d" % q])
                    S.op("act", lambda h, q=q, c0=c0, cw=cw, i=i, t=t: h.activation(
                        Hb[:, i, c0:c0 + cw], xl[q][:, :cw], AF.Identity,
                        bias=shift_col(li, s, i, t), scale=GSv[:, li, s, i, t:t + 1]),
                        reads=["fx%d" % q, "MOD", "GS"], writes=[("H", i, n)])

        def _acc(xs_ap, n, cw_, first, sqt_ap, sqt_key, fxkey):
            c0_ = CH[n][0]
            accv = ACC[:, c0_:c0_ + cw_]
            if first:
                S.op("pool", lambda h: h.tensor_tensor(accv, xs_ap, xs_ap, ALU.mult), reads=[fxkey], writes=[("acc", n)])
            else:
                S.op("pool", lambda h: h.tensor_tensor(sqt_ap[:, :cw_], xs_ap, xs_ap, ALU.mult), reads=[fxkey], writes=[sqt_key])
                S.op("pool", lambda h: h.tensor_tensor(accv, accv, sqt_ap[:, :cw_], ALU.add),
                     reads=[sqt_key], writes=[("acc", n)])

        def ffn(li, f, s, bg=None):
            S.barrier()
            o = 0
            Hb, o = carve(o, [128, DC, T], BF16)
            Ub, o = carve(o, [128, HPP, T], BF16)
            w1b = []
            w3b = []
            w2b = []
            for q in range(2):
                v, o = carve(o, [128, DC, 128], BF16); w1b.append(v)
                v, o = carve(o, [128, DC, 128], BF16); w3b.append(v)
                v, o = carve(o, [128, HPP, 128], BF16); w2b.append(v)
            sab = []
            for q in range(2):
                v, o = carve(o, [128, 512], F32); sab.append(v)
            xsb = []
            for q in range(4):
                v, o = carve(o, [128, 512], F32); xsb.append(v)
            sqt, o = carve_list(o, 2, [128, 512], F32)
            assert o <= ACC_OFF, o
            norm_mod(li, s, Hb, xsb)
            w1s = ffn_w1[li, f].rearrange("(k p) n -> p k n", p=128)
            w3s = ffn_w3[li, f].rearrange("(k p) n -> p k n", p=128)
            w2s = ffn_w2[li, f].rearrange("(j p) n -> p j n", p=128)
            ng = 0
            nx = 0
            nw2 = 0
            for p in range(NPASS):
                for jj in range(HPP):
                    j = p * HPP + jj
                    q = j % 2
                    S.dma("pool", "wa%d" % q, w1b[q], w1s[:, :, j * 128:(j + 1) * 128], writes=["w1b%d" % q])
                    S.dma("pool", "wb%d" % q, w3b[q], w3s[:, :, j * 128:(j + 1) * 128], writes=["w3b%d" % q])
                    for n, (c0, cw) in enumerate(CH):
                        g = ng % 2
                        ng += 1
                        mm_group(PS[g][:, :cw], [(w1b[q][:, k, :], Hb[:, k, c0:c0 + cw]) for k in range(DC)],
                                 reads=["w1b%d" % q] + [("H", k, n) for k in range(DC)], writes=["ps%d" % g])
                        mm_group(PS[2 + g][:, :cw], [(w3b[q][:, k, :], Hb[:, k, c0:c0 + cw]) for k in range(DC)],
                                 reads=["w3b%d" % q] + [("H", k, n) for k in range(DC)], writes=["ps%d" % (2 + g)])
                        S.op("act", lambda h, g=g, cw=cw: h.activation(sab[g][:, :cw], PS[g][:, :cw], AF.Silu),
                             reads=["ps%d" % g], writes=["sa%d" % g])
                        S.op("dve", lambda h, g=g, jj=jj, c0=c0, cw=cw: h.tensor_tensor(
                            Ub[:, jj, c0:c0 + cw], sab[g][:, :cw], PS[2 + g][:, :cw], ALU.mult),
                            reads=["sa%d" % g, "ps%d" % (2 + g)], writes=[("U", jj, n)])
                    bg_step(bg, 4)
                for i in range(DC):
                    q = nw2 % 2
                    nw2 += 1
                    S.dma("pool", "wc%d" % q, w2b[q], w2s[:, p * HPP:(p + 1) * HPP, i * 128:(i + 1) * 128],
                          writes=["w2b%d" % q])
                    for n, (c0, cw) in enumerate(CH):
                        t = 0 if n == 0 else 1
                        g = 4 + (ng % 2)
                        ng += 1
                        xq = nx % 4
                        nx += 1
                        S.dma("sp", "xl%d" % xq, xsb[xq][:, :cw], XD[i, :, c0:c0 + cw],
                              reads=[("XD", i, n)], writes=["fx%d" % xq])
                        mm_group(PS[g][:, :cw], [(w2b[q][:, jj, :], Ub[:, jj, c0:c0 + cw]) for jj in range(HPP)],
                                 reads=["w2b%d" % q] + [("U", jj, n) for jj in range(HPP)], writes=["ps%d" % g])
                        S.op("dve", lambda h, g=g, xq=xq, cw=cw, i=i, t=t: h.scalar_tensor_tensor(
                            xsb[xq][:, :cw], PS[g][:, :cw], HGv[:, li, s, i, t:t + 1], xsb[xq][:, :cw],
                            ALU.mult, ALU.add), reads=["ps%d" % g, "fx%d" % xq, "HG"], writes=["fx%d" % xq])
                        if p == NPASS - 1:
                            _acc(xsb[xq][:, :cw], n, cw, i == 0, sqt[nx % 2], "sqt%d" % (nx % 2), "fx%d" % xq)
                        S.dma("act", "xs%d" % xq, XD[i, :, c0:c0 + cw], xsb[xq][:, :cw],
                              reads=["fx%d" % xq], writes=[("XD", i, n)])
            bg_drain(bg)

        def x_rmw(xsb, nx, i, n, pb, pcols, scal_ap, bias_ap=None, acc=None):
            c0, cw = CH[n]
            xq = nx % 4
            S.dma("sp", "xl%d" % xq, xsb[xq][:, :cw], XD[i, :, c0:c0 + cw],
                  reads=[("XD", i, n)], writes=["fx%d" % xq])
            S.op("dve", lambda h: h.scalar_tensor_tensor(
                xsb[xq][:, :cw], PS[pb][:, pcols:pcols + cw], scal_ap, xsb[xq][:, :cw], ALU.mult, ALU.add),
                reads=["ps%d" % pb, "fx%d" % xq, "HG", "PG"], writes=["fx%d" % xq])
            if bias_ap is not None:
                S.op("dve", lambda h: h.tensor_scalar(xsb[xq][:, :cw], xsb[xq][:, :cw], bias_ap, None, ALU.add),
                     reads=["fx%d" % xq, "G2B2"], writes=["fx%d" % xq])
            if acc is not None:
                _acc(xsb[xq][:, :cw], n, cw, acc[0], acc[1][nx % 2],
                     "sqt%d" % (0 if acc[1][0] is acc[1][1] else nx % 2), "fx%d" % xq)
            S.dma("act", "xs%d" % xq, XD[i, :, c0:c0 + cw], xsb[xq][:, :cw],
                  reads=["fx%d" % xq], writes=[("XD", i, n)])

        def carve_list(o, k, shape, dt):
            out = []
            for _ in range(k):
                v, o = carve(o, shape, dt)
                out.append(v)
            return out, o

        def pool_mixer(li, j):
            S.barrier()
            o = 0
            Hb, o = carve(o, [128, DC, T], BF16)
            norm_mod(li, 1, Hb, xl_at(o))
            S.barrier()
            hp, o = carve(o, [128, PW], F32)
            sA, o = carve(o, [128, PW], F32)
            sB, o = carve(o, [128, PW], F32)
            IC, o = carve(o, [128, PW], F32)
            MK, o = carve(o, [128, WIN], F32)
            DF, o = carve(o, [128, 4, T], BF16)
            wp, o = carve_list(o, 2, [128, 4, 512], BF16)
            xsb, o = carve_list(o, 4, [128, 512], F32)
            sqt, o = carve_list(o, 2, [128, 512], F32)
            assert o <= ACC_OFF, o
            S.dma("sp", "misc", MK, maskw[:, :], writes=["MK"])
            for nm_, b_ in (("hp", hp), ("sA", sA), ("sB", sB)):
                S.op("pool", lambda h, b_=b_: h.memset(b_, 0.0), writes=[nm_])
            nx = 0
            ng = 0
            for g in range(4):
                w = POOLW[g]
                S.dma("sp", "ld0", IC, invcnt[g], writes=["IC"])
                S.dma("pool", "wa%d" % (g % 2), wp[g % 2], pool_w[j, g].rearrange("(k p) n -> p k n", p=128),
                      writes=["wp%d" % (g % 2)])
                for kk in range(4):
                    i = g * 4 + kk
                    S.op("dve", lambda h, i=i: h.tensor_copy(hp[:, 16:272], Hb[:, i, 0:256]),
                         reads=[("H", i, 0)], writes=["hp"])
                    S.op("dve", lambda h, i=i: h.tensor_copy(hp[:, 288:544], Hb[:, i, 256:512]),
                         reads=[("H", i, 0)], writes=["hp"])
                    S.op("dve", lambda h, i=i: h.tensor_tensor(hp[:, 560:560 + WIN], Hb[:, i, 512:T], MK, ALU.mult),
                         reads=[("H", i, n) for n in range(1, NCH)] + ["MK"], writes=["hp"])
                    S.op("dve", lambda h: h.tensor_tensor(sA[:, 1:PW], hp[:, 0:PW - 1], hp[:, 1:PW], ALU.add),
                         reads=["hp"], writes=["sA"])
                    cur, curn, oth, othn = sA, "sA", sB, "sB"
                    for lvl, sh in ((4, 1), (8, 2), (16, 4)):
                        if w >= lvl:
                            S.op("dve", lambda h, cur=cur, oth=oth, sh=sh: h.tensor_tensor(
                                oth[:, sh:PW - sh], cur[:, 0:PW - 2 * sh], cur[:, 2 * sh:PW], ALU.add),
                                reads=[curn], writes=[othn])
                            cur, curn, oth, othn = oth, othn, cur, curn
                    S.op("dve", lambda h, cur=cur, oth=oth: h.tensor_tensor(oth[:, :], cur[:, :], IC[:, :], ALU.mult),
                         reads=[curn, "IC"], writes=[othn])
                    for (sc0, sw, pc0) in SEGS:
                        S.op("dve", lambda h, oth=oth, kk=kk, sc0=sc0, sw=sw, pc0=pc0: h.tensor_tensor(
                            DF[:, kk, sc0:sc0 + sw], oth[:, pc0:pc0 + sw], hp[:, pc0:pc0 + sw], ALU.subtract),
                            reads=[othn, "hp"], writes=[("DF", kk)])
                for oc in range(4):
                    io = g * 4 + oc
                    for n, (c0, cw) in enumerate(CH):
                        t = 0 if n == 0 else 1
                        pb = 4 + (ng % 2)
                        ng += 1
                        mm_group(PS[pb][:, :cw],
                                 [(wp[g % 2][:, kk, oc * 128:(oc + 1) * 128], DF[:, kk, c0:c0 + cw]) for kk in range(4)],
                                 reads=["wp%d" % (g % 2)] + [("DF", kk) for kk in range(4)], writes=["ps%d" % pb])
                        x_rmw(xsb, nx, io, n, pb, 0, PGv[:, j, io, t:t + 1], acc=(io == 0, sqt))
                        nx += 1

        def conv_mixer(li):
            S.barrier()
            A0, B0, C0 = 0, DC * T * 2, DC * T * 2 + DC * PW * 2
            Hb, _ = carve(A0, [128, DC, T], BF16)
            norm_mod(li, 1, Hb, xl_at(DC * T * 2))
            S.barrier()
            Gf, _ = carve(B0, [128, DC * PW], BF16)
            G = Gf.rearrange("p (a b) -> p a b", a=DC)
            o = C0
            w1a, o = carve(o, [128, DC, 128], BF16)
            w1b, o = carve(o, [128, DC, 128], BF16)
            sab, o = carve_list(o, 2, [128, 512], F32)
            MKb, o = carve(o, [128, WIN], BF16)
            assert o <= ARENA_F32 * 4, o
            S.dma("pool", "misc", MKb[:, 0:1056], maskw[:, 0:1056], writes=["MKb"])
            S.dma("pool", "misc", MKb[:, 1056:WIN], maskw[:, 1056:WIN], reads=["MKb"], writes=["MKb"])
            S.op("pool", lambda h: h.memset(Gf, 0.0), writes=["G"])
            w1s = conv_w1.rearrange("(k p) n -> p k n", p=128)
            b1o = VOFF["conv_b1"]
            ng = 0
            for oc in range(DC):
                S.dma("pool", "wa0", w1a, w1s[:, :, oc * 128:(oc + 1) * 128], writes=["cw1a"])
                S.dma("pool", "wb0", w1b, w1s[:, :, (DC + oc) * 128:(DC + oc + 1) * 128], writes=["cw1b"])
                for n, (c0, cw) in enumerate(CH):
                    g = ng % 2
                    ng += 1
                    mm_group(PS[g][:, :cw], [(w1a[:, k, :], Hb[:, k, c0:c0 + cw]) for k in range(DC)],
                             reads=["cw1a"] + [("H", k, n) for k in range(DC)], writes=["ps%d" % g])
                    mm_group(PS[2 + g][:, :cw], [(w1b[:, k, :], Hb[:, k, c0:c0 + cw]) for k in range(DC)],
                             reads=["cw1b"] + [("H", k, n) for k in range(DC)], writes=["ps%d" % (2 + g)])
                    S.op("act", lambda h, g=g, cw=cw, oc=oc: h.activation(
                        sab[g][:, :cw], PS[2 + g][:, :cw], AF.Sigmoid, bias=VEC[:, b1o + DC + oc:b1o + DC + oc + 1]),
                        reads=["ps%d" % (2 + g), "VEC"], writes=["sa%d" % g])
                    if n == 0:
                        for (sc0, sw, pc0) in SEGS[:2]:
                            S.op("dve", lambda h, g=g, oc=oc, sc0=sc0, sw=sw, pc0=pc0: h.scalar_tensor_tensor(
                                G[:, oc, pc0:pc0 + sw], PS[g][:, sc0:sc0 + sw], VEC[:, b1o + oc:b1o + oc + 1],
                                sab[g][:, sc0:sc0 + sw], ALU.add, ALU.mult),
                                reads=["ps%d" % g, "sa%d" % g, "VEC", "G"], writes=[("G", oc)])
                    else:
                        S.op("dve", lambda h, g=g, oc=oc, cw=cw: h.scalar_tensor_tensor(
                            sab[g][:, :cw], PS[g][:, :cw], VEC[:, b1o + oc:b1o + oc + 1],
                            sab[g][:, :cw], ALU.add, ALU.mult),
                            reads=["ps%d" % g, "sa%d" % g, "VEC"], writes=["sa%d" % g])
                        S.op("dve", lambda h, g=g, oc=oc, c0=c0, cw=cw: h.tensor_tensor(
                            G[:, oc, c0 + 48:c0 + 48 + cw], sab[g][:, :cw], MKb[:, c0 - NPR:c0 - NPR + cw], ALU.mult),
                            reads=["sa%d" % g, "MKb", "G"], writes=[("G", oc)])
            S.barrier()
            CB, _ = carve(A0, [128, DC, T], BF16)
            DG, _ = carve_list(C0, 2, [128, CONVW, 128], BF16)
            dwo = VOFF["conv_dw"]
            for c in range(DC):
                q = c % 2
                for k in range(CONVW):
                    S.op("dve", lambda h, q=q, k=k, c=c: h.tensor_scalar(
                        DG[q][:, k, :], identb[:], VEC[:, dwo + k * DC + c:dwo + k * DC + c + 1], None, ALU.mult),
                        reads=["identb", "VEC"], writes=[("DG", q, k)])
                for n, (c0, cw) in enumerate(CH):
                    pb = 4 + (ng % 2)
                    ng += 1
                    segs = SEGS[:2] if n == 0 else [(c0, cw, c0 + 48)]
                    for (sc0, sw, pc0) in segs:
                        mm_group(PS[pb][:, sc0 - c0:sc0 - c0 + sw],
                                 [(DG[q][:, k, :], G[:, c, pc0 + k - 15:pc0 + k - 15 + sw]) for k in range(CONVW)],
                                 reads=[("DG", q, k) for k in range(CONVW)], writes=["ps%d" % pb])
                    S.op("act", lambda h, pb=pb, c=c, c0=c0, cw=cw: h.activation(
                        CB[:, c, c0:c0 + cw], PS[pb][:, :cw], AF.Identity,
                        bias=VEC[:, VOFF["conv_dw_b"] + c:VOFF["conv_dw_b"] + c + 1]),
                        reads=["ps%d" % pb, "VEC"], writes=[("C", c, n)])
            S.barrier()
            o = C0
            MU, o = carve(o, [128, T], F32)
            RS, o = carve(o, [128, T], F32)
            sq1, o = carve(o, [128, 512], BF16)
            tmp, o = carve(o, [128, 512], F32)
            assert o <= ARENA_F32 * 4, o
            for n, (c0, cw) in enumerate(CH):
                mm_group(PS[6][:, :cw], [(ones_b[:], CB[:, c, c0:c0 + cw]) for c in range(DC)],
                         reads=["ones"], writes=["ps6"])
                for c in range(DC):
                    S.op("act", lambda h, c=c, c0=c0, cw=cw: h.activation(sq1[:, :cw], CB[:, c, c0:c0 + cw], AF.Square),
                         writes=["sq1"])
                    def fn(h, c=c, cw=cw):
                        return h.matmul(PS[7][:, :cw], ones_b[:], sq1[:, :cw], start=(c == 0), stop=(c == DC - 1))
                    S.op("pe", fn, reads=["sq1"], writes=["ps7"])
                S.op("act", lambda h, c0=c0, cw=cw: h.activation(MU[:, c0:c0 + cw], PS[6][:, :cw], AF.Identity, scale=1.0 / D),
                     reads=["ps6"], writes=[("MU", n)])
                S.op("dve", lambda h, c0=c0, cw=cw: h.tensor_tensor(tmp[:, :cw], MU[:, c0:c0 + cw], MU[:, c0:c0 + cw], ALU.mult),
                     reads=[("MU", n)], writes=["tmp"])
                S.op("dve", lambda h, cw=cw: h.scalar_tensor_tensor(tmp[:, :cw], PS[7][:, :cw], 1.0 / D, tmp[:, :cw],
                                                                    ALU.mult, ALU.subtract),
                     reads=["ps7", "tmp"], writes=["tmp"])
                S.op("act", lambda h, c0=c0, cw=cw: h.activation(RS[:, c0:c0 + cw], tmp[:, :cw], AF.Sqrt, bias=eps_t[:, 0:1]),
                     reads=["tmp", "eps"], writes=[("RS", n)])
                S.op("dve", lambda h, c0=c0, cw=cw: h.reciprocal(RS[:, c0:c0 + cw], RS[:, c0:c0 + cw]),
                     reads=[("RS", n)], writes=[("RS", n)])
            ZB, _ = carve(B0, [128, DC, T], BF16)
            lg, lb = VOFF["conv_ln_g"], VOFF["conv_ln_b"]
            for n, (c0, cw) in enumerate(CH):
                for c in range(DC):
                    S.op("dve", lambda h, c=c, c0=c0, cw=cw: h.tensor_tensor(
                        tmp[:, :cw], CB[:, c, c0:c0 + cw], MU[:, c0:c0 + cw], ALU.subtract),
                        reads=[("MU", n)], writes=["tmp"])
                    S.op("dve", lambda h, c0=c0, cw=cw: h.tensor_tensor(tmp[:, :cw], tmp[:, :cw], RS[:, c0:c0 + cw], ALU.mult),
                         reads=["tmp", ("RS", n)], writes=["tmp"])
                    S.op("act", lambda h, c=c, c0=c0, cw=cw: h.activation(
                        ZB[:, c, c0:c0 + cw], tmp[:, :cw], AF.Silu, bias=VEC[:, lb + c:lb + c + 1],
                        scale=VEC[:, lg + c:lg + c + 1]), reads=["tmp", "VEC"], writes=[("Z", c, n)])
            S.barrier()
            o = C0
            wz, o = carve_list(o, 1, [128, DC, 128], BF16)
            wz = wz * 2
            xsb, o = carve_list(o, 4, [128, 512], F32)
            assert o <= ACC_OFF, o
            sq1c, _ = carve(B0 + DC * T * 2, [128, 512], F32)
            w2s = conv_w2.rearrange("(k p) n -> p k n", p=128)
            nx = 0
            for oc in range(DC):
                q = oc % 2
                S.dma("pool", "wa0", wz[0], w2s[:, :, oc * 128:(oc + 1) * 128], writes=["wz0"])
                q = 0
                for n, (c0, cw) in enumerate(CH):
                    t = 0 if n == 0 else 1
                    pb = 4 + (ng % 2)
                    ng += 1
                    mm_group(PS[pb][:, :cw], [(wz[q][:, k, :], ZB[:, k, c0:c0 + cw]) for k in range(DC)],
                             reads=["wz%d" % q], writes=["ps%d" % pb])
                    x_rmw(xsb, nx, oc, n, pb, 0, HGv[:, li, 1, oc, t:t + 1], G2B2v[:, oc, t:t + 1],
                          acc=(oc == 0, [sq1c, sq1c]))
                    nx += 1

        def mla_mixer(li):
            _stop = float(os.environ.get('KSTOP', '99'))
            S.barrier()
            Hb, _ = carve(0, [128, DC, T], BF16)
            norm_mod(li, 1, Hb, xl_at(DC * T * 2))
            S.barrier()
            R_CQR, R_KVR, R_KPR = 83968, 125952, 167936
            CQR, _ = carve(R_CQR, [128, 4, T], F32)
            KVR, _ = carve(R_KVR, [128, 4, T], F32)
            KPR, o = carve(R_KPR, [128, T], F32)
            wd, o = carve_list(o, 2, [128, DC, 128], BF16)
            assert o <= ARENA_F32 * 4, o
            ng = 0
            jobs = [(w_dq, oc * 128, 128, CQR[:, oc, :]) for oc in range(4)] + \
                   [(w_dkv, oc * 128, 128, KVR[:, oc, :]) for oc in range(4)] + [(w_dkv, 512, 64, KPR)]
            for ji, (wsrc, col0, m, dst) in enumerate(jobs):
                q = ji % 2
                S.dma("pool", "wa%d" % q, wd[q][:, :, :m], wsrc.rearrange("(k p) n -> p k n", p=128)[:, :, col0:col0 + m],
                      writes=["wd%d" % q])
                for n, (c0, cw) in enumerate(CH):
                    pb = ng % 2
                    ng += 1
                    mm_group(PS[pb][:m, :cw], [(wd[q][:, k, :m], Hb[:, k, c0:c0 + cw]) for k in range(DC)],
                             reads=["wd%d" % q] + [("H", k, n) for k in range(DC)], writes=["ps%d" % pb])
                    S.op("act" if pb else "dve",
                         (lambda h, pb=pb, m=m, dst=dst, c0=c0, cw=cw: h.activation(dst[:m, c0:c0 + cw], PS[pb][:m, :cw], AF.Identity))
                         if pb else
                         (lambda h, pb=pb, m=m, dst=dst, c0=c0, cw=cw: h.tensor_copy(dst[:m, c0:c0 + cw], PS[pb][:m, :cw])),
                         reads=["ps%d" % pb], writes=[("raw", ji, n)])
            if _stop <= 1:
                return
            S.barrier()
            CQ, o = carve(0, [128, 4, T], BF16)
            CKV, o = carve(o, [128, 4, NK], BF16)
            KPF, o = carve(o, [128, NK], BF16)
            KPSW, o = carve(o, [128, NK], BF16)
            assert o <= 83968
            o = R_KPR + T * 4
            rt, o = carve(o, [128, 512], F32)
            sq1, o = carve(o, [128, 512], BF16)
            for (SRC, which) in ((CQR, 0), (KVR, 1)):
                gno = VOFF["q_norm"] if which == 0 else VOFF["kv_norm"]
                for n, (c0, cw) in enumerate(CH):
                    for c in range(4):
                        S.op("act", lambda h, c=c, c0=c0, cw=cw, SRC=SRC: h.activation(sq1[:, :cw], SRC[:, c, c0:c0 + cw], AF.Square),
                             writes=["sq1"])
                        def fn(h, c=c, cw=cw):
                            return h.matmul(PS[5][:, :cw], ones_b[:], sq1[:, :cw], start=(c == 0), stop=(c == 3))
                        S.op("pe", fn, reads=["sq1"], writes=["ps5"])
                    S.op("act", lambda h, cw=cw: h.activation(rt[:, :cw], PS[5][:, :cw], AF.Sqrt, bias=eps_t[:, 0:1], scale=1.0 / 512),
                         reads=["ps5"], writes=["rt"])
                    S.op("dve", lambda h, cw=cw: h.reciprocal(rt[:, :cw], rt[:, :cw]), reads=["rt"], writes=["rt"])
                    for c in range(4):
                        S.op("dve", lambda h, c=c, c0=c0, cw=cw, SRC=SRC: h.tensor_tensor(
                            SRC[:, c, c0:c0 + cw], SRC[:, c, c0:c0 + cw], rt[:, :cw], ALU.mult),
                            reads=["rt"], writes=[("nrm", which, c, n)])
                        dst = CQ if which == 0 else KVR
                        S.op("act", lambda h, c=c, c0=c0, cw=cw, SRC=SRC, dst=dst, gno=gno: h.activation(
                            dst[:, c, c0:c0 + cw], SRC[:, c, c0:c0 + cw], AF.Identity, scale=VEC[:, gno + c:gno + c + 1]),
                            reads=[("nrm", which, c, n)], writes=[("nrm2", which, c, n)])
            if _stop <= 2:
                return
            S.barrier()
            o = R_CQR
            CKS, o = carve(o, [128, 4, 2048], BF16)
            KPS, o = carve(o, [128, 2048], BF16)
            ct, o = carve_list(o, 4, [128, 576], F32)
            so, o = carve_list(o, 2, [128, 512], F32)
            sk, o = carve_list(o, 2, [128, 64], F32)
            assert o <= R_KVR, o
            OWN0 = NPR + HALO
            for c in range(4):
                S.op("dve", lambda h, c=c: h.tensor_copy(CKV[:, c, 0:NPR], KVR[:, c, 0:NPR]), writes=[("CKV", c)])
                S.op("act", lambda h, c=c: h.activation(CKS[:, c, :], KVR[:, c, OWN0:OWN0 + 2048], AF.Identity), writes=[("CKS", c)])
            S.op("dve", lambda h: h.tensor_copy(KPF[0:64, 0:NPR], KPR[0:64, 0:NPR]), writes=["KPF"])
            S.op("act", lambda h: h.activation(KPS[0:64, :], KPR[0:64, OWN0:OWN0 + 2048], AF.Identity), writes=["KPS"])
            S.op("pool", lambda h: h.memset(KPSW[:, :], 0.0), writes=["KPSW"])
            if _stop <= 2.1:
                return
            for c in range(4):
                S.dma("sp", "gth", bin_[c * 128:(c + 1) * 128, :], CKS[:, c, :], reads=[("CKS", c)], writes=["bin"])
            S.dma("sp", "gth", binp[:, :], KPS[0:64, :], reads=["KPS"], writes=["binp"])
            if _stop <= 2.2:
                return
            S.coll(lambda h: h.collective_compute("AllGather", ALU.bypass, replica_groups=PAIRS,
                                                  ins=[bin_[:, :]], outs=[bout[:, :]]),
                   reads=["bin"], writes=["bout"])
            S.coll(lambda h: h.collective_compute("AllGather", ALU.bypass, replica_groups=PAIRS,
                                                  ins=[binp[:, :]], outs=[boutp[:, :]]),
                   reads=["binp"], writes=["boutp"])
            if _stop <= 2.3:
                return
            for r in range(2):
                for c in range(4):
                    S.dma("sp", "tb%d" % c, CKV[:, c, 1024 + r * 2048:1024 + (r + 1) * 2048],
                          bout[r * 512 + c * 128:r * 512 + (c + 1) * 128, :], reads=["bout"], writes=[("CKVg", r, c)])
                S.dma("sp", "ld0", KPF[0:64, 1024 + r * 2048:1024 + (r + 1) * 2048],
                      boutp[r * 64:(r + 1) * 64, :], reads=["boutp", "KPF"], writes=[("KPFg", r)])
                for blk in range(4):
                    src0 = r * 64 + PERM[blk * 16]
                    S.dma("sp", "ld1", KPSW[blk * 16:(blk + 1) * 16, 1024 + r * 2048:1024 + (r + 1) * 2048],
                          boutp[src0:src0 + 16, :], reads=["boutp", "KPSW"], writes=[("KPSWg", r, blk)])
            if _stop <= 3:
                return
            for ti in range(4):
                q = ti % 2
                transpose_group([PS[q][:, c * 128:(c + 1) * 128] for c in range(4)],
                                [KVR[:, c, ti * 128:(ti + 1) * 128] for c in range(4)], reads=["ident"], writes=["ps%d" % q])
                S.op("dve", lambda h, q=q: h.tensor_copy(so[q][:, :], PS[q][:, :]), reads=["ps%d" % q], writes=["so%d" % q])
                S.dma("sp", "out%d" % q, sck_o[ti * 128:(ti + 1) * 128, :], so[q][:, :], reads=["so%d" % q], writes=[("sck", ti)])
                transpose_group([PS[2 + q][:, 0:64]], [KPR[0:64, ti * 128:(ti + 1) * 128]], reads=["ident"], writes=["ps%d" % (2 + q)])
                S.op("act", lambda h, q=q: h.activation(sk[q][:, :], PS[2 + q][:, 0:64], AF.Identity),
                     reads=["ps%d" % (2 + q)], writes=["sk%d" % q])
                S.dma("sp", "xs%d" % q, skp_o[ti * 128:(ti + 1) * 128, :], sk[q][:, :], reads=["sk%d" % q], writes=[("skp", ti)])
            if _stop <= 4:
                return
            for ti in range(4):
                S.dma("sp", "ld2", ct[ti][:, 0:512], cck[ti * 128:(ti + 1) * 128, :], writes=[("ct", ti)])
                S.dma("sp", "ld3", ct[ti][:, 512:576], ckp[ti * 128:(ti + 1) * 128, :], writes=[("ctp", ti)])
            for c in range(4):
                pb = 4 + (c % 2)
                transpose_group([PS[pb][:, ti * 128:(ti + 1) * 128] for ti in range(4)],
                                [ct[ti][:, c * 128:(c + 1) * 128] for ti in range(4)],
                                reads=[("ct", ti) for ti in range(4)] + ["ident"], writes=["ps%d" % pb])
                S.op("dve", lambda h, c=c, pb=pb: h.tensor_copy(CKV[:, c, 512:1024], PS[pb][:, :]),
                     reads=["ps%d" % pb], writes=[("CKVc", c)])
            transpose_group([PS[6][0:64, ti * 128:(ti + 1) * 128] for ti in range(4)],
                            [ct[ti][:, 512:576] for ti in range(4)],
                            reads=[("ctp", ti) for ti in range(4)] + ["ident"], writes=["ps6"])
            S.op("dve", lambda h: h.tensor_copy(KPF[0:64, 512:1024], PS[6][0:64, :]), reads=["ps6", "KPF"], writes=["KPFc"])
            if _stop <= 5:
                return
            S.barrier()
            KPG, o = carve(R_KVR, [128, NK], BF16)
            SQPE, o = carve(o, [128, NK], BF16)
            o = R_CQR
            tk, o = carve_list(o, 4, [128, 512], F32)
            ta, o = carve(o, [128, 512], F32)
            tb, o = carve(o, [128, 512], F32)
            go = VOFF["gains"]
            S.op("pool", lambda h: h.memset(KPG[64:128, :], 0.0), writes=["KPGz"])
            S.op("pool", lambda h: h.memset(SQPE[64:128, :], 0.0), writes=["SQPEz"])
            for kc in range(NK // 512):
                ks = slice(kc * 512, (kc + 1) * 512)
                q = kc % 2
                S.dma("sp", "tb%d" % (2 * q), tk[2 * q][0:64, :], cosk[:, ks], writes=[("tk", 2 * q)])
                S.dma("sp", "tb%d" % (2 * q + 1), tk[2 * q + 1][0:64, :], sink[:, ks], writes=[("tk", 2 * q + 1)])
                S.op("dve", lambda h, ks=ks, q=q: h.scalar_tensor_tensor(
                    ta[0:64, :], KPF[0:64, ks], VEC[0:64, go + 4:go + 5], tk[2 * q][0:64, :], ALU.mult, ALU.mult),
                    reads=[("tk", 2 * q)], writes=["ta"])
                S.op("dve", lambda h, ks=ks, q=q: h.scalar_tensor_tensor(
                    tb[0:64, :], KPSW[0:64, ks], VEC[0:64, go + 5:go + 6], tk[2 * q + 1][0:64, :], ALU.mult, ALU.mult),
                    reads=[("tk", 2 * q + 1)], writes=["tb"])
                S.op("dve", lambda h, ks=ks: h.tensor_tensor(KPG[0:64, ks], ta[0:64, :], tb[0:64, :], ALU.add),
                     reads=["ta", "tb"], writes=[("KPG", kc)])
                S.op("act", lambda h, ks=ks: h.activation(SQPE[0:64, ks], KPF[0:64, ks], AF.Square), writes=[("SQPE", kc)])
            if _stop <= 6:
                return
            S.barrier()
            o = 61952
            KN, o = carve(o, [128, NK], BF16)
            Vt, o = carve(o, [128, NKT, 128], BF16)
            assert o <= 83968
            o = R_CQR
            OT, o = carve(o, [128, 4, T], BF16)
            SQN, o = carve(o, [128, NK], BF16)
            QN, o = carve(o, [128, T], BF16)
            QR, o = carve(o, [128, T], BF16)
            assert o <= R_KVR, o
            o = R_KVR + 2 * NK * 2
            PT, o = carve_list(o, 4, [128, 512], BF16)
            wkv, o = carve(o, [128, 4, 256], BF16)
            wq, o = carve(o, [128, 4, 256], BF16)
            wqs, o = carve(o, [128, 4, 128], BF16)
            wo, o = carve_list(o, 2, [128, 4, 128], BF16)
            RQ, o = carve(o, [128, 512], F32)
            ta, o = carve(o, [128, 512], F32)
            tb, o = carve(o, [128, 512], F32)
            tq, o = carve_list(o, 2, [128, 512], F32)
            sqa, o = carve(o, [128, 512], BF16)
            sqb, o = carve(o, [128, 512], BF16)
            RK, o = carve(o, [128, NKT], F32)
            rden = RQ
            xsb, o = carve_list(o, 4, [128, 512], F32)
            sqt, o = carve_list(o, 2, [128, 512], F32)
            assert o <= ACC_OFF, o
            S.op("pool", lambda h: h.memset(wq[:, :, :].rearrange("p a b -> p (a b)"), 0.0), writes=["wq"])
            S.op("pool", lambda h: h.memset(wqs[:, :, :].rearrange("p a b -> p (a b)"), 0.0), writes=["wqs"])
            S.op("pool", lambda h: h.memset(QR[64:128, :], 0.0), writes=["QRz"])
            S.op("pool", lambda h: h.memset(sqb[64:128, :], 0.0), writes=["sqbz"])
            ukv_s = w_ukv.rearrange("(k p) n -> p k n", p=128)
            uq_s = w_uq.rearrange("(k p) n -> p k n", p=128)
            uqs_s = w_uqs.rearrange("(k p) n -> p k n", p=128)
            att_jobs = [((0, 256), (0, 2)), ((256, 256), (2, 4))] + [(CH[n], (4, NKT)) for n in range(1, NCH)]
            nx = 0
            nwo = 0
            inv192 = 1.0 / 192.0
            for hd in range(16):
                hh = hd % 4
                S.dma("pool", "wa0", wkv, ukv_s[:, :, hd * 256:(hd + 1) * 256], writes=["wkv"])
                S.dma("pool", "wb0", wq[:, :, 0:192], uq_s[:, :, hd * 192:(hd + 1) * 192], reads=["wq"], writes=["wq"])
                S.dma("pool", "wc0", wqs[:, :, 0:64], uqs_s[:, :, hd * 64:(hd + 1) * 64], reads=["wqs"], writes=["wqs"])
                for kc in range(NK // 512):
                    ks = slice(kc * 512, (kc + 1) * 512)
                    mm_group(PS[4][:, :], [(wkv[:, k, 0:128], CKV[:, k, ks]) for k in range(4)], reads=["wkv"], writes=["ps4"])
                    S.op("act", lambda h, ks=ks: h.activation(KN[:, ks], PS[4][:, :], AF.Identity), reads=["ps4"], writes=[("KN", kc)])
                    S.op("act", lambda h, ks=ks: h.activation(SQN[:, ks], PS[4][:, :], AF.Square), reads=["ps4"], writes=[("SQN", kc)])
                for kg in range(NKT // 4):
                    for t4 in range(4):
                        kt = kg * 4 + t4
                        mm_group(PS[5][:, t4 * 128:(t4 + 1) * 128],
                                 [(CKV[:, k, kt * 128:(kt + 1) * 128], wkv[:, k, 128:256]) for k in range(4)],
                                 reads=["wkv"], writes=["ps5"])
                    S.op("dve", lambda h, kg=kg: h.tensor_copy(
                        Vt[:, kg * 4:(kg + 1) * 4, :].rearrange("p a b -> p (a b)"), PS[5][:, :]),
                        reads=["ps5"], writes=[("V", kg)])
                for kt in range(NKT):
                    kc = kt // 4
                    mm_group(PS[7][:, kt:kt + 1],
                             [(SQN[:, kt * 128:(kt + 1) * 128], ones_b[:, 0:1]),
                              (SQPE[:, kt * 128:(kt + 1) * 128], ones_b[:, 0:1])],
                             reads=[("SQN", kc)], writes=["ps7"])
                S.op("act", lambda h: h.activation(RK[:, :], PS[7][:, 0:NKT], AF.Sqrt, bias=eps_t[:, 0:1], scale=inv192),
                     reads=["ps7"], writes=["RK"])
                S.op("dve", lambda h: h.reciprocal(RK[:, :], RK[:, :]), reads=["RK"], writes=["RK"])
                S.op("dve", lambda h: h.tensor_scalar(RK[:, :], RK[:, :], float(192.0 ** -0.5), None, ALU.mult),
                     reads=["RK"], writes=["RK"])
                if _stop <= 7:
                    return
                for n, (c0, cw) in enumerate(CH):
                    S.dma("sp", "tb0", tq[0][0:64, :cw], cosq[:, c0:c0 + cw], writes=["tq0"])
                    S.dma("sp", "tb1", tq[1][0:64, :cw], sinq[:, c0:c0 + cw], writes=["tq1"])
                    mm_group(PS[4][:, :cw], [(wq[:, k, 0:128], CQ[:, k, c0:c0 + cw]) for k in range(4)], reads=["wq"], writes=["ps4"])
                    mm_group(PS[5][:, :cw], [(wq[:, k, 128:256], CQ[:, k, c0:c0 + cw]) for k in range(4)], reads=["wq"], writes=["ps5"])
                    mm_group(PS[6][:, :cw], [(wqs[:, k, :], CQ[:, k, c0:c0 + cw]) for k in range(4)], reads=["wqs"], writes=["ps6"])
                    S.op("act", lambda h, cw=cw: h.activation(sqa[:, :cw], PS[4][:, :cw], AF.Square), reads=["ps4"], writes=["sqa"])
                    S.op("act", lambda h, cw=cw: h.activation(sqb[0:64, :cw], PS[5][0:64, :cw], AF.Square), reads=["ps5"], writes=["sqb"])
                    mm_group(PS[7][:, :cw], [(ones_b[:, :], sqa[:, :cw]), (ones_b[:, :], sqb[:, :cw])],
                             reads=["sqa", "sqb", "sqbz"], writes=["ps7"])
                    S.op("act", lambda h, cw=cw: h.activation(RQ[:, :cw], PS[7][:, :cw], AF.Sqrt, bias=eps_t[:, 0:1], scale=inv192),
                         reads=["ps7"], writes=["RQ"])
                    S.op("dve", lambda h, cw=cw: h.reciprocal(RQ[:, :cw], RQ[:, :cw]), reads=["RQ"], writes=["RQ"])
                    S.op("dve", lambda h, c0=c0, cw=cw: h.scalar_tensor_tensor(
                        QN[:, c0:c0 + cw], PS[4][:, :cw], GQK[:, 0:1], RQ[:, :cw], ALU.mult, ALU.mult),
                        reads=["ps4", "RQ"], writes=[("QN", n)])
                    S.op("dve", lambda h, cw=cw: h.scalar_tensor_tensor(
                        ta[0:64, :cw], PS[5][0:64, :cw], VEC[0:64, go + 2:go + 3], RQ[0:64, :cw], ALU.mult, ALU.mult),
                        reads=["ps5", "RQ"], writes=["ta"])
                    S.op("dve", lambda h, cw=cw: h.scalar_tensor_tensor(
                        tb[0:64, :cw], PS[6][0:64, :cw], VEC[0:64, go + 3:go + 4], RQ[0:64, :cw], ALU.mult, ALU.mult),
                        reads=["ps6", "RQ"], writes=["tb"])
                    S.op("dve", lambda h, cw=cw: h.tensor_tensor(ta[0:64, :cw], ta[0:64, :cw], tq[0][0:64, :cw], ALU.mult),
                         reads=["ta", "tq0"], writes=["ta"])
                    S.op("dve", lambda h, cw=cw: h.tensor_tensor(tb[0:64, :cw], tb[0:64, :cw], tq[1][0:64, :cw], ALU.mult),
                         reads=["tb", "tq1"], writes=["tb"])
                    S.op("dve", lambda h, c0=c0, cw=cw: h.tensor_tensor(QR[0:64, c0:c0 + cw], ta[0:64, :cw], tb[0:64, :cw], ALU.add),
                         reads=["ta", "tb"], writes=[("QR", n)])
                if _stop <= 8:
                    return
                for (q0, qw), (k0, k1) in att_jobs:
                    qn_ = [n for n, (a, w_) in enumerate(CH) if a < q0 + qw and q0 < a + w_]
                    qreads = [("QN", n) for n in qn_] + [("QR", n) for n in qn_]

                    SB = (0, 1, 4, 5)
                    LA = 3

                    def score(kt, slot):
                        mm_group(PS[SB[slot]][:, :qw],
                                 [(KN[:, kt * 128:(kt + 1) * 128], QN[:, q0:q0 + qw]),
                                  (KPG[:, kt * 128:(kt + 1) * 128], QR[:, q0:q0 + qw])],
                                 reads=[("KN", kt // 4), "QRz"] + qreads, writes=["ps%d" % SB[slot]])
                    for a_ in range(min(LA, k1 - k0)):
                        score(k0 + a_, a_ % 4)
                    for kt in range(k0, k1):
                        sl = (kt - k0) % 4
                        if kt + LA < k1:
                            score(kt + LA, (kt - k0 + LA) % 4)
                        S.op("act", lambda h, kt=kt, sl=sl, qw=qw: h.activation(
                            PT[sl][:, :qw], PS[SB[sl]][:, :qw], AF.Exp, bias=NEGC[:, 0:1], scale=RK[:, kt:kt + 1]),
                            reads=["ps%d" % SB[sl], "RK"], writes=["pt%d" % sl])
                        def fpv(h, kt=kt, sl=sl, qw=qw):
                            return h.matmul(PS[2][:, :qw], Vt[:, kt, :], PT[sl][:, :qw], start=(kt == k0), stop=(kt == k1 - 1))
                        S.op("pe", fpv, reads=["pt%d" % sl, ("V", kt // 4)], writes=["ps2"])
                        def fdn(h, kt=kt, sl=sl, qw=qw):
                            return h.matmul(PS[3][:, :qw], ones_b[:, :], PT[sl][:, :qw], start=(kt == k0), stop=(kt == k1 - 1))
                        S.op("pe", fdn, reads=["pt%d" % sl], writes=["ps3"])
                    S.op("dve", lambda h, qw=qw: h.reciprocal(rden[:, :qw], PS[3][:, :qw]), reads=["ps3"], writes=["RQ"])
                    S.op("dve", lambda h, q0=q0, qw=qw, hh=hh: h.tensor_tensor(
                        OT[:, hh, q0:q0 + qw], PS[2][:, :qw], rden[:, :qw], ALU.mult),
                        reads=["ps2", "RQ"], writes=[("OT", hh, q0)])
                if _stop <= 9:
                    return
                if hh == 3:
                    g4 = hd // 4
                    wos = w_o[g4 * 512:(g4 + 1) * 512, :].rearrange("(k p) n -> p k n", p=128)
                    for oc in range(DC):
                        q = nwo % 2
                        nwo += 1
                        S.dma("pool", "wc1" if q else "wb1", wo[q], wos[:, :, oc * 128:(oc + 1) * 128], writes=["wo%d" % q])
                        for n, (c0, cw) in enumerate(CH):
                            t = 0 if n == 0 else 1
                            oreads = [("OT", h4, a) for h4 in range(4) for (a, w_) in [j_[0] for j_ in att_jobs]]
                            mm_group(PS[6][:, :cw], [(wo[q][:, h4, :], OT[:, h4, c0:c0 + cw]) for h4 in range(4)],
                                     reads=["wo%d" % q] + oreads, writes=["ps6"])
                            x_rmw(xsb, nx, oc, n, 6, 0, HGv[:, li, 1, oc, t:t + 1],
                                  acc=((oc == 0, sqt) if g4 == 3 else None))
                            nx += 1

        eps_t = _es.enter_context(nc.sbuf_tensor("sb_eps", [128, 1], F32))
        if True:
            S.op("dve", lambda h: h.memset(eps_t[:], EPS), writes=["eps"])
            for li in range(nlayers):
                if do_ffn:
                    ffn(li, 0, 0)
                if do_mix:
                    if li % 3 == 0:
                        pool_mixer(li, li // 3)
                    elif li % 3 == 1:
                        mla_mixer(li)
                    else:
                        conv_mixer(li)
                if do_ffn:
                    ffn(li, 1, 2)

            S.barrier()
            o = 0
            xf = []
            for q in range(2):
                v, o = carve(o, [128, DC, 128], F32); xf.append(v)
            yo = []
            for q in range(2):
                v, o = carve(o, [128, D], F32); yo.append(v)
            tiles = [(c, 128, c) for c in range(0, NPR, 128)] + \
                    [(NPR + HALO + c, 128, NPR + c) for c in range(0, 2048, 128)]
            XDt = XD.rearrange("c p t -> p c t")
            for ti, (c0, tw, r0) in enumerate(tiles):
                q = ti % 2
                n_of = [n for n, (a, w) in enumerate(CH) if a < c0 + tw and c0 < a + w]
                S.dma("sp", "ld%d" % q, xf[q], XDt[:, :, c0:c0 + tw],
                      reads=[("XD", i, n) for i in range(DC) for n in n_of], writes=["xf%d" % q])
                for i4 in range(4):
                    pb = (ti * 4 + i4) % 4
                    transpose_group([PS[pb][:, k * 128:(k + 1) * 128] for k in range(4)],
                                    [xf[q][:, i4 * 4 + k, :] for k in range(4)],
                                    reads=["xf%d" % q, "ident"], writes=["ps%d" % pb])
                    S.op("act" if i4 % 2 else "dve",
                         (lambda h, q=q, pb=pb, i4=i4: h.activation(yo[q][:, i4 * 512:(i4 + 1) * 512], PS[pb][:, :], AF.Identity))
                         if i4 % 2 else
                         (lambda h, q=q, pb=pb, i4=i4: h.tensor_copy(yo[q][:, i4 * 512:(i4 + 1) * 512], PS[pb][:, :])),
                         reads=["ps%d" % pb], writes=[("yo", q, i4)])
                S.dma("sp", "out%d" % q, yout[r0:r0 + tw, :], yo[q][:, :],
                      reads=[("yo", q, i4) for i4 in range(4)], writes=[("yout", ti)])
            S.barrier()
    build_program.last_counts = {n: e.count * e.inc for n, e in S.engs.items()}
    return nc


VOFF = {}
_o = 0
for _nm, _rows in [("b_ada", DEPTH * 144), ("norm_g", DEPTH * 3 * DC), ("pool_scale", 2 * DC), ("conv_b1", 2 * DC),
                   ("conv_dw", CONVW * DC), ("conv_dw_b", DC), ("conv_ln_g", DC), ("conv_ln_b", DC), ("conv_b2", DC),
                   ("q_norm", 4), ("kv_norm", 4), ("gains", 6)]:
    VOFF[_nm] = _o
    _o += _rows
NVEC = _o
NVEC_PAD = ((NVEC + 127) // 128) * 128


def _vec_table(inp):
    f = lambda k: np.asarray(inp[k], np.float32).reshape(-1, 128)
    qg = np.asarray(inp["mla_q_gain"], np.float32).reshape(192)
    kg = np.asarray(inp["mla_k_gain"], np.float32).reshape(192)
    perm = np.asarray(PERM)

    def row(v):
        r = np.zeros((1, 128), np.float32)
        r[0, :v.shape[0]] = v
        return r
    gains = [row(qg[:128]), row(kg[:128]), row(qg[128:]), row(qg[128:][perm]), row(kg[128:]), row(kg[128:][perm])]
    rows = [f("b_ada"), f("norm_g"), f("pool_scale"), f("conv_b1"), f("conv_dw"), f("conv_dw_b"), f("conv_ln_g"),
            f("conv_ln_b"), f("conv_b2"), f("mla_q_norm"), f("mla_kv_norm")] + gains
    tab = np.concatenate(rows, axis=0)
    assert tab.shape[0] == NVEC, (tab.shape, NVEC)
    out = np.zeros((NVEC_PAD, 128), np.float32)
    out[:NVEC] = tab
    return out


def _rope_tables(pos):
    pos = np.asarray(pos, np.int64)
    inv_freq = (np.float32(10000.0) ** (-np.arange(0, 32, 2, dtype=np.float32) / np.float32(32))).astype(np.float32)
    row = (pos // 64).astype(np.float32)
    col = (pos % 64).astype(np.float32)
    cos = np.zeros((64, pos.shape[0]), np.float32)
    sin = np.zeros((64, pos.shape[0]), np.float32)
    for d in range(64):
        a, b, fi = d // 32, (d % 32) // 16, d % 16
        ang = ((row if a == 0 else col) * inv_freq[fi]).astype(np.float32)
        cos[d] = np.cos(ang)
        sin[d] = (-np.sin(ang)) if b == 0 else np.sin(ang)
    return cos, sin


def _invcnt(tseq, pos, valid):
    out = np.zeros((4, pos.shape[0]), np.float32)
    for g, w in enumerate(POOLW):
        lo = np.clip(pos - w // 2, 0, tseq - 1)
        hi = np.clip(pos + w // 2 - 1, 0, tseq - 1)
        out[g] = np.where(valid, 1.0 / (hi - lo + 1).astype(np.float32), 0.0)
    return out


def make_in_maps(inp, ncores=NCORES, do_ffn=True):
    xp = np.asarray(inp["x_prompt"], np.float32)
    xs = np.asarray(inp["x_sample"], np.float32)
    c = np.asarray(inp["c"], np.float32)
    cctx = np.asarray(inp["c_ctx"], np.float32)
    w_uq = np.ascontiguousarray(inp["mla_w_uq"][0], dtype=np.float32)
    sw_cols = np.concatenate([h * 192 + 128 + np.asarray(PERM) for h in range(16)])
    cosk = np.ones((64, NK), np.float32)
    sink = np.zeros((64, NK), np.float32)
    ck, sk = _rope_tables(np.arange(DSEQ))
    cosk[:, 1024:] = ck
    sink[:, 1024:] = sk
    f32 = lambda a: np.ascontiguousarray(a, dtype=np.float32)
    shared = {
        "ident": np.eye(128, dtype=np.float32),
        "vecs": _vec_table(inp),
        "w_ada": f32(inp["w_ada"]) if do_ffn else None, "ffn_w1": f32(inp["ffn_w1"]) if do_ffn else None,
        "ffn_w3": f32(inp["ffn_w3"]) if do_ffn else None, "ffn_w2": f32(inp["ffn_w2"]) if do_ffn else None,
        "pool_w": f32(inp["pool_w"]),
        "mla_w_dq": f32(inp["mla_w_dq"][0]), "mla_w_uq": w_uq, "mla_w_uq_sw": f32(w_uq[:, sw_cols]),
        "mla_w_dkv": f32(inp["mla_w_dkv"][0]), "mla_w_ukv": f32(inp["mla_w_ukv"][0]), "mla_w_o": f32(inp["mla_w_o"][0]),
        "conv_w1": f32(inp["conv_w1"][0]), "conv_w2": f32(inp["conv_w2"][0]),
        "cosk": cosk, "sink": sink,
        "gains": np.concatenate([np.asarray(inp["mla_q_gain"], np.float32).reshape(1, 192),
                                 np.asarray(inp["mla_k_gain"], np.float32).reshape(1, 192)], axis=1),
    }
    if not do_ffn:
        for k_ in ("w_ada", "ffn_w1", "ffn_w3", "ffn_w2"):
            shared.pop(k_)
    ic_prompt = _invcnt(SEQ, np.arange(SEQ), np.ones(SEQ, bool))
    maps = []
    for r in range(ncores):
        b, half = r // 2, r % 2
        xin = np.zeros((T, D), np.float32)
        xin[0:256] = xp[2 * r]
        xin[256:512] = xp[2 * r + 1]
        start = half * 2048 - HALO
        pos = start + np.arange(WIN)
        valid = (pos >= 0) & (pos < DSEQ)
        xin[NPR:][valid] = xs[b, pos[valid]]
        m = dict(shared)
        m["xin"] = xin
        m["cond2"] = np.stack([cctx, c[b]]).astype(np.float32)
        m["maskw"] = np.ascontiguousarray(np.broadcast_to(valid.astype(np.float32)[None, :], (128, WIN)))
        ic = np.zeros((4, PW), np.float32)
        ic[:, 16:272] = ic_prompt
        ic[:, 288:544] = ic_prompt
        ic[:, 560:560 + WIN] = _invcnt(DSEQ, np.clip(pos, 0, DSEQ - 1), valid)
        m["invcnt"] = np.ascontiguousarray(np.broadcast_to(ic[:, None, :], (4, 128, PW)))
        cq = np.ones((64, T), np.float32)
        sq = np.zeros((64, T), np.float32)
        cw_, sw_ = _rope_tables(np.clip(pos, 0, DSEQ - 1))
        cq[:, NPR:] = cw_
        sq[:, NPR:] = sw_
        m["cosq"] = cq
        m["sinq"] = sq
        m["cck"] = f32(inp["cache_ckv"][b, 0])
        m["ckp"] = f32(inp["cache_kpe"][b, 0])
        maps.append(m)
    return maps


_NC_CACHE = {}


def run(inp, nlayers=DEPTH, do_mix=True, ncores=NCORES, do_ffn=True):
    key = (nlayers, do_mix, ncores, do_ffn)
    if key not in _NC_CACHE:
        _NC_CACHE[key] = build_program(nlayers, do_mix, ncores, do_ffn)
    nc = _NC_CACHE[key]
    res = run_bass_kernel_spmd(nc, make_in_maps(inp, ncores, do_ffn), core_ids=list(range(ncores)))
    return res.results


def kernel(**inputs):
    results = run(inputs)
    y_prompt = np.zeros((16, SEQ, D), np.float32)
    y_sample = np.zeros((4, DSEQ, D), np.float32)
    state_ckv = np.zeros((16, 1, SEQ, 512), np.float32)
    state_kpe = np.zeros((16, 1, SEQ, 64), np.float32)
    for r in range(NCORES):
        y = np.asarray(results[r]["yout"])
        y_prompt[2 * r] = y[0:256]
        y_prompt[2 * r + 1] = y[256:512]
        b, half = r // 2, r % 2
        y_sample[b, half * 2048:(half + 1) * 2048] = y[512:]
        a = np.asarray(results[r]["sck"])
        k = np.asarray(results[r]["skp"])
        state_ckv[2 * r, 0] = a[0:256]
        state_ckv[2 * r + 1, 0] = a[256:512]
        state_kpe[2 * r, 0] = k[0:256]
        state_kpe[2 * r + 1, 0] = k[256:512]
    return (y_prompt, y_sample, state_ckv, state_kpe)
```
